# Optimizing a Trainium2 kernel written in Bass

```python
import math, functools
import jax, jax.numpy as jnp
from jax import lax
import numpy as np

D_MODEL = 1024
BATCH = 8
SEQ = 8192
DEPTH = 4
DEC_BATCH = 8
DEC_SEQ = 16
PAST_LEN = 2048

CHUNK = 64
HEAD_DIM = 64
N_A_LAYERS = DEPTH // 2
N_B_LAYERS = DEPTH - N_A_LAYERS
N_HEADS_A = 16
N_KV_A = 4
WINDOW = 128
A_BACK = WINDOW // CHUNK
N_HEADS_B = 16
B_BACK = 8
B_REACH = B_BACK * CHUNK
REL_CLIP = 128
N_BUCKETS = 32
T5_MAX_DIST = 128
D_FF = 2816
CONV_W = 3
LN_EPS = 1e-5
DEEPNORM_ALPHA = (2.0 * DEPTH) ** 0.25
DEEPNORM_BETA = (8.0 * DEPTH) ** -0.25
ATTN_SCALE = HEAD_DIM ** -0.5
NEG_INF = -1e30

kernel_name = 'yoco_streaming_swa_chunkband_encoder'


def layer_norm(x, g, b):
    xf = x.astype(jnp.float32)
    mu = xf.mean(-1, keepdims=True)
    var = jnp.square(xf - mu).mean(-1, keepdims=True)
    y = (xf - mu) * lax.rsqrt(var + LN_EPS)
    return (y * g.astype(jnp.float32) + b.astype(jnp.float32)).astype(x.dtype)


def t5_bias(table, qpos, kpos):
    rel = kpos[None, :] - qpos[:, None]
    nb = N_BUCKETS // 2
    max_exact = nb // 2
    n = jnp.abs(rel)
    large = max_exact + (jnp.log(jnp.maximum(n, 1).astype(jnp.float32) / max_exact)
                         / math.log(T5_MAX_DIST / max_exact) * (nb - max_exact)).astype(jnp.int32)
    large = jnp.minimum(large, nb - 1)
    bucket = jnp.where(rel > 0, nb, 0) + jnp.where(n < max_exact, n, large)
    return jnp.transpose(table[bucket], (2, 0, 1))


def relclip_bias(table, qpos, kpos):
    d = jnp.clip(qpos[:, None] - kpos[None, :], -REL_CLIP, REL_CLIP) + REL_CLIP
    return table[:, d]


def band_attend(q, k, v, qpos, kpos, bias, sinks, n_back):
    B, Lq, H, hd = q.shape
    Lk, KV = k.shape[1], k.shape[2]
    G = H // KV
    qg = q.reshape(B, Lq, KV, G, hd)
    s = jnp.einsum('bqkgd,bskd->bkgqs', qg, k).astype(jnp.float32) * ATTN_SCALE
    s = s + bias.reshape(KV, G, Lq, Lk).astype(jnp.float32)
    qc = qpos // CHUNK
    kc = kpos // CHUNK
    valid = (kpos[None, :] >= 0) & (kc[None, :] <= qc[:, None]) & (kc[None, :] >= qc[:, None] - n_back)
    s = jnp.where(valid, s, NEG_INF)
    if sinks is None:
        p = jax.nn.softmax(s, axis=-1)
    else:
        sink = sinks.astype(jnp.float32).reshape(KV, G, 1, 1)
        m = jnp.maximum(s.max(-1, keepdims=True), sink)
        e = jnp.exp(s - m)
        p = e / (e.sum(-1, keepdims=True) + jnp.exp(sink - m))
    o = jnp.einsum('bkgqs,bskd->bqkgd', p.astype(v.dtype), v)
    return o.reshape(B, Lq, H * hd)


def attn_prompt(q, k, v, n_back, bias_fn, sinks):
    B, S, H, hd = q.shape
    n_chunks = S // CHUNK
    band = (n_back + 1) * CHUNK
    pad = n_back * CHUNK
    kp = jnp.pad(k, ((0, 0), (pad, 0), (0, 0), (0, 0)))
    vp = jnp.pad(v, ((0, 0), (pad, 0), (0, 0), (0, 0)))

    def one_chunk(ci):
        start = ci * CHUNK
        qc = lax.dynamic_slice_in_dim(q, start, CHUNK, axis=1)
        kc = lax.dynamic_slice_in_dim(kp, start, band, axis=1)
        vc = lax.dynamic_slice_in_dim(vp, start, band, axis=1)
        qpos = start + jnp.arange(CHUNK, dtype=jnp.int32)
        kpos = start - pad + jnp.arange(band, dtype=jnp.int32)
        return band_attend(qc, kc, vc, qpos, kpos, bias_fn(qpos, kpos), sinks, n_back)

    o = lax.map(one_chunk, jnp.arange(n_chunks, dtype=jnp.int32))
    return jnp.swapaxes(o, 0, 1).reshape(B, S, H * hd)


def attn_sample(q, k_new, v_new, k_cache, v_cache, n_back, bias_fn, sinks):
    B, T, H, hd = q.shape
    P = k_cache.shape[1]
    k = jnp.concatenate([k_cache.astype(k_new.dtype), k_new], axis=1)
    v = jnp.concatenate([v_cache.astype(v_new.dtype), v_new], axis=1)
    qpos = PAST_LEN + jnp.arange(T, dtype=jnp.int32)
    kpos = PAST_LEN - P + jnp.arange(P + T, dtype=jnp.int32)
    return band_attend(q, k, v, qpos, kpos, bias_fn(qpos, kpos), sinks, n_back)


def conv_ffn(h, prev, w_up, conv_w, conv_b, w_down):
    u = h @ w_up
    B, T, C = u.shape
    if prev is None:
        prev = jnp.zeros((B, CONV_W - 1, C), u.dtype)
    ue = jnp.concatenate([prev.astype(u.dtype), u], axis=1)
    y = conv_b
    for tap in range(CONV_W):
        y = y + conv_w[tap] * ue[:, tap:tap + T]
    a, g = jnp.split(y, 2, axis=-1)
    out = (jax.nn.gelu(g) * a) @ w_down
    return out, ue[:, T:]


def run_group(x, c, cache_a_k, cache_a_v, cache_b_k, cache_b_v, state_conv,
              w_ada, b_ada, ln_g, ln_b, w_qkv_a, w_o_a, sinks_a, t5_table,
              w_ada_kv, b_ada_kv, w_kv_b, w_q_b, w_o_b, relpos_b,
              w_up, conv_w, conv_b, w_down):
    prompt = cache_a_k is None
    B, T, _ = x.shape
    sc = jax.nn.silu(c)
    new_ak, new_av, new_conv = [], [], []
    k_b = v_b = None
    qa = N_HEADS_A * HEAD_DIM
    ka = N_KV_A * HEAD_DIM
    for l in range(DEPTH):
        ada = (sc @ w_ada[l] + b_ada[l])[:, None, :]
        sh_m, sc_m, g_m, sh_f, sc_f, g_f = jnp.split(ada, 6, axis=-1)
        h = x * (1 + sc_m) + sh_m
        if l < N_A_LAYERS:
            qkv = h @ w_qkv_a[l]
            q = qkv[..., :qa].reshape(B, T, N_HEADS_A, HEAD_DIM)
            k = qkv[..., qa:qa + ka].reshape(B, T, N_KV_A, HEAD_DIM)
            v = qkv[..., qa + ka:].reshape(B, T, N_KV_A, HEAD_DIM)
            bias_fn = functools.partial(t5_bias, t5_table)
            if prompt:
                o = attn_prompt(q, k, v, A_BACK, bias_fn, sinks_a[l])
                rows = min(WINDOW, T)
                new_ak.append(k[:, T - rows:])
                new_av.append(v[:, T - rows:])
            else:
                o = attn_sample(q, k, v, cache_a_k[l], cache_a_v[l], A_BACK, bias_fn, sinks_a[l])
                new_ak.append(k)
                new_av.append(v)
            mix = o @ w_o_a[l]
        else:
            j = l - N_A_LAYERS
            q = (h @ w_q_b[j]).reshape(B, T, N_HEADS_B, HEAD_DIM)
            bias_fn = functools.partial(relclip_bias, relpos_b[j])
            if prompt:
                o = attn_prompt(q, k_b, v_b, B_BACK, bias_fn, None)
            else:
                o = attn_sample(q, k_b, v_b, cache_b_k, cache_b_v, B_BACK, bias_fn, None)
            mix = o @ w_o_b[j]
        x = layer_norm(DEEPNORM_ALPHA * x + (1 + g_m) * mix, ln_g[l, 0], ln_b[l, 0])
        h = x * (1 + sc_f) + sh_f
        f, conv_rows = conv_ffn(h, None if prompt else state_conv[l], w_up[l], conv_w[l], conv_b[l], w_down[l])
        new_conv.append(conv_rows)
        x = layer_norm(DEEPNORM_ALPHA * x + (1 + g_f) * f, ln_g[l, 1], ln_b[l, 1])
        if l == N_A_LAYERS - 1:
            ada_kv = (sc @ w_ada_kv + b_ada_kv)[:, None, :]
            sh_kv, sc_kv = jnp.split(ada_kv, 2, axis=-1)
            kv = ((x * (1 + sc_kv) + sh_kv) @ w_kv_b).reshape(B, T, 2, N_HEADS_B, HEAD_DIM)
            k_b = kv[:, :, 0]
            v_b = kv[:, :, 1]
    if prompt:
        rows = min(B_REACH, T)
        b_k_rows = k_b[:, T - rows:]
        b_v_rows = v_b[:, T - rows:]
    else:
        b_k_rows = k_b
        b_v_rows = v_b
    return (x, jnp.stack(new_ak), jnp.stack(new_av), b_k_rows, b_v_rows, jnp.stack(new_conv))


def setup_inputs(seed: int = 0) -> dict:
    key = jax.random.key(seed)
    ks = jax.random.split(key, 32)
    f32 = jnp.float32

    def nrm(k, shape, s):
        return jax.random.normal(k, shape, f32) * s

    a_rows = min(WINDOW, PAST_LEN)
    b_rows = min(B_REACH, PAST_LEN)
    beta = DEEPNORM_BETA
    return {
        'x_prompt': nrm(ks[0], (BATCH, SEQ, D_MODEL), 1.0),
        'x_sample': nrm(ks[1], (DEC_BATCH, DEC_SEQ, D_MODEL), 1.0),
        'c_prompt': nrm(ks[2], (BATCH, D_MODEL), 1.0),
        'c_sample': nrm(ks[3], (DEC_BATCH, D_MODEL), 1.0),
        'cache_a_k': nrm(ks[4], (N_A_LAYERS, DEC_BATCH, a_rows, N_KV_A, HEAD_DIM), 1.0),
        'cache_a_v': nrm(ks[5], (N_A_LAYERS, DEC_BATCH, a_rows, N_KV_A, HEAD_DIM), 1.0),
        'cache_b_k': nrm(ks[6], (DEC_BATCH, b_rows, N_HEADS_B, HEAD_DIM), 1.0),
        'cache_b_v': nrm(ks[7], (DEC_BATCH, b_rows, N_HEADS_B, HEAD_DIM), 1.0),
        'state_conv': nrm(ks[8], (DEPTH, DEC_BATCH, CONV_W - 1, 2 * D_FF), 1.0),
        'w_ada': nrm(ks[9], (DEPTH, D_MODEL, 6 * D_MODEL), 0.1 * D_MODEL ** -0.5),
        'b_ada': nrm(ks[10], (DEPTH, 6 * D_MODEL), 0.02),
        'ln_g': 1.0 + nrm(ks[11], (DEPTH, 2, D_MODEL), 0.02),
        'ln_b': nrm(ks[12], (DEPTH, 2, D_MODEL), 0.02),
        'w_qkv_a': nrm(ks[13], (N_A_LAYERS, D_MODEL, (N_HEADS_A + 2 * N_KV_A) * HEAD_DIM), D_MODEL ** -0.5),
        'w_o_a': nrm(ks[14], (N_A_LAYERS, N_HEADS_A * HEAD_DIM, D_MODEL), beta * (N_HEADS_A * HEAD_DIM) ** -0.5),
        'sinks_a': nrm(ks[15], (N_A_LAYERS, N_HEADS_A), 0.5),
        't5_table': nrm(ks[16], (N_BUCKETS, N_HEADS_A), 0.5),
        'w_ada_kv': nrm(ks[17], (D_MODEL, 2 * D_MODEL), 0.1 * D_MODEL ** -0.5),
        'b_ada_kv': nrm(ks[18], (2 * D_MODEL,), 0.02),
        'w_kv_b': nrm(ks[19], (D_MODEL, 2 * N_HEADS_B * HEAD_DIM), D_MODEL ** -0.5),
        'w_q_b': nrm(ks[20], (N_B_LAYERS, D_MODEL, N_HEADS_B * HEAD_DIM), D_MODEL ** -0.5),
        'w_o_b': nrm(ks[21], (N_B_LAYERS, N_HEADS_B * HEAD_DIM, D_MODEL), beta * (N_HEADS_B * HEAD_DIM) ** -0.5),
        'relpos_b': nrm(ks[22], (N_B_LAYERS, N_HEADS_B, 2 * REL_CLIP + 1), 0.5),
        'w_up': nrm(ks[23], (DEPTH, D_MODEL, 2 * D_FF), D_MODEL ** -0.5),
        'conv_w': nrm(ks[24], (DEPTH, CONV_W, 2 * D_FF), CONV_W ** -0.5),
        'conv_b': nrm(ks[25], (DEPTH, 2 * D_FF), 0.02),
        'w_down': nrm(ks[26], (DEPTH, D_FF, D_MODEL), beta * D_FF ** -0.5),
    }


def reference(x_prompt, x_sample, c_prompt, c_sample, cache_a_k, cache_a_v, cache_b_k, cache_b_v,
              state_conv, w_ada, b_ada, ln_g, ln_b, w_qkv_a, w_o_a, sinks_a, t5_table,
              w_ada_kv, b_ada_kv, w_kv_b, w_q_b, w_o_b, relpos_b, w_up, conv_w, conv_b, w_down):
    weights = (w_ada, b_ada, ln_g, ln_b, w_qkv_a, w_o_a, sinks_a, t5_table,
               w_ada_kv, b_ada_kv, w_kv_b, w_q_b, w_o_b, relpos_b, w_up, conv_w, conv_b, w_down)
    y_prompt, p_ak, p_av, p_bk, p_bv, p_conv = run_group(
        x_prompt, c_prompt, None, None, None, None, None, *weights)
    y_sample, s_ak, s_av, s_bk, s_bv, s_conv = run_group(
        x_sample, c_sample, cache_a_k, cache_a_v, cache_b_k, cache_b_v, state_conv, *weights)
    return (y_prompt, y_sample, p_ak, p_av, p_bk, p_bv, p_conv, s_ak, s_av, s_bk, s_bv, s_conv)
```

```python
import math
import contextlib
import numpy as np
import concourse.bass as bass
import concourse.mybir as mybir
from concourse.bass_utils import run_bass_kernel_spmd

F32 = mybir.dt.float32
BF16 = mybir.dt.bfloat16
AF = mybir.ActivationFunctionType
ALU = mybir.AluOpType

D = 1024
C8 = 8
TT = 512
DFF = 2816
FC = 22
NCH = 44
DEPTH = 4
DEC = 16
ALPHA = (2.0 * DEPTH) ** 0.25
LN_EPS = 1e-5
EPS_P = LN_EPS / (ALPHA * ALPHA)
SCALE = 64 ** -0.5
UP_ORDER = list(range(22, 44)) + list(range(0, 22))

V_BADA = 0
V_BKV = 192
V_LNG = 208
V_LNB = 272
V_CW = 336
V_CB = 864
V_C = 1040
V_ROWS = 1152


class Res:
    __slots__ = ("w", "r", "rd")

    def __init__(self):
        self.w = None
        self.r = {}
        self.rd = []


class Chan:
    def __init__(self, sem):
        self.sem = sem
        self.count = 0


class Op:
    __slots__ = ("eng", "fn", "waits", "sig", "pos", "cnt", "dma", "dmaval", "epoch", "sigs", "cnts", "waiter")


ENGS = ["pe", "act", "dve", "pool", "sp"]
import os as _osm
MAXSIG = int(_osm.environ.get("MAXSIG", "1"))


class Prog:
    def __init__(self):
        self.ops = {e: [] for e in ENGS}
        self.seen = {e: {} for e in ENGS}
        self.epoch = 0

    def emit(self, eng, fn, reads=(), writes=(), chan=None):
        op = Op()
        op.eng = eng
        op.fn = fn
        op.pos = len(self.ops[eng])
        op.sig = False
        op.sigs = set()
        op.cnts = {}
        op.waiter = {}
        op.dma = chan
        op.epoch = self.epoch
        op.cnt = 0
        if chan is not None:
            chan.count += 16
            op.dmaval = chan.count
        deps = []
        for r in reads:
            if r.w is not None:
                deps.append((r.w, True))
        for w in writes:
            if w.w is not None:
                deps.append((w.w, False))
            for rd in w.r.values():
                deps.append((rd, False))
            for rd in w.rd:
                deps.append((rd, False))
        waits = []
        seen = self.seen[eng]
        for d, raw in deps:
            if d is op:
                continue
            if d.dma is not None:
                key = ("dma", id(d.dma))
                if seen.get(key, -1) >= d.dmaval:
                    continue
                seen[key] = d.dmaval
                waits.append(d)
                continue
            if d.eng == eng:
                if eng == "pe":
                    continue
            hops = 0
            while d.dma is None and eng not in d.sigs and len(d.sigs) >= MAXSIG:
                plst = self.ops[d.eng]
                zf = None
                for k in range(d.pos + 1, min(len(plst), d.pos + 13)):
                    z = plst[k]
                    if z.dma is None and z.fn is not None and (eng in z.sigs or len(z.sigs) == 0):
                        zf = z
                        break
                if zf is not None and zf.eng != eng:
                    d = zf
                    break
                c0 = sorted(d.sigs)[0]
                w0 = d.waiter[c0]
                hops += 1
                if w0.eng == eng or w0.dma is not None:
                    d = w0
                    break
                lst = self.ops[c0]
                found = None
                for k in range(w0.pos, min(len(lst), w0.pos + 400)):
                    z = lst[k]
                    if z.dma is None and z.fn is not None and (eng in z.sigs or len(z.sigs) == 0):
                        found = z
                        break
                if found is not None:
                    d = found
                    break
                d = lst[min(len(lst), w0.pos + 400) - 1]
                if d.dma is not None or d.fn is None:
                    d = w0
            if d.dma is not None:
                key = ("dma", id(d.dma))
                if seen.get(key, -1) >= d.dmaval:
                    continue
                seen[key] = d.dmaval
                waits.append(d)
                continue
            if d.eng == eng and (eng == "pe" or d is op):
                continue
            if d.eng == eng and hops > 0:
                continue
            key = d.eng
            if seen.get(key, -1) >= d.pos:
                continue
            seen[key] = d.pos
            d.sig = True
            d.sigs.add(eng)
            if eng not in d.waiter:
                d.waiter[eng] = op
            waits.append(d)
        op.waits = waits
        for r in reads:
            if chan is not None:
                r.rd.append(op)
            else:
                r.r[eng] = op
        for w in writes:
            w.w = op
            w.r = {}
            w.rd = []
        self.ops[eng].append(op)
        return op


def _t5_onehot():
    try:
        import jax
        import jax.numpy as jnp
        cpu = jax.devices("cpu")[0]
        with jax.default_device(cpu):
            rel = jnp.asarray(np.concatenate([(-128 + 127 - np.arange(255)), (127 - np.arange(255))]).astype(np.int32))
            nb = 16
            max_exact = 8
            n = jnp.abs(rel)
            large = max_exact + (jnp.log(jnp.maximum(n, 1).astype(jnp.float32) / max_exact)
                                 / math.log(128 / max_exact) * (nb - max_exact)).astype(jnp.int32)
            large = jnp.minimum(large, nb - 1)
            bucket = jnp.where(rel > 0, nb, 0) + jnp.where(n < max_exact, n, large)
            bucket = np.asarray(bucket)
    except Exception:
        rel = np.concatenate([(-128 + 127 - np.arange(255)), (127 - np.arange(255))]).astype(np.int32)
        n = np.abs(rel)
        large = 8 + (np.log(np.maximum(n, 1).astype(np.float32) / np.float32(8))
                     / np.float32(math.log(16.0)) * np.float32(8)).astype(np.int32)
        large = np.minimum(large, 15)
        bucket = np.where(rel > 0, 16, 0) + np.where(n < 8, n, large)
    oh = np.zeros((32, 510), np.float32)
    oh[bucket, np.arange(510)] = 1.0
    return oh


def build(NT):
    S = NT * TT
    nc = bass.Bass("TRN2", target_bir_lowering=False)
    P = Prog()

    def din(name, shape, dt=F32):
        return nc.dram_tensor(name, list(shape), dt, kind="ExternalInput").ap()

    def dout(name, shape, dt=F32):
        return nc.dram_tensor(name, list(shape), dt, kind="ExternalOutput").ap()

    def dscr(name, shape, dt=BF16):
        return nc.dram_tensor(name, list(shape), dt, kind="Internal").ap()

    x_prompt = din("x_prompt", [S, D])
    x_sample = din("x_sample", [DEC, D])
    cache_a_k = din("cache_a_k", [2, 128, 256])
    cache_a_v = din("cache_a_v", [2, 128, 256])
    cache_b_k = din("cache_b_k", [512, 1024])
    cache_b_v = din("cache_b_v", [512, 1024])
    state_conv = din("state_conv", [4, 2, 2 * DFF])
    w_ada = din("w_ada", [4, D, 6 * D])
    w_qkv_a = din("w_qkv_a", [2, D, 1536])
    w_o_a = din("w_o_a", [2, D, D])
    sinks_a = din("sinks_a", [2, 16])
    t5_table = din("t5_table", [32, 16])
    w_ada_kv = din("w_ada_kv", [D, 2 * D])
    w_kv_b = din("w_kv_b", [D, 2 * D])
    w_q_b = din("w_q_b", [2, D, D])
    w_o_b = din("w_o_b", [2, D, D])
    relpos_b = din("relpos_b", [2, 16, 257])
    w_up = din("w_up", [4, D, 2 * DFF])
    w_down = din("w_down", [4, DFF, D])
    vecs = din("vecs", [V_ROWS, 128])
    ident_d = din("ident", [128, 128])
    anti_d = din("anti", [128, 128])
    onehot_d = din("t5oh", [32, 510])

    y_prompt = dout("y_prompt", [S, D])
    y_sample = dout("y_sample", [DEC, D])
    p_ak = dout("p_ak", [2, 128, 256])
    p_av = dout("p_av", [2, 128, 256])
    p_bk = dout("p_bk", [512, 1024])
    p_bv = dout("p_bv", [512, 1024])
    p_conv = dout("p_conv", [4, 2, 2 * DFF])
    s_ak = dout("s_ak", [2, DEC, 256])
    s_av = dout("s_av", [2, DEC, 256])
    s_bk = dout("s_bk", [DEC, 1024])
    s_bv = dout("s_bv", [DEC, 1024])
    s_conv = dout("s_conv", [4, 2, 2 * DFF])

    sc_up = [dscr(f"sc_up{l}", [11, 128, 8 * 512]) for l in range(4)]
    sc_dn = [dscr(f"sc_dn{l}", [8, 128, 22 * 128]) for l in range(4)]
    sc_qa = [dscr(f"sc_qa{l}", [4, 128, 8 * 512]) for l in range(2)]
    sc_oa = [dscr(f"sc_oa{l}", [2, 128, 8 * 512]) for l in range(2)]
    sc_qb = [dscr(f"sc_qb{l}", [2, 128, 8 * 512]) for l in range(2)]
    sc_ob = [dscr(f"sc_ob{l}", [2, 128, 8 * 512]) for l in range(2)]
    sc_kvb = dscr("sc_kvb", [4, 128, 8 * 512])
    sc_veca = dscr("sc_veca", [16, 510], F32)
    sc_vecb = [dscr(f"sc_vecb{j}", [16, 510], F32) for j in range(2)]
    sc_ebb = [dscr(f"sc_ebb{j}", [128, 4096]) for j in range(2)]

    es = contextlib.ExitStack()
    with es:
        def sb(name, shape, dt):
            return es.enter_context(nc.sbuf_tensor(name, list(shape), dt))

        def ps(name):
            return es.enter_context(nc.psum_tensor(name, [128, 512], F32))

        xT = sb("xT", [128, C8, TT], F32)
        hT = sb("hT", [128, C8, TT], BF16)
        actT = sb("actT", [128, FC, TT], BF16)
        KB = sb("KB", [128, 2, C8, TT], BF16)
        VB = sb("VB", [128, 2, 4, D], BF16)
        KA = [sb(f"KA{l}", [128, 4, 5 * 128], BF16) for l in range(2)]
        VA = [sb(f"VA{l}", [128, 5, 256], BF16) for l in range(2)]
        Ubuf = [[sb(f"U{a}{b}", [128, TT + 2], F32) for b in range(2)] for a in range(2)]
        Ybuf = [[sb(f"Y{a}{b}", [128, TT], F32) for b in range(2)] for a in range(2)]
        Ebuf = [sb(f"E{i}", [128, 512], BF16) for i in range(5)]
        E0buf = sb("E0m", [128, 512], BF16)
        Tbuf = sb("Tb", [128, 512], F32)
        R2buf = sb("R2", [128, 512], F32)
        ybf = [sb(f"ybf{i}", [128, TT], BF16) for i in range(2)]
        sqb = [sb(f"sq{i}", [128, TT], BF16) for i in range(2)]
        mean_sb = sb("mean", [128, TT], F32)
        var_sb = sb("var", [128, TT], F32)
        rstd_sb = sb("rstd", [128, TT], F32)
        tmpb = [sb(f"tmp{i}", [128, TT], F32) for i in range(2)]
        EBA = sb("EBA", [128, 2, 2, 8, 128], BF16)
        EBB = sb("EBB", [128, 2, 2, 8, 128], BF16)
        W8 = [sb(f"W8_{i}", [128, 8 * 512], BF16) for i in range(3)]
        W22 = [sb(f"W22_{i}", [128, 22 * 128], BF16) for i in range(2)]
        stage = [sb(f"stg{i}", [128, D], F32) for i in range(2)]
        ident = sb("identS", [128, 128], F32)
        anti = sb("antiS", [128, 128], F32)
        ones_bf = sb("ones_bf", [128, 64], BF16)
        onesf = sb("onesf", [128, 128], BF16)
        VF = sb("VF", [128, V_ROWS], F32)
        SCt = sb("SCt", [128, 16], F32)
        ADA = sb("ADA", [128, 2, 208], F32)
        ADA1 = sb("ADA1", [128, 2, 208], F32)
        ADAG = sb("ADAG", [128, 2, 208], F32)
        FT = sb("FT", [128, 2, 4, 2, 2, 8], F32)
        FKV = sb("FKV", [128, 2, 2, 8], F32)
        carry = sb("carry", [128, 4, NCH, 2], F32)
        esink = sb("esink", [128, 2, 8], F32)
        t5_sb = sb("t5_sb", [32, 16], F32)
        tokA = tmpb[0][0:2, 0:256]
        oh_sb = mean_sb[0:32, 0:510]
        vec_sb = var_sb[0:16, 0:510]
        rp_sb = rstd_sb[0:16, 0:257]

        PS = [ps(f"ps{i}") for i in range(8)]
        D0, D1, S0, S1, O0, O1, N0, N1 = range(8)

        sems = {}
        sem_list = []

        def new_sem(name):
            s = es.enter_context(nc.semaphore(name))
            sem_list.append(s)
            return s

        def new_chan(name):
            return Chan(new_sem(name))

        R_x = [Res() for _ in range(C8)]
        R_h = [Res() for _ in range(C8)]
        R_act = [Res() for _ in range(FC)]
        R_ps = [Res() for _ in range(8)]
        R_w8 = [Res() for _ in range(3)]
        R_w22 = [Res() for _ in range(2)]
        R_ka = [[Res() for _ in range(5)] for _ in range(2)]
        R_va = [[Res() for _ in range(5)] for _ in range(2)]
        R_kb = [[Res() for _ in range(4)] for _ in range(2)]
        R_vb = [[Res() for _ in range(4)] for _ in range(2)]
        R_u = [[Res() for _ in range(2)] for _ in range(2)]
        R_uh = [[Res() for _ in range(2)] for _ in range(2)]
        R_y = [[Res() for _ in range(2)] for _ in range(2)]
        R_e = [Res() for _ in range(5)]
        R_e0 = Res()
        R_t = Res()
        R_r2 = Res()
        R_ybf = [Res(), Res()]
        R_sq = [Res(), Res()]
        R_mean = Res()
        R_var = Res()
        R_rstd = Res()
        R_tmp = [Res(), Res()]
        R_eba = Res()
        R_ebb = Res()
        R_stage = [Res(), Res()]
        R_const = Res()
        R_vf = Res()
        R_sc = Res()
        R_ada = Res()
        R_tabs = Res()
        R_carry = [[Res() for _ in range(NCH)] for _ in range(4)]
        R_esink = Res()
        R_tok = Res()
        R_misc = Res()
        R_scr = {}
        R_out = []

        ch_w8 = [new_chan(f"cw8_{i}") for i in range(3)]
        ch_w22 = [new_chan(f"cw22_{i}") for i in range(2)]
        ch_stage = [new_chan(f"cst{i}") for i in range(2)]
        ch_misc = new_chan("cmisc")
        ch_cast = [new_chan(f"ccast{i}") for i in range(4)]
        R_castq = [Res() for _ in range(4)]
        R_miscq = Res()
        cast_key_idx = {}
        ch_ebb = new_chan("cebb")

        def mm(out, lhsT, rhs, start, stop, reads, writes, skip=False):
            if skip:
                return P.emit("pe", lambda e: e.matmul(out, lhsT, rhs, start=start, stop=stop, skip_group_check=True),
                              reads, writes)
            return P.emit("pe", lambda e: e.matmul(out, lhsT, rhs, start=start, stop=stop), reads, writes)

        def tr(out, in_, idn, reads, writes):
            return P.emit("pe", lambda e: e.transpose(out, in_, idn), reads, writes)

        def act(out, in_, func, reads, writes, bias=0.0, scale=1.0):
            return P.emit("act", lambda e: e.activation(out, in_, func, bias=bias, scale=scale), reads, writes)

        def tsc(eng, out, in0, s1, s2, op0, op1, reads, writes):
            if op1 is None:
                return P.emit(eng, lambda e: e.tensor_scalar(out, in0, s1, None, op0), reads, writes)
            return P.emit(eng, lambda e: e.tensor_scalar(out, in0, s1, s2, op0, op1), reads, writes)

        def stt(eng, out, in0, scalar, in1, op0, op1, reads, writes):
            return P.emit(eng, lambda e: e.scalar_tensor_tensor(out, in0, scalar, in1, op0, op1), reads, writes)

        def ttn(eng, out, in0, in1, op, reads, writes):
            return P.emit(eng, lambda e: e.tensor_tensor(out, in0, in1, op), reads, writes)

        def cpy(eng, out, in_, reads, writes):
            if eng == "act":
                return P.emit("act", lambda e: e.activation(out, in_, AF.Identity), reads, writes)
            return P.emit(eng, lambda e: e.tensor_copy(out, in_), reads, writes)

        def mset(eng, ap, val, writes):
            return P.emit(eng, lambda e: e.memset(ap, val), (), writes)

        import os as _os0
        NOOUT = _os0.environ.get("NOOUT", "0") == "1"
        out_names = ("y_prompt", "y_sample", "p_ak", "p_av", "p_bk", "p_bv", "p_conv", "s_ak", "s_av", "s_bk",
                     "s_bv", "s_conv")

        def dma(q, out, in_, reads, writes, chan, slow=False):
            if NOOUT and out.tensor.name in out_names:
                op_ = Op()
                op_.dma = None
                return op_
            if chan is ch_misc:
                writes = list(writes) + [R_miscq]
            if slow:
                return P.emit(q, lambda e: e.dma_start(out=out, in_=in_, allow_slow_non_contiguous=True),
                              reads, writes, chan)
            return P.emit(q, lambda e: e.dma_start(out=out, in_=in_), reads, writes, chan)

        def scr_res(key):
            if key not in R_scr:
                R_scr[key] = Res()
            return R_scr[key]

        def cast(dst, src, key):
            if key not in cast_key_idx:
                cast_key_idx[key] = len(cast_key_idx) % 4
            ci = cast_key_idx[key]
            dma("pool", dst, src, (), [scr_res(key), R_castq[ci]], ch_cast[ci])

        def cast_cols(scr, slot, w2d, col0, key):
            src = w2d[:, col0:col0 + 512].rearrange("(kc p) c -> p kc c", p=128)
            dst = scr[slot].rearrange("p (kc c) -> p kc c", kc=8)
            cast(dst, src, key)

        def emit_casts():
            for l in range(4):
                if l < 2:
                    for s in range(2):
                        cast_cols(sc_qa[l], s, w_qkv_a[l], s * 512, ("qa", l, s))
                    for dup in range(2):
                        for kc in range(8):
                            src = w_qkv_a[l][kc * 128:(kc + 1) * 128, 1024:1280].rearrange("p (g d) -> p g d", g=4)
                            dst = sc_qa[l][2].rearrange("p (kc g u d) -> p kc g u d", kc=8, g=4, u=2)[:, kc, :, dup, :]
                            cast(dst, src, ("qa", l, 2))
                    cast_cols(sc_qa[l], 3, w_qkv_a[l], 1024, ("qa", l, 3))
                    for s in range(2):
                        cast_cols(sc_oa[l], s, w_o_a[l], s * 512, ("oa", l, s))
                else:
                    j = l - 2
                    for s in range(2):
                        cast_cols(sc_qb[j], s, w_q_b[j], s * 512, ("qb", j, s))
                    for s in range(2):
                        cast_cols(sc_ob[j], s, w_o_b[j], s * 512, ("ob", j, s))
                for s in range(11):
                    chs = UP_ORDER[4 * s:4 * s + 4]
                    if chs[3] == chs[0] + 3:
                        cast_cols(sc_up[l], s, w_up[l], chs[0] * 128, ("up", l, s))
                    else:
                        for hh in range(2):
                            c0 = chs[2 * hh] * 128
                            src = w_up[l][:, c0:c0 + 256].rearrange("(kc p) c -> p kc c", p=128)
                            dst = sc_up[l][s].rearrange("p (kc c) -> p kc c", kc=8)[:, :, hh * 256:(hh + 1) * 256]
                            cast(dst, src, ("up", l, s))
                for m in range(8):
                    src = w_down[l][:, m * 128:(m + 1) * 128].rearrange("(kc p) c -> p kc c", p=128)
                    dst = sc_dn[l][m].rearrange("p (kc c) -> p kc c", kc=22)
                    cast(dst, src, ("dn", l, m))
                if l == 1:
                    for s in range(4):
                        cast_cols(sc_kvb, s, w_kv_b, s * 512, ("kvb", s))

        w8_n = [0]
        w22_n = [0]

        def load_w8(scr_ap, key):
            i = w8_n[0] % 3
            w8_n[0] += 1
            dma("sp", W8[i][:], scr_ap, [scr_res(key)], [R_w8[i]], ch_w8[i])
            return i

        def load_w22(scr_ap, key):
            i = w22_n[0] % 2
            w22_n[0] += 1
            dma("sp", W22[i][:], scr_ap, [scr_res(key)], [R_w22[i]], ch_w22[i])
            return i

        def w8v(i):
            return W8[i][:].rearrange("p (kc c) -> p kc c", kc=8)

        def setup():
            dma("sp", ident[:], ident_d[:, :], (), [R_const], ch_misc)
            dma("sp", anti[:], anti_d[:, :], (), [R_const], ch_misc)
            dma("sp", oh_sb, onehot_d[:, :], (), [R_mean], ch_misc)
            dma("sp", t5_sb[:], t5_table[:, :], (), [R_const], ch_misc)
            mset("dve", ones_bf[:], 1.0, [R_const])
            mset("dve", onesf[:], 1.0 / 1024.0, [R_const])
            mset("pool", carry[:], 0.0, [R_carry[l][c] for l in range(4) for c in range(NCH)])
            mset("pool", E0buf[:], 0.0, [R_e0])
            mset("pool", hT[:], 0.0, R_h)
            mset("pool", KB[:], 0.0, R_kb[0] + R_kb[1])
            for la_ in range(2):
                mset("pool", KA[la_][:], 0.0, R_ka[la_])
            for r in range(V_ROWS // 128):
                st = stage[r % 2]
                dma("sp", st[:, 0:128], vecs[r * 128:(r + 1) * 128, :], (), [R_stage[r % 2]], ch_stage[r % 2])
                b = D0 + (r % 2)
                tr(PS[b][:, 0:128], st[:, 0:128], ident[:], [R_stage[r % 2], R_const], [R_ps[b]])
                cpy("dve", VF[:, r * 128:(r + 1) * 128], PS[b][:, 0:128], [R_ps[b]], [R_vf])
            act(SCt[:], VF[:, V_C:V_C + 16], AF.Silu, [R_vf], [R_sc])
            blocks = [(w_ada[l], cb, l * 48 + cb * 2) for l in range(4) for cb in range(24)]
            blocks += [(w_ada_kv, cb, 192 + cb * 2) for cb in range(8)]
            for bi, (w2d, cb, col) in enumerate(blocks):
                i = w8_n[0] % 3
                w8_n[0] += 1
                wv = W8[i][:].bitcast(F32).rearrange("p (kc c) -> p kc c", kc=8)
                src = w2d[:, cb * 256:(cb + 1) * 256].rearrange("(kc p) c -> p kc c", p=128)
                dma("sp", wv, src, (), [R_w8[i]], ch_w8[i])
                b = D0 + (bi % 2)
                for kc in range(8):
                    mm(PS[b][0:2, 0:256], SCt[:, kc:16:8], wv[:, kc, :], kc == 0, kc == 7,
                       [R_sc, R_w8[i]], [R_ps[b]])
                cpy("act", tokA, PS[b][0:2, 0:256], [R_ps[b]], [R_tmp[0]])
                b2 = O0 + (bi % 2)
                for hh in range(2):
                    tr(PS[b2][:, hh * 2:hh * 2 + 2], tokA[0:2, hh * 128:(hh + 1) * 128], ident[0:2, 0:2],
                       [R_tmp[0], R_const], [R_ps[b2]])
                cpy("dve", ADA[:, :, col:col + 2].rearrange("p g c -> p c g"),
                    PS[b2][:, 0:4].rearrange("p (c g) -> p c g", c=2), [R_ps[b2]], [R_ada])
            for g in range(2):
                ttn("dve", ADA[:, g, :], ADA[:, g, :], VF[:, 0:208], ALU.add, [R_ada, R_vf], [R_ada])
            tsc("dve", ADA1[:], ADA[:], 1.0, None, ALU.add, None, [R_ada], [R_tabs])
            tsc("dve", ADAG[:], ADA[:], 1.0, 1.0 / ALPHA, ALU.add, ALU.mult, [R_ada], [R_tabs])
            for g in range(2):
                for l in range(4):
                    g1 = VF[:, V_LNG + (l * 2) * 8: V_LNG + (l * 2) * 8 + 8]
                    b1 = VF[:, V_LNB + (l * 2) * 8: V_LNB + (l * 2) * 8 + 8]
                    a1 = ADA1[:, g, l * 48 + 32: l * 48 + 40]
                    sh = ADA[:, g, l * 48 + 24: l * 48 + 32]
                    ttn("dve", FT[:, g, l, 0, 0, :], g1, a1, ALU.mult, [R_vf, R_tabs], [R_tabs])
                    ttn("dve", FT[:, g, l, 0, 1, :], b1, a1, ALU.mult, [R_vf, R_tabs], [R_tabs])
                    ttn("dve", FT[:, g, l, 0, 1, :], FT[:, g, l, 0, 1, :], sh, ALU.add, [R_tabs, R_ada], [R_tabs])
                    if l < 3:
                        g2 = VF[:, V_LNG + (l * 2 + 1) * 8: V_LNG + (l * 2 + 1) * 8 + 8]
                        b2 = VF[:, V_LNB + (l * 2 + 1) * 8: V_LNB + (l * 2 + 1) * 8 + 8]
                        a1 = ADA1[:, g, (l + 1) * 48 + 8: (l + 1) * 48 + 16]
                        sh = ADA[:, g, (l + 1) * 48: (l + 1) * 48 + 8]
                        ttn("dve", FT[:, g, l, 1, 0, :], g2, a1, ALU.mult, [R_vf, R_tabs], [R_tabs])
                        ttn("dve", FT[:, g, l, 1, 1, :], b2, a1, ALU.mult, [R_vf, R_tabs], [R_tabs])
                        ttn("dve", FT[:, g, l, 1, 1, :], FT[:, g, l, 1, 1, :], sh, ALU.add, [R_tabs, R_ada], [R_tabs])
                g2 = VF[:, V_LNG + 3 * 8: V_LNG + 3 * 8 + 8]
                b2 = VF[:, V_LNB + 3 * 8: V_LNB + 3 * 8 + 8]
                a1 = ADA1[:, g, 200:208]
                sh = ADA[:, g, 192:200]
                ttn("dve", FKV[:, g, 0, :], g2, a1, ALU.mult, [R_vf, R_tabs], [R_tabs])
                ttn("dve", FKV[:, g, 1, :], b2, a1, ALU.mult, [R_vf, R_tabs], [R_tabs])
                ttn("dve", FKV[:, g, 1, :], FKV[:, g, 1, :], sh, ALU.add, [R_tabs, R_ada], [R_tabs])
            for l in range(2):
                for par in range(2):
                    src = bass.AP(sinks_a.tensor, l * 16 + par, [[0, 64], [2, 8]])
                    dma("sp", esink[par * 64:(par + 1) * 64, l, :], src, (), [R_esink], ch_misc, slow=True)
            act(esink[:], esink[:], AF.Exp, [R_esink], [R_esink])
            mm(PS[D0][0:16, 0:510], t5_sb[:], oh_sb, True, True, [R_const, R_mean], [R_ps[D0]])
            cpy("dve", vec_sb, PS[D0][0:16, 0:510], [R_ps[D0]], [R_var])
            dma("sp", sc_veca[:, :], vec_sb, [R_var], [scr_res("veca")], ch_misc)
            make_eb(sc_veca, "veca", EBA, R_eba)
            mset("pool", EBA[0:64, 0, :, :, 64:128], 0.0, [R_eba])
            mset("pool", EBA[64:128, 1, :, :, 0:64], 0.0, [R_eba])
            for j in range(2):
                dma("sp", rp_sb, relpos_b[j], (), [R_rstd], ch_misc)
                mset("dve", vec_sb, 0.0, [R_var])
                tsc("dve", vec_sb[:, 0:128], rp_sb[:, 129:257], rp_sb[:, 256:257], None, ALU.subtract, None,
                    [R_rstd, R_var], [R_var])
                tsc("dve", vec_sb[:, 255:510], rp_sb[:, 1:256], rp_sb[:, 256:257], None, ALU.subtract, None,
                    [R_rstd, R_var], [R_var])
                dma("sp", sc_vecb[j][:, :], vec_sb, [R_var], [scr_res(("vecb", j))], ch_misc)
                make_eb(sc_vecb[j], ("vecb", j), EBB, R_ebb)
                mset("pool", EBB[64:128, 1, :, :, 0:64], 0.0, [R_ebb])
                dma("sp", sc_ebb[j][:, :], EBB[:].rearrange("p a b c d -> p (a b c d)"), [R_ebb],
                    [scr_res(("ebb", j))], ch_misc)

        def make_eb(vec_dram, key, EB, R_eb):
            for ty in range(2):
                i = w8_n[0] % 3
                w8_n[0] += 1
                brev = W8[i][:].bitcast(F32).rearrange("p (h q) -> p h q", h=16)
                src = bass.AP(vec_dram.tensor, ty * 255, [[1, 128], [510, 16], [1, 128]])
                dma("sp", brev, src, [scr_res(key)], [R_w8[i]], ch_w8[i])
                for par in range(2):
                    for jg in range(2):
                        b = D0 + ((par * 2 + jg) % 2)
                        rhs = brev[:, par + 8 * jg: par + 8 * jg + 7: 2, :]
                        mm(PS[b][:, :], anti[:], rhs, True, True, [R_const, R_w8[i]], [R_ps[b]])
                        act(EB[:, ty, par, jg * 4:(jg + 1) * 4, :],
                            PS[b][:, :].rearrange("p (j q) -> p j q", j=4), AF.Exp, [R_ps[b]], [R_eb])

        def tab(g, l, part, c):
            col = l * 48 + part * 8 + c
            return col

        def load_x_group(src_rows, nrows, si):
            dma("sp", stage[si][0:nrows, :], src_rows, (), [R_stage[si]], ch_stage[si])

        def x_in(xsrc, T, g, first_groups_loaded=0):
            ngr = (T + 127) // 128
            for gi in range(ngr):
                nr = min(128, T - gi * 128)
                si = gi % 2
                if gi >= first_groups_loaded:
                    load_x_group(xsrc[gi * 128: gi * 128 + nr, :], nr, si)
                for half in range(2):
                    b = D0 + half
                    for cc in range(4):
                        c = half * 4 + cc
                        tr(PS[b][:, cc * 128: cc * 128 + nr], stage[si][0:nr, c * 128:(c + 1) * 128],
                           ident[0:nr, 0:nr], [R_stage[si], R_const], [R_ps[b]])
                    out = xT[:, half * 4:(half + 1) * 4, gi * 128: gi * 128 + nr]
                    inp = PS[b][:, :].rearrange("p (c t) -> p c t", c=4)[:, :, 0:nr]
                    if half == 0:
                        cpy("act", out, inp, [R_ps[b]], R_x[0:4])
                    else:
                        cpy("dve", out, inp, [R_ps[b]], R_x[4:8])

        def h_from_x(g, l, T):
            for c in range(C8):
                act(hT[:, c, 0:T], xT[:, c, 0:T], AF.Identity, [R_x[c], R_tabs], [R_h[c]],
                    bias=ADA[:, g, tab(g, l, 0, c): tab(g, l, 0, c) + 1],
                    scale=ADA1[:, g, tab(g, l, 1, c): tab(g, l, 1, c) + 1])

        dcount = [0]

        DRING = [D0, D1, S0, S1, O0, O1]

        def next_d():
            b = DRING[dcount[0] % len(DRING)]
            dcount[0] += 1
            return b

        def proj_fm(slot_keys, scr_list, KCn, src_fn, src_res, n_out, T, evac):
            cur = None
            for m in range(n_out):
                if m % 4 == 0:
                    cur = load_w8(scr_list[m // 4], slot_keys[m // 4])
                wv = w8v(cur)
                b = next_d()
                for kc in range(KCn):
                    mm(PS[b][:, 0:T], wv[:, kc, (m % 4) * 128:(m % 4 + 1) * 128], src_fn(kc), kc == 0, kc == KCn - 1,
                       [R_w8[cur], src_res[kc]], [R_ps[b]])
                evac(m, b)

        def layer_norm(g, l, sub, T, ft1, ft2, h_out):
            for c in range(C8):
                i = c % 2
                act(ybf[i][:, 0:T], xT[:, c, 0:T], AF.Copy, [R_x[c]], [R_ybf[i]])
                act(sqb[i][:, 0:T], xT[:, c, 0:T], AF.Square, [R_x[c]], [R_sq[i]])
                mm(PS[D0][:, 0:T], onesf[:], ybf[i][:, 0:T], c == 0, c == 7, [R_const, R_ybf[i]], [R_ps[D0]])
                mm(PS[D1][:, 0:T], onesf[:], sqb[i][:, 0:T], c == 0, c == 7, [R_const, R_sq[i]], [R_ps[D1]])
            cpy("act", mean_sb[:, 0:T], PS[D0][:, 0:T], [R_ps[D0]], [R_mean])
            ttn("dve", var_sb[:, 0:T], mean_sb[:, 0:T], mean_sb[:, 0:T], ALU.mult, [R_mean], [R_var])
            ttn("dve", var_sb[:, 0:T], PS[D1][:, 0:T], var_sb[:, 0:T], ALU.subtract, [R_ps[D1], R_var], [R_var])
            tsc("dve", var_sb[:, 0:T], var_sb[:, 0:T], EPS_P, None, ALU.add, None, [R_var], [R_var])
            act(var_sb[:, 0:T], var_sb[:, 0:T], AF.Sqrt, [R_var], [R_var])
            P.emit("dve", lambda e: e.reciprocal(rstd_sb[:, 0:T], var_sb[:, 0:T]), [R_var], [R_rstd])
            lng = V_LNG + (l * 2 + sub) * 8
            lnb = V_LNB + (l * 2 + sub) * 8
            for c in range(C8):
                i = c % 2
                eng = "pool" if c % 2 == 0 else "dve"
                ttn(eng, tmpb[i][:, 0:T], xT[:, c, 0:T], mean_sb[:, 0:T], ALU.subtract, [R_x[c], R_mean], [R_tmp[i]])
                ttn(eng, tmpb[i][:, 0:T], tmpb[i][:, 0:T], rstd_sb[:, 0:T], ALU.mult, [R_tmp[i], R_rstd], [R_tmp[i]])
                act(xT[:, c, 0:T], tmpb[i][:, 0:T], AF.Identity, [R_tmp[i], R_vf], [R_x[c]],
                    bias=VF[:, lnb + c: lnb + c + 1], scale=VF[:, lng + c: lng + c + 1])
                if h_out:
                    act(hT[:, c, 0:T], tmpb[i][:, 0:T], AF.Identity, [R_tmp[i], R_tabs], [R_h[c]],
                        bias=ft2[:, c:c + 1], scale=ft1[:, c:c + 1])

        def attention(l, g, T, pairs, tiles_of_pair, is_A):
            QT = actT
            OT = hT
            groups = []
            for pi, (q0, nq) in enumerate(pairs):
                kts = tiles_of_pair(pi)
                for jg in range(2):
                    for par in range(2):
                        for ti, kt in enumerate(kts):
                            groups.append((pi, q0, nq, jg, par, ti, len(kts), kt))
            nG = len(groups)
            state = {}
            pjcount = [0]

            def rec_S(gi):
                pi, q0, nq, jg, par, ti, nkt, kt = groups[gi]
                b = (S0, S1, D0, D1)[gi % 4]
                nk = kt["nk"]
                for jj in range(4):
                    j = jg * 4 + jj
                    mm(PS[b][0:max(nk, 32), jj * nq:(jj + 1) * nq], kt["kT"](par, j),
                       QT[par * 64:(par + 1) * 64, j, q0:q0 + nq],
                       True, True, kt["kres"] + [R_act[j]], [R_ps[b]])
                ty = kt["type"]
                if ty == "m0":
                    ebuf, eres = E0buf, R_e0
                else:
                    ebuf, eres = Ebuf[gi % 5], R_e[gi % 5]
                ev = ebuf[:, 0:4 * nq].rearrange("p (j q) -> p j q", j=4)
                sv = PS[b][:, 0:4 * nq].rearrange("p (j q) -> p j q", j=4)
                if ty == "m0":
                    n1 = min(nq, 64)
                    act(ev[0:nk, :, 0:n1], sv[0:nk, :, 0:n1], AF.Exp, [R_ps[b]], [eres], scale=SCALE)
                    if nq > 64:
                        act(ev[64:128, :, 64:nq], sv[64:128, :, 64:nq], AF.Exp, [R_ps[b]], [eres], scale=SCALE)
                else:
                    act(ebuf[0:nk, 0:4 * nq], PS[b][0:nk, 0:4 * nq], AF.Exp, [R_ps[b]], [eres], scale=SCALE)
                if ty in ("e0", "e1"):
                    t_i = 0 if ty == "e0" else 1
                    EB, R_eb = (EBA, R_eba) if is_A else (EBB, R_ebb)
                    eng = "dve" if gi % 4 != 3 else "pool"
                    ttn(eng, ev[0:nk, :, :], ev[0:nk, :, :], EB[0:nk, t_i, par, jg * 4:(jg + 1) * 4, 0:nq], ALU.mult,
                        [eres, R_eb], [eres])
                state[gi] = (ebuf, eres)

            def rec_DP(gi):
                pi, q0, nq, jg, par, ti, nkt, kt = groups[gi]
                ebuf, eres = state.pop(gi)
                nk = kt["nk"]
                pj = pjcount[0]
                ob = O0 + (pj % 2)
                nb = N0 + (pj % 2)
                first = ti == 0
                last = ti == nkt - 1
                mm(PS[nb][par * 64:(par + 1) * 64, 0:4 * nq], ones_bf[0:nk, :], ebuf[0:nk, 0:4 * nq], first, last,
                   [R_const, eres], [R_ps[nb]])
                for jj in range(4):
                    j = jg * 4 + jj
                    mm(PS[ob][par * 64:(par + 1) * 64, jj * nq:(jj + 1) * nq], kt["v"](par, j),
                       ebuf[0:nk, jj * nq:(jj + 1) * nq], first and jj == 0, last and jj == 3,
                       kt["vres"] + [eres], [R_ps[ob]], skip=True)
                if last and par == 1:
                    n4 = 4 * nq
                    if is_A:
                        ttn("dve", Tbuf[:, 0:n4].rearrange("p (j q) -> p j q", j=4),
                            PS[nb][:, 0:n4].rearrange("p (j q) -> p j q", j=4),
                            esink[:, l, jg * 4:(jg + 1) * 4].unsqueeze(2).to_broadcast([128, 4, nq]), ALU.add,
                            [R_ps[nb], R_esink], [R_t])
                        P.emit("dve", lambda e: e.reciprocal(R2buf[:, 0:n4], Tbuf[:, 0:n4]), [R_t], [R_r2])
                    else:
                        P.emit("dve", lambda e: e.reciprocal(R2buf[:, 0:n4], PS[nb][:, 0:n4]), [R_ps[nb]], [R_r2])
                    ttn("dve", OT[:, jg * 4:(jg + 1) * 4, q0:q0 + nq],
                        PS[ob][:, 0:n4].rearrange("p (j q) -> p j q", j=4),
                        R2buf[:, 0:n4].rearrange("p (j q) -> p j q", j=4), ALU.mult,
                        [R_ps[ob], R_r2], R_h[jg * 4:(jg + 1) * 4])
                    pjcount[0] += 1

            for gi in range(nG + 3):
                if gi < nG:
                    rec_S(gi)
                if gi >= 3:
                    rec_DP(gi - 3)

        def ffn(l, g, T, last_out):
            cur = None
            for oi, ch in enumerate(UP_ORDER):
                if oi % 4 == 0:
                    cur = load_w8(sc_up[l][oi // 4], ("up", l, oi // 4))
                wv = w8v(cur)
                b = next_d()
                for kc in range(8):
                    mm(PS[b][:, 0:T], wv[:, kc, (oi % 4) * 128:(oi % 4 + 1) * 128], hT[:, kc, 0:T], kc == 0, kc == 7,
                       [R_w8[cur], R_h[kc]], [R_ps[b]])
                isg = ch >= FC
                i = ch - FC if isg else ch
                bq = oi % 4
                U = Ubuf[bq // 2][bq % 2]
                Y = Ybuf[bq // 2][bq % 2]
                RU = R_u[bq // 2][bq % 2]
                RUH = R_uh[bq // 2][bq % 2]
                RY = R_y[bq // 2][bq % 2]
                w0 = VF[:, V_CW + (l * 3 + 0) * NCH + ch: V_CW + (l * 3 + 0) * NCH + ch + 1]
                w1 = VF[:, V_CW + (l * 3 + 1) * NCH + ch: V_CW + (l * 3 + 1) * NCH + ch + 1]
                w2 = VF[:, V_CW + (l * 3 + 2) * NCH + ch: V_CW + (l * 3 + 2) * NCH + ch + 1]
                cb = VF[:, V_CB + l * NCH + ch: V_CB + l * NCH + ch + 1]
                cpy("act", U[:, 0:2], carry[:, l, ch, :], [R_carry[l][ch]], [RUH])
                cpy("act", U[:, 2:2 + T], PS[b][:, 0:T], [R_ps[b]], [RU])
                cpy("act", carry[:, l, ch, :], PS[b][:, T - 2:T], [R_ps[b]], [R_carry[l][ch]])
                if isg:
                    act(Y[:, 0:T], PS[b][:, 0:T], AF.Identity, [R_ps[b], R_vf], [RY], bias=cb, scale=w2)
                    stt("dve", Y[:, 0:T], U[:, 1:1 + T], w1, Y[:, 0:T], ALU.mult, ALU.add, [RU, RUH, RY, R_vf], [RY])
                    stt("dve", Y[:, 0:T], U[:, 0:T], w0, Y[:, 0:T], ALU.mult, ALU.add, [RU, RUH, RY, R_vf], [RY])
                    act(actT[:, i, 0:T], Y[:, 0:T], AF.Gelu_apprx_tanh, [RY], [R_act[i]])
                else:
                    act(Y[:, 0:T], PS[b][:, 0:T], AF.Identity, [R_ps[b], R_vf], [RY], bias=cb, scale=w2)
                    stt("dve", Y[:, 0:T], U[:, 1:1 + T], w1, Y[:, 0:T], ALU.mult, ALU.add, [RU, RUH, RY, R_vf], [RY])
                    stt("dve", Y[:, 0:T], U[:, 0:T], w0, Y[:, 0:T], ALU.mult, ALU.add, [RU, RUH, RY, R_vf], [RY])
                    ttn("pool", actT[:, i, 0:T], Y[:, 0:T], actT[:, i, 0:T], ALU.mult, [RY, R_act[i]], [R_act[i]])
            if last_out is not None:
                for r in range(2):
                    dst = bass.AP(last_out.tensor, l * 2 * 2 * DFF + r * 2 * DFF, [[1, 128], [128, NCH]])
                    o = dma("sp", dst, carry[:, l, :, r], [R_carry[l][c] for c in range(NCH)], [Res()], ch_misc,
                            slow=True)
                    R_out.append(o)
            for m in range(8):
                wi = load_w22(sc_dn[l][m], ("dn", l, m))
                wv = W22[wi][:].rearrange("p (kc c) -> p kc c", kc=22)
                b = next_d()
                for kc in range(FC):
                    mm(PS[b][:, 0:T], wv[:, kc, :], actT[:, kc, 0:T], kc == 0, kc == FC - 1,
                       [R_w22[wi], R_act[kc]], [R_ps[b]])
                gate = ADAG[:, g, tab(g, l, 5, m): tab(g, l, 5, m) + 1]
                stt("dve", xT[:, m, 0:T], PS[b][:, 0:T], gate, xT[:, m, 0:T], ALU.mult, ALU.add,
                    [R_ps[b], R_x[m], R_tabs], [R_x[m]])

        def tok_major(src_chunks_res, T, w_slot, col0, ncols, dst_fn, f32_out_fn):
            wv = w8v(w_slot)
            ngr = (T + 127) // 128
            for gi in range(ngr):
                nr = min(128, T - gi * 128)
                mr = max(nr, 32)
                b = next_d()
                for kc in range(8):
                    mm(PS[b][0:mr, 0:ncols], hT[:, kc, gi * 128: gi * 128 + mr], wv[:, kc, col0:col0 + ncols],
                       kc == 0, kc == 7, [R_h[kc], R_w8[w_slot]], [R_ps[b]])
                if dst_fn is not None:
                    dst_fn(gi, nr, b)
                if f32_out_fn is not None:
                    f32_out_fn(gi, nr, b)

        def run_tile(g, T, t, xsrc, ydst, is_last, outs):
            sample = g == 1
            npairs = (T + 127) // 128
            pairs = [(p * 128, min(128, T - p * 128)) for p in range(npairs)]
            cur_half = 1 if sample else t % 2
            prev_half = 1 - cur_half
            import os as _os2
            KSUB = int(_os2.environ.get("KSUB", "99"))
            x_in(xsrc, T, g)
            h_from_x(g, 0, T)
            if KSUB <= 1:
                return
            for l in range(4):
                is_A = l < 2
                if is_A:
                    la = l
                    def evq(m, b):
                        eng = "act" if m % 2 == 0 else "dve"
                        cpy(eng, actT[:, m, 0:T], PS[b][:, 0:T], [R_ps[b]], [R_act[m]])
                    proj_fm([("qa", la, 0), ("qa", la, 1)], [sc_qa[la][0], sc_qa[la][1]], 8,
                            lambda kc: hT[:, kc, 0:T], R_h, 8, T, evq)
                    KS2 = int(_os2.environ.get("KSUB2", "99"))
                    if KS2 <= 1:
                        return
                    kslot = load_w8(sc_qa[la][2], ("qa", la, 2))
                    wv = w8v(kslot)
                    for g4 in range(4):
                        b = next_d()
                        for kc in range(8):
                            mm(PS[b][:, 0:T], wv[:, kc, g4 * 128:(g4 + 1) * 128], hT[:, kc, 0:T], kc == 0, kc == 7,
                               [R_w8[kslot], R_h[kc]], [R_ps[b]])
                        eng = "act" if g4 % 2 == 0 else "dve"
                        cpy(eng, KA[la][:, g4, 128:128 + T], PS[b][:, 0:T], [R_ps[b]], R_ka[la][1:5])
                    if KS2 <= 2:
                        return
                    kvslot = load_w8(sc_qa[la][3], ("qa", la, 3))
                    KS3 = int(_os2.environ.get("KS3", "99"))
                    if KS3 <= 1:
                        return
                    if KS3 <= 2:
                        tok_major(R_h, T, kvslot, 256, 256, None, None)
                        return

                    def vdst(gi, nr, b):
                        if _os2.environ.get("NOVD3") and gi == 3 and not sample:
                            return
                        cpy("act" if nr == 128 else "dve", VA[la][0:nr, 1 + gi, :], PS[b][0:nr, 0:256], [R_ps[b]],
                            [R_va[la][1 + gi]])
                    want_out = (is_last or sample) and _os2.environ.get("NOWANT", "0") != "1"
                    if want_out:
                        okd, ovd = outs["ak"], outs["av"]

                        def vout(gi, nr, b):
                            if sample or gi == 3:
                                si = 0
                                if _os2.environ.get("ALT"):
                                    if _os2.environ.get("ALT") == "2":
                                        cpy("dve", R2buf[0:nr, 0:256], Tbuf[0:nr, 0:256], [R_ps[b]], [R_r2])
                                    elif _os2.environ.get("ALT") == "3":
                                        cpy("dve", R2buf[0:nr, 0:256], PS[b][0:nr, 0:256], [R_ps[b]], [R_r2])
                                        cpy("dve", Tbuf[0:nr, 0:256], PS[b][0:nr, 0:256], [R_ps[b]], [R_t])
                                    else:
                                        tsc("dve", R2buf[0:nr, 0:256], PS[b][0:nr, 0:256], 1.0, None, ALU.mult, None, [R_ps[b]], [R_r2])
                                else:
                                    cpy(_os2.environ.get("VENG2", "dve"), stage[si][0:nr, 0:256], PS[b][0:nr, 0:256], [R_ps[b]], [R_stage[si]])
                                o = dma("sp", ovd[la][0:nr, :], stage[si][0:nr, 0:256], [R_stage[si]], [Res()],
                                        ch_stage[si])
                                R_out.append(o)

                        def kout(gi, nr, b):
                            if sample or gi == 3:
                                si = 1
                                cpy(_os2.environ.get("VENG2", "dve"), stage[si][0:nr, 0:256], PS[b][0:nr, 0:256], [R_ps[b]], [R_stage[si]])
                                o = dma("sp", okd[la][0:nr, :], stage[si][0:nr, 0:256], [R_stage[si]], [Res()],
                                        ch_stage[si])
                                R_out.append(o)
                        tok_major(R_h, T, kvslot, 256, 256, vdst, None if _os2.environ.get("NOV") else vout)
                        if not _os2.environ.get("NOK"):
                            tok_major(R_h, T, kvslot, 0, 256, None, kout)
                    else:
                        tok_major(R_h, T, kvslot, 256, 256, vdst, None)

                    if KSUB <= 2:
                        return
                    def tiles_of_pair(pi, la=la):
                        res = []
                        for ty, ti in (("e0", pi), ("e1", pi + 1)):
                            if ti == 0 and (not sample) and t == 0:
                                continue
                            nk = 128
                            if ti >= 1:
                                nk = min(128, T - (ti - 1) * 128)
                            res.append({
                                "nk": nk, "type": ty,
                                "kT": (lambda par, j, ti=ti, nk=nk: KA[la][par * 64:(par + 1) * 64, j // 2,
                                                                            ti * 128: ti * 128 + max(nk, 32)]),
                                "kres": [R_ka[la][ti]],
                                "v": (lambda par, j, ti=ti, nk=nk: VA[la][0:nk, ti, (j // 2) * 64:(j // 2) * 64 + 64]),
                                "vres": [R_va[la][ti]],
                            })
                        return res
                    attention(la, g, T, pairs, tiles_of_pair, True)
                    if KSUB <= 3:
                        return
                    if not sample:
                        cpy("pool", KA[la][:, :, 0:128], KA[la][:, :, 512:640], [R_ka[la][4]], [R_ka[la][0]])
                        cpy("pool", VA[la][:, 0, :], VA[la][:, 4, :], [R_va[la][4]], [R_va[la][0]])
                    wo_keys = [("oa", la, 0), ("oa", la, 1)]
                    wo_scr = [sc_oa[la][0], sc_oa[la][1]]
                else:
                    jb = l - 2
                    dma("sp", EBB[:].rearrange("p a b c d -> p (a b c d)"), sc_ebb[jb][:, :],
                        [scr_res(("ebb", jb))], [R_ebb], ch_ebb)

                    def evq(m, b):
                        eng = "act" if m % 2 == 0 else "dve"
                        cpy(eng, actT[:, m, 0:T], PS[b][:, 0:T], [R_ps[b]], [R_act[m]])
                    proj_fm([("qb", jb, 0), ("qb", jb, 1)], [sc_qb[jb][0], sc_qb[jb][1]], 8,
                            lambda kc: hT[:, kc, 0:T], R_h, 8, T, evq)

                    def tiles_of_pair(pi):
                        res = []
                        for k5 in range(5):
                            ti = pi + k5
                            if ti < 4 and (not sample) and t == 0:
                                continue
                            half = prev_half if ti < 4 else cur_half
                            tl = ti % 4
                            nk = 128 if ti < 4 else min(128, T - tl * 128)
                            ty = ("m0", "n", "n", "e0", "e1")[k5]
                            res.append({
                                "nk": nk, "type": ty,
                                "kT": (lambda par, j, half=half, tl=tl, nk=nk:
                                       KB[par * 64:(par + 1) * 64, half, j, tl * 128: tl * 128 + max(nk, 32)]),
                                "kres": [R_kb[half][tl]],
                                "v": (lambda par, j, half=half, tl=tl, nk=nk:
                                      VB[0:nk, half, tl, (2 * j + par) * 64:(2 * j + par) * 64 + 64]),
                                "vres": [R_vb[half][tl]],
                            })
                        return res
                    attention(l, g, T, pairs, tiles_of_pair, False)
                    wo_keys = [("ob", jb, 0), ("ob", jb, 1)]
                    wo_scr = [sc_ob[jb][0], sc_ob[jb][1]]

                def evo(m, b, l=l):
                    gate = ADAG[:, g, tab(g, l, 2, m): tab(g, l, 2, m) + 1]
                    stt("dve", xT[:, m, 0:T], PS[b][:, 0:T], gate, xT[:, m, 0:T], ALU.mult, ALU.add,
                        [R_ps[b], R_x[m], R_tabs], [R_x[m]])
                proj_fm(wo_keys, wo_scr, 8, lambda kc: hT[:, kc, 0:T], R_h, 8, T, evo)
                layer_norm(g, l, 0, T, FT[:, g, l, 0, 0, :], FT[:, g, l, 0, 1, :], True)
                if KSUB <= 4:
                    return
                conv_out = outs["conv"] if (is_last or sample) else None
                ffn(l, g, T, conv_out)
                if KSUB <= 5:
                    return
                if l == 1:
                    layer_norm(g, l, 1, T, FKV[:, g, 0, :], FKV[:, g, 1, :], True)
                    def evk(m, b):
                        eng = "act" if m % 2 == 0 else "dve"
                        if T >= 128:
                            cpy(eng, KB[:, cur_half, m, 0:T], PS[b][:, 0:T], [R_ps[b]], R_kb[cur_half])
                        else:
                            cpy(eng, KB[:, cur_half, m, 0:T], PS[b][:, 0:T], [R_ps[b]], [R_kb[cur_half][0]])
                    proj_fm([("kvb", 0), ("kvb", 1)], [sc_kvb[0], sc_kvb[1]], 8,
                            lambda kc: hT[:, kc, 0:T], R_h, 8, T, evk)
                    want_out = is_last or sample
                    for hv in range(2):
                        vslot = load_w8(sc_kvb[2 + hv], ("kvb", 2 + hv))

                        def vdst(gi, nr, b, hv=hv):
                            cpy("act" if nr == 128 else "dve", VB[0:nr, cur_half, gi, hv * 512:(hv + 1) * 512],
                                PS[b][0:nr, 0:512], [R_ps[b]], [R_vb[cur_half][gi]])

                        def vout(gi, nr, b, hv=hv):
                            si = gi % 2
                            cpy("dve", stage[si][0:nr, 0:512], PS[b][0:nr, 0:512], [R_ps[b]], [R_stage[si]])
                            o = dma("sp", outs["bv"][gi * 128: gi * 128 + nr, hv * 512:(hv + 1) * 512],
                                    stage[si][0:nr, 0:512], [R_stage[si]], [Res()], ch_stage[si])
                            R_out.append(o)
                        tok_major(R_h, T, vslot, 0, 512, vdst, vout if want_out else None)
                    if want_out:
                        for hk in range(2):
                            kslot = load_w8(sc_kvb[hk], ("kvb", hk))

                            def kout(gi, nr, b, hk=hk):
                                si = gi % 2
                                cpy("dve", stage[si][0:nr, 0:512], PS[b][0:nr, 0:512], [R_ps[b]], [R_stage[si]])
                                o = dma("sp", outs["bk"][gi * 128: gi * 128 + nr, hk * 512:(hk + 1) * 512],
                                        stage[si][0:nr, 0:512], [R_stage[si]], [Res()], ch_stage[si])
                                R_out.append(o)
                            tok_major(R_h, T, kslot, 0, 512, None, kout)
                    h_from_x(g, 2, T)
                elif l < 3:
                    layer_norm(g, l, 1, T, FT[:, g, l, 1, 0, :], FT[:, g, l, 1, 1, :], True)
                else:
                    layer_norm(g, l, 1, T, None, None, False)
            ngr = (T + 127) // 128
            for gi in range(ngr):
                nr = min(128, T - gi * 128)
                si = gi % 2
                for half in range(2):
                    b = next_d()
                    for cc in range(4):
                        c = half * 4 + cc
                        tr(PS[b][0:nr, cc * 128:(cc + 1) * 128], xT[:, c, gi * 128: gi * 128 + nr], ident[:],
                           [R_x[c], R_const], [R_ps[b]])
                    eng = "act" if half == 0 else "dve"
                    cpy(eng, stage[si][0:nr, half * 512:(half + 1) * 512], PS[b][0:nr, 0:512], [R_ps[b]], [R_stage[si]])
                o = dma("sp", ydst[gi * 128: gi * 128 + nr, :], stage[si][0:nr, :], [R_stage[si]], [Res()], ch_stage[si])
                R_out.append(o)

        def load_sample_caches():
            for la in range(2):
                st = stage[0]
                stv = st[:, 0:512].rearrange("p (g u d) -> p g u d", g=4, u=2)
                for dup in range(2):
                    dma("sp", stv[:, :, dup, :], cache_a_k[la].rearrange("k (g d) -> k g d", g=4), (), [R_stage[0]],
                        ch_stage[0])
                for g4 in range(4):
                    b = next_d()
                    tr(PS[b][:, 0:128], st[:, g4 * 128:(g4 + 1) * 128], ident[:], [R_stage[0], R_const], [R_ps[b]])
                    cpy("dve", KA[la][:, g4, 0:128], PS[b][:, 0:128], [R_ps[b]], [R_ka[la][0]])
                dma("sp", stage[1][:, 0:256], cache_a_v[la], (), [R_stage[1]], ch_stage[1])
                cpy("dve", VA[la][:, 0, :], stage[1][:, 0:256], [R_stage[1]], [R_va[la][0]])
            for tl in range(4):
                si = tl % 2
                dma("sp", stage[si][:, :], cache_b_k[tl * 128:(tl + 1) * 128, :], (), [R_stage[si]], ch_stage[si])
                for half in range(2):
                    b = next_d()
                    for cc in range(4):
                        c = half * 4 + cc
                        tr(PS[b][:, cc * 128:(cc + 1) * 128], stage[si][:, c * 128:(c + 1) * 128], ident[:],
                           [R_stage[si], R_const], [R_ps[b]])
                    cpy("dve" if half else "act", KB[:, 0, half * 4:(half + 1) * 4, tl * 128:(tl + 1) * 128],
                        PS[b][:, :].rearrange("p (c t) -> p c t", c=4), [R_ps[b]], [R_kb[0][tl]])
            for tl in range(4):
                si = tl % 2
                dma("sp", stage[si][:, :], cache_b_v[tl * 128:(tl + 1) * 128, :], (), [R_stage[si]], ch_stage[si])
                cpy("dve", VB[:, 0, tl, :], stage[si][:, :], [R_stage[si]], [R_vb[0][tl]])
            for l in range(4):
                for r in range(2):
                    src = bass.AP(state_conv.tensor, l * 2 * 2 * DFF + r * 2 * DFF, [[1, 128], [128, NCH]])
                    dma("sp", carry[:, l, :, r], src, (), [R_carry[l][c] for c in range(NCH)], ch_misc, slow=True)

        import os as _os
        KSTOP = int(_os.environ.get("KSTOP", "99"))
        if KSTOP >= 1:
            emit_casts()
        if KSTOP >= 2:
            setup()
        P.epoch = 1
        if KSTOP >= 3:
            load_sample_caches()
        if KSTOP >= 4:
            run_tile(1, DEC, -1, x_sample, y_sample, False,
                     {"ak": s_ak, "av": s_av, "bk": s_bk, "bv": s_bv, "conv": s_conv})
            mset("pool", carry[:], 0.0, [R_carry[l][c] for l in range(4) for c in range(NCH)])
        for t in range(NT if KSTOP >= 5 else 0):
            P.epoch = 2 + t
            run_tile(0, TT, t, x_prompt[t * TT:(t + 1) * TT, :], y_prompt[t * TT:(t + 1) * TT, :], t == NT - 1,
                     {"ak": p_ak, "av": p_av, "bk": p_bk, "bv": p_bv, "conv": p_conv})
        n_epochs = NT + 2

        eng_sems = {}
        for e in ENGS:
            users = set()
            for op in P.ops[e]:
                if op.dma is None:
                    users |= op.sigs
            for c in sorted(users):
                eng_sems[(e, c)] = new_sem(f"s_{e}_{c}")
        for e in ENGS:
            cnt = {}
            for op in P.ops[e]:
                if op.dma is None:
                    for c in op.sigs:
                        cnt[c] = cnt.get(c, 0) + 1
                        op.cnts[c] = cnt[c]
            print("SEMCNT", e, cnt)
            for op in P.ops[e]:
                if op.dma is None and len(op.sigs) > 1:
                    print("MULTISIG", e, op.pos, op.epoch, op.sigs)
                    break
        out_ops = list(R_out)

        if _os.environ.get("DUMPW"):
            for en in ENGS:
                for op in P.ops[en][-6:]:
                    print("TAIL", en, op.pos, op.epoch, "sig", op.sig, op.cnt, "dma", None if op.dma is None else op.dmaval,
                          [(d.eng, d.epoch, d.pos, d.cnt if d.dma is None else ("dma", d.dmaval)) for d in op.waits])

        def replay(eng_name, e):
            for op in P.ops[eng_name]:
                for d in op.waits:
                    if d.dma is not None:
                        e.wait_ge(d.dma.sem, d.dmaval)
                    else:
                        e.wait_ge(eng_sems[(d.eng, eng_name)], d.cnts[eng_name])
                if op.fn is None:
                    continue
                ins = op.fn(e)
                if op.dma is not None:
                    ins.then_inc(op.dma.sem, 16)
                else:
                    for c in sorted(op.sigs):
                        ins.then_inc(eng_sems[(op.eng, c)], 1)
            if eng_name == "sp":
                done = {}
                for o in out_ops:
                    if o.dma is None:
                        continue
                    key = id(o.dma)
                    if key not in done or done[key][1] < o.dmaval:
                        done[key] = (o.dma, o.dmaval)
                for chn, val in done.values():
                    e.wait_ge(chn.sem, val)

        with nc.Block() as block:
            @block.tensor
            def _(e):
                replay("pe", e)

            @block.scalar
            def _(e):
                replay("act", e)

            @block.vector
            def _(e):
                replay("dve", e)

            @block.gpsimd
            def _(e):
                replay("pool", e)

            @block.sync
            def _(e):
                replay("sp", e)
    stats = {e: len(P.ops[e]) for e in ENGS}
    print("SBUF remaining", nc.sbuf_bytes_remaining)
    return nc, stats


_CACHE = {}


def _host_inputs(inputs, NT):
    f = lambda a: np.ascontiguousarray(np.asarray(a, dtype=np.float32))
    shared = {}
    for k in ("w_ada", "w_qkv_a", "w_o_a", "sinks_a", "t5_table", "w_ada_kv", "w_kv_b", "w_q_b", "w_o_b",
              "relpos_b", "w_up", "w_down"):
        shared[k] = f(inputs[k])
    shared["ident"] = np.eye(128, dtype=np.float32)
    shared["anti"] = np.ascontiguousarray(np.eye(128, dtype=np.float32)[::-1])
    shared["t5oh"] = _t5_onehot()
    in_maps = []
    for b in range(8):
        vec = np.zeros((V_ROWS, 128), np.float32)
        vec[V_BADA:V_BADA + 192] = f(inputs["b_ada"]).reshape(192, 128)
        vec[V_BKV:V_BKV + 16] = f(inputs["b_ada_kv"]).reshape(16, 128)
        vec[V_LNG:V_LNG + 64] = f(inputs["ln_g"]).reshape(64, 128)
        vec[V_LNB:V_LNB + 64] = f(inputs["ln_b"]).reshape(64, 128)
        vec[V_CW:V_CW + 528] = f(inputs["conv_w"]).reshape(528, 128)
        vec[V_CB:V_CB + 176] = f(inputs["conv_b"]).reshape(176, 128)
        vec[V_C:V_C + 8] = f(inputs["c_prompt"])[b].reshape(8, 128)
        vec[V_C + 8:V_C + 16] = f(inputs["c_sample"])[b].reshape(8, 128)
        m = dict(shared)
        m["vecs"] = vec
        m["x_prompt"] = f(inputs["x_prompt"][b])
        m["x_sample"] = f(inputs["x_sample"][b])
        m["cache_a_k"] = f(inputs["cache_a_k"][:, b]).reshape(2, 128, 256)
        m["cache_a_v"] = f(inputs["cache_a_v"][:, b]).reshape(2, 128, 256)
        m["cache_b_k"] = f(inputs["cache_b_k"][b]).reshape(512, 1024)
        m["cache_b_v"] = f(inputs["cache_b_v"][b]).reshape(512, 1024)
        m["state_conv"] = f(inputs["state_conv"][:, b])
        in_maps.append(m)
    return in_maps


def kernel(**inputs):
    S = int(np.asarray(inputs["x_prompt"]).shape[1])
    NT = S // TT
    if NT not in _CACHE:
        _CACHE[NT] = build(NT)[0]
    nc = _CACHE[NT]
    in_maps = _host_inputs(inputs, NT)
    res = run_bass_kernel_spmd(nc, in_maps, core_ids=list(range(8)))
    R = res.results
    st = lambda k: np.stack([np.asarray(r[k], dtype=np.float32) for r in R], axis=0)
    y_prompt = st("y_prompt")
    y_sample = st("y_sample")
    rows_a = min(128, S)
    rows_b = min(512, S)
    p_ak = np.ascontiguousarray(st("p_ak").transpose(1, 0, 2, 3)).reshape(2, 8, 128, 4, 64)[:, :, 128 - rows_a:]
    p_av = np.ascontiguousarray(st("p_av").transpose(1, 0, 2, 3)).reshape(2, 8, 128, 4, 64)[:, :, 128 - rows_a:]
    p_bk = st("p_bk").reshape(8, 512, 16, 64)[:, 512 - rows_b:]
    p_bv = st("p_bv").reshape(8, 512, 16, 64)[:, 512 - rows_b:]
    p_conv = np.ascontiguousarray(st("p_conv").transpose(1, 0, 2, 3))
    s_ak = np.ascontiguousarray(st("s_ak").transpose(1, 0, 2, 3)).reshape(2, 8, DEC, 4, 64)
    s_av = np.ascontiguousarray(st("s_av").transpose(1, 0, 2, 3)).reshape(2, 8, DEC, 4, 64)
    s_bk = st("s_bk").reshape(8, DEC, 16, 64)
    s_bv = st("s_bv").reshape(8, DEC, 16, 64)
    s_conv = np.ascontiguousarray(st("s_conv").transpose(1, 0, 2, 3))
    return (y_prompt, y_sample, p_ak, p_av, p_bk, p_bv, p_conv, s_ak, s_av, s_bk, s_bv, s_conv)
```

```python
import math
import contextlib
import numpy as np
import concourse.bass as bass
import concourse.mybir as mybir
from concourse.bass_utils import run_bass_kernel_spmd

F32 = mybir.dt.float32
BF16 = mybir.dt.bfloat16
AF = mybir.ActivationFunctionType
ALU = mybir.AluOpType

D = 1024
C8 = 8
TT = 512
DFF = 2816
FC = 22
NCH = 44
DEPTH = 4
DEC = 16
ALPHA = (2.0 * DEPTH) ** 0.25
LN_EPS = 1e-5
EPS_P = LN_EPS / (ALPHA * ALPHA)
SCALE = 64 ** -0.5
UP_ORDER = list(range(22, 44)) + list(range(0, 22))

V_BADA = 0
V_BKV = 192
V_LNG = 208
V_LNB = 272
V_CW = 336
V_CB = 864
V_C = 1040
V_ROWS = 1152


class Res:
    __slots__ = ("w", "r", "rd")

    def __init__(self):
        self.w = None
        self.r = {}
        self.rd = []


class Chan:
    def __init__(self, sem):
        self.sem = sem
        self.count = 0


class Op:
    __slots__ = ("eng", "fn", "waits", "sig", "pos", "cnt", "dma", "dmaval", "epoch", "sigs", "cnts", "waiter")


ENGS = ["pe", "act", "dve", "pool", "sp"]
import os as _osm
MAXSIG = int(_osm.environ.get("MAXSIG", "1"))


class Prog:
    def __init__(self):
        self.ops = {e: [] for e in ENGS}
        self.seen = {e: {} for e in ENGS}
        self.epoch = 0

    def emit(self, eng, fn, reads=(), writes=(), chan=None):
        op = Op()
        op.eng = eng
        op.fn = fn
        op.pos = len(self.ops[eng])
        op.sig = False
        op.sigs = set()
        op.cnts = {}
        op.waiter = {}
        op.dma = chan
        op.epoch = self.epoch
        op.cnt = 0
        if chan is not None:
            chan.count += 16
            op.dmaval = chan.count
        deps = []
        for r in reads:
            if r.w is not None:
                deps.append((r.w, True))
        for w in writes:
            if w.w is not None:
                deps.append((w.w, False))
            for rd in w.r.values():
                deps.append((rd, False))
            for rd in w.rd:
                deps.append((rd, False))
        waits = []
        seen = self.seen[eng]
        for d, raw in deps:
            if d is op:
                continue
            if d.dma is not None:
                key = ("dma", id(d.dma))
                if seen.get(key, -1) >= d.dmaval:
                    continue
                seen[key] = d.dmaval
                waits.append(d)
                continue
            if d.eng == eng:
                if eng == "pe":
                    continue
            hops = 0
            while d.dma is None and eng not in d.sigs and len(d.sigs) >= MAXSIG:
                plst = self.ops[d.eng]
                zf = None
                for k in range(d.pos + 1, min(len(plst), d.pos + 13)):
                    z = plst[k]
                    if z.dma is None and z.fn is not None and (eng in z.sigs or len(z.sigs) == 0):
                        zf = z
                        break
                if zf is not None and zf.eng != eng:
                    d = zf
                    break
                c0 = sorted(d.sigs)[0]
                w0 = d.waiter[c0]
                hops += 1
                if w0.eng == eng or w0.dma is not None:
                    d = w0
                    break
                lst = self.ops[c0]
                found = None
                for k in range(w0.pos, min(len(lst), w0.pos + 400)):
                    z = lst[k]
                    if z.dma is None and z.fn is not None and (eng in z.sigs or len(z.sigs) == 0):
                        found = z
                        break
                if found is not None:
                    d = found
                    break
                d = lst[min(len(lst), w0.pos + 400) - 1]
                if d.dma is not None or d.fn is None:
                    d = w0
            if d.dma is not None:
                key = ("dma", id(d.dma))
                if seen.get(key, -1) >= d.dmaval:
                    continue
                seen[key] = d.dmaval
                waits.append(d)
                continue
            if d.eng == eng and (eng == "pe" or d is op):
                continue
            if d.eng == eng and hops > 0:
                continue
            key = d.eng
            if seen.get(key, -1) >= d.pos:
                continue
            seen[key] = d.pos
            d.sig = True
            d.sigs.add(eng)
            if eng not in d.waiter:
                d.waiter[eng] = op
            waits.append(d)
        op.waits = waits
        for r in reads:
            if chan is not None:
                r.rd.append(op)
            else:
                r.r[eng] = op
        for w in writes:
            w.w = op
            w.r = {}
            w.rd = []
        self.ops[eng].append(op)
        return op


def _t5_onehot():
    try:
        import jax
        import jax.numpy as jnp
        cpu = jax.devices("cpu")[0]
        with jax.default_device(cpu):
            rel = jnp.asarray(np.concatenate([(-128 + 127 - np.arange(255)), (127 - np.arange(255))]).astype(np.int32))
            nb = 16
            max_exact = 8
            n = jnp.abs(rel)
            large = max_exact + (jnp.log(jnp.maximum(n, 1).astype(jnp.float32) / max_exact)
                                 / math.log(128 / max_exact) * (nb - max_exact)).astype(jnp.int32)
            large = jnp.minimum(large, nb - 1)
            bucket = jnp.where(rel > 0, nb, 0) + jnp.where(n < max_exact, n, large)
            bucket = np.asarray(bucket)
    except Exception:
        rel = np.concatenate([(-128 + 127 - np.arange(255)), (127 - np.arange(255))]).astype(np.int32)
        n = np.abs(rel)
        large = 8 + (np.log(np.maximum(n, 1).astype(np.float32) / np.float32(8))
                     / np.float32(math.log(16.0)) * np.float32(8)).astype(np.int32)
        large = np.minimum(large, 15)
        bucket = np.where(rel > 0, 16, 0) + np.where(n < 8, n, large)
    oh = np.zeros((32, 510), np.float32)
    oh[bucket, np.arange(510)] = 1.0
    return oh


def build(NT):
    S = NT * TT
    nc = bass.Bass("TRN2", target_bir_lowering=False)
    P = Prog()

    def din(name, shape, dt=F32):
        return nc.dram_tensor(name, list(shape), dt, kind="ExternalInput").ap()

    def dout(name, shape, dt=F32):
        return nc.dram_tensor(name, list(shape), dt, kind="ExternalOutput").ap()

    def dscr(name, shape, dt=BF16):
        return nc.dram_tensor(name, list(shape), dt, kind="Internal").ap()

    x_prompt = din("x_prompt", [S, D])
    x_sample = din("x_sample", [DEC, D])
    cache_a_k = din("cache_a_k", [2, 128, 256])
    cache_a_v = din("cache_a_v", [2, 128, 256])
    cache_b_k = din("cache_b_k", [512, 1024])
    cache_b_v = din("cache_b_v", [512, 1024])
    state_conv = din("state_conv", [4, 2, 2 * DFF])
    w_ada = din("w_ada", [4, D, 6 * D])
    w_qkv_a = din("w_qkv_a", [2, D, 1536])
    w_o_a = din("w_o_a", [2, D, D])
    sinks_a = din("sinks_a", [2, 16])
    t5_table = din("t5_table", [32, 16])
    w_ada_kv = din("w_ada_kv", [D, 2 * D])
    w_kv_b = din("w_kv_b", [D, 2 * D])
    w_q_b = din("w_q_b", [2, D, D])
    w_o_b = din("w_o_b", [2, D, D])
    relpos_b = din("relpos_b", [2, 16, 257])
    w_up = din("w_up", [4, D, 2 * DFF])
    w_down = din("w_down", [4, DFF, D])
    vecs = din("vecs", [V_ROWS, 128])
    ident_d = din("ident", [128, 128])
    anti_d = din("anti", [128, 128])
    onehot_d = din("t5oh", [32, 510])

    y_prompt = dout("y_prompt", [S, D])
    y_sample = dout("y_sample", [DEC, D])
    p_ak = dout("p_ak", [2, 128, 256])
    p_av = dout("p_av", [2, 128, 256])
    p_bk = dout("p_bk", [512, 1024])
    p_bv = dout("p_bv", [512, 1024])
    p_conv = dout("p_conv", [4, 2, 2 * DFF])
    s_ak = dout("s_ak", [2, DEC, 256])
    s_av = dout("s_av", [2, DEC, 256])
    s_bk = dout("s_bk", [DEC, 1024])
    s_bv = dout("s_bv", [DEC, 1024])
    s_conv = dout("s_conv", [4, 2, 2 * DFF])

    sc_up = [dscr(f"sc_up{l}", [11, 128, 8 * 512]) for l in range(4)]
    sc_dn = [dscr(f"sc_dn{l}", [8, 128, 22 * 128]) for l in range(4)]
    sc_qa = [dscr(f"sc_qa{l}", [4, 128, 8 * 512]) for l in range(2)]
    sc_oa = [dscr(f"sc_oa{l}", [2, 128, 8 * 512]) for l in range(2)]
    sc_qb = [dscr(f"sc_qb{l}", [2, 128, 8 * 512]) for l in range(2)]
    sc_ob = [dscr(f"sc_ob{l}", [2, 128, 8 * 512]) for l in range(2)]
    sc_kvb = dscr("sc_kvb", [4, 128, 8 * 512])
    sc_veca = dscr("sc_veca", [16, 510], F32)
    sc_vecb = [dscr(f"sc_vecb{j}", [16, 510], F32) for j in range(2)]
    sc_ebb = [dscr(f"sc_ebb{j}", [128, 4096]) for j in range(2)]

    es = contextlib.ExitStack()
    with es:
        def sb(name, shape, dt):
            return es.enter_context(nc.sbuf_tensor(name, list(shape), dt))

        def ps(name):
            return es.enter_context(nc.psum_tensor(name, [128, 512], F32))

        xT = sb("xT", [128, C8, TT], F32)
        hT = sb("hT", [128, C8, TT], BF16)
        actT = sb("actT", [128, FC, TT], BF16)
        KB = sb("KB", [128, 2, C8, TT], BF16)
        VB = sb("VB", [128, 2, 4, D], BF16)
        KA = [sb(f"KA{l}", [128, 4, 5 * 128], BF16) for l in range(2)]
        VA = [sb(f"VA{l}", [128, 5, 256], BF16) for l in range(2)]
        Ubuf = [[sb(f"U{a}{b}", [128, TT + 2], F32) for b in range(2)] for a in range(2)]
        Ybuf = [[sb(f"Y{a}{b}", [128, TT], F32) for b in range(2)] for a in range(2)]
        Ebuf = [sb(f"E{i}", [128, 512], BF16) for i in range(5)]
        E0buf = sb("E0m", [128, 512], BF16)
        Tbuf = sb("Tb", [128, 512], F32)
        R2buf = sb("R2", [128, 512], F32)
        ybf = [sb(f"ybf{i}", [128, TT], BF16) for i in range(2)]
        sqb = [sb(f"sq{i}", [128, TT], BF16) for i in range(2)]
        mean_sb = sb("mean", [128, TT], F32)
        var_sb = sb("var", [128, TT], F32)
        rstd_sb = sb("rstd", [128, TT], F32)
        tmpb = [sb(f"tmp{i}", [128, TT], F32) for i in range(2)]
        EBA = sb("EBA", [128, 2, 2, 8, 128], BF16)
        EBB = sb("EBB", [128, 2, 2, 8, 128], BF16)
        W8 = [sb(f"W8_{i}", [128, 8 * 512], BF16) for i in range(3)]
        W22 = [sb(f"W22_{i}", [128, 22 * 128], BF16) for i in range(2)]
        stage = [sb(f"stg{i}", [128, D], F32) for i in range(2)]
        ident = sb("identS", [128, 128], F32)
        anti = sb("antiS", [128, 128], F32)
        ones_bf = sb("ones_bf", [128, 64], BF16)
        onesf = sb("onesf", [128, 128], BF16)
        VF = sb("VF", [128, V_ROWS], F32)
        SCt = sb("SCt", [128, 16], F32)
        ADA = sb("ADA", [128, 2, 208], F32)
        ADA1 = sb("ADA1", [128, 2, 208], F32)
        ADAG = sb("ADAG", [128, 2, 208], F32)
        FT = sb("FT", [128, 2, 4, 2, 2, 8], F32)
        FKV = sb("FKV", [128, 2, 2, 8], F32)
        carry = sb("carry", [128, 4, NCH, 2], F32)
        esink = sb("esink", [128, 2, 8], F32)
        t5_sb = sb("t5_sb", [32, 16], F32)
        tokA = tmpb[0][0:2, 0:256]
        oh_sb = mean_sb[0:32, 0:510]
        vec_sb = var_sb[0:16, 0:510]
        rp_sb = rstd_sb[0:16, 0:257]

        PS = [ps(f"ps{i}") for i in range(8)]
        D0, D1, S0, S1, O0, O1, N0, N1 = range(8)

        sems = {}
        sem_list = []

        def new_sem(name):
            s = es.enter_context(nc.semaphore(name))
            sem_list.append(s)
            return s

        def new_chan(name):
            return Chan(new_sem(name))

        R_x = [Res() for _ in range(C8)]
        R_h = [Res() for _ in range(C8)]
        R_act = [Res() for _ in range(FC)]
        R_ps = [Res() for _ in range(8)]
        R_w8 = [Res() for _ in range(3)]
        R_w22 = [Res() for _ in range(2)]
        R_ka = [[Res() for _ in range(5)] for _ in range(2)]
        R_va = [[Res() for _ in range(5)] for _ in range(2)]
        R_kb = [[Res() for _ in range(4)] for _ in range(2)]
        R_vb = [[Res() for _ in range(4)] for _ in range(2)]
        R_u = [[Res() for _ in range(2)] for _ in range(2)]
        R_uh = [[Res() for _ in range(2)] for _ in range(2)]
        R_y = [[Res() for _ in range(2)] for _ in range(2)]
        R_e = [Res() for _ in range(5)]
        R_e0 = Res()
        R_t = Res()
        R_r2 = Res()
        R_ybf = [Res(), Res()]
        R_sq = [Res(), Res()]
        R_mean = Res()
        R_var = Res()
        R_rstd = Res()
        R_tmp = [Res(), Res()]
        R_eba = Res()
        R_ebb = Res()
        R_stage = [Res(), Res()]
        R_const = Res()
        R_vf = Res()
        R_sc = Res()
        R_ada = Res()
        R_tabs = Res()
        R_carry = [[Res() for _ in range(NCH)] for _ in range(4)]
        R_esink = Res()
        R_tok = Res()
        R_misc = Res()
        R_scr = {}
        R_out = []

        ch_w8 = [new_chan(f"cw8_{i}") for i in range(3)]
        ch_w22 = [new_chan(f"cw22_{i}") for i in range(2)]
        ch_stage = [new_chan(f"cst{i}") for i in range(2)]
        ch_misc = new_chan("cmisc")
        ch_cast = [new_chan(f"ccast{i}") for i in range(4)]
        R_castq = [Res() for _ in range(4)]
        R_miscq = Res()
        cast_key_idx = {}
        ch_ebb = new_chan("cebb")

        def mm(out, lhsT, rhs, start, stop, reads, writes, skip=False):
            if skip:
                return P.emit("pe", lambda e: e.matmul(out, lhsT, rhs, start=start, stop=stop, skip_group_check=True),
                              reads, writes)
            return P.emit("pe", lambda e: e.matmul(out, lhsT, rhs, start=start, stop=stop), reads, writes)

        def tr(out, in_, idn, reads, writes):
            return P.emit("pe", lambda e: e.transpose(out, in_, idn), reads, writes)

        def act(out, in_, func, reads, writes, bias=0.0, scale=1.0):
            return P.emit("act", lambda e: e.activation(out, in_, func, bias=bias, scale=scale), reads, writes)

        def tsc(eng, out, in0, s1, s2, op0, op1, reads, writes):
            if op1 is None:
                return P.emit(eng, lambda e: e.tensor_scalar(out, in0, s1, None, op0), reads, writes)
            return P.emit(eng, lambda e: e.tensor_scalar(out, in0, s1, s2, op0, op1), reads, writes)

        def stt(eng, out, in0, scalar, in1, op0, op1, reads, writes):
            return P.emit(eng, lambda e: e.scalar_tensor_tensor(out, in0, scalar, in1, op0, op1), reads, writes)

        def ttn(eng, out, in0, in1, op, reads, writes):
            return P.emit(eng, lambda e: e.tensor_tensor(out, in0, in1, op), reads, writes)

        def cpy(eng, out, in_, reads, writes):
            if eng == "act":
                return P.emit("act", lambda e: e.activation(out, in_, AF.Identity), reads, writes)
            return P.emit(eng, lambda e: e.tensor_copy(out, in_), reads, writes)

        def mset(eng, ap, val, writes):
            return P.emit(eng, lambda e: e.memset(ap, val), (), writes)

        import os as _os0
        NOOUT = _os0.environ.get("NOOUT", "0") == "1"
        out_names = ("y_prompt", "y_sample", "p_ak", "p_av", "p_bk", "p_bv", "p_conv", "s_ak", "s_av", "s_bk",
                     "s_bv", "s_conv")

        def dma(q, out, in_, reads, writes, chan, slow=False):
            if NOOUT and out.tensor.name in out_names:
                op_ = Op()
                op_.dma = None
                return op_
            if chan is ch_misc:
                writes = list(writes) + [R_miscq]
            if slow:
                return P.emit(q, lambda e: e.dma_start(out=out, in_=in_, allow_slow_non_contiguous=True),
                              reads, writes, chan)
            return P.emit(q, lambda e: e.dma_start(out=out, in_=in_), reads, writes, chan)

        def scr_res(key):
            if key not in R_scr:
                R_scr[key] = Res()
            return R_scr[key]

        def cast(dst, src, key):
            if key not in cast_key_idx:
                cast_key_idx[key] = len(cast_key_idx) % 4
            ci = cast_key_idx[key]
            dma("pool", dst, src, (), [scr_res(key), R_castq[ci]], ch_cast[ci])

        def cast_cols(scr, slot, w2d, col0, key):
            src = w2d[:, col0:col0 + 512].rearrange("(kc p) c -> p kc c", p=128)
            dst = scr[slot].rearrange("p (kc c) -> p kc c", kc=8)
            cast(dst, src, key)

        def emit_casts():
            for l in range(4):
                if l < 2:
                    for s in range(2):
                        cast_cols(sc_qa[l], s, w_qkv_a[l], s * 512, ("qa", l, s))
                    for dup in range(2):
                        for kc in range(8):
                            src = w_qkv_a[l][kc * 128:(kc + 1) * 128, 1024:1280].rearrange("p (g d) -> p g d", g=4)
                            dst = sc_qa[l][2].rearrange("p (kc g u d) -> p kc g u d", kc=8, g=4, u=2)[:, kc, :, dup, :]
                            cast(dst, src, ("qa", l, 2))
                    cast_cols(sc_qa[l], 3, w_qkv_a[l], 1024, ("qa", l, 3))
                    for s in range(2):
                        cast_cols(sc_oa[l], s, w_o_a[l], s * 512, ("oa", l, s))
                else:
                    j = l - 2
                    for s in range(2):
                        cast_cols(sc_qb[j], s, w_q_b[j], s * 512, ("qb", j, s))
                    for s in range(2):
                        cast_cols(sc_ob[j], s, w_o_b[j], s * 512, ("ob", j, s))
                for s in range(11):
                    chs = UP_ORDER[4 * s:4 * s + 4]
                    if chs[3] == chs[0] + 3:
                        cast_cols(sc_up[l], s, w_up[l], chs[0] * 128, ("up", l, s))
                    else:
                        for hh in range(2):
                            c0 = chs[2 * hh] * 128
                            src = w_up[l][:, c0:c0 + 256].rearrange("(kc p) c -> p kc c", p=128)
                            dst = sc_up[l][s].rearrange("p (kc c) -> p kc c", kc=8)[:, :, hh * 256:(hh + 1) * 256]
                            cast(dst, src, ("up", l, s))
                for m in range(8):
                    src = w_down[l][:, m * 128:(m + 1) * 128].rearrange("(kc p) c -> p kc c", p=128)
                    dst = sc_dn[l][m].rearrange("p (kc c) -> p kc c", kc=22)
                    cast(dst, src, ("dn", l, m))
                if l == 1:
                    for s in range(4):
                        cast_cols(sc_kvb, s, w_kv_b, s * 512, ("kvb", s))

        w8_n = [0]
        w22_n = [0]

        def load_w8(scr_ap, key):
            i = w8_n[0] % 3
            w8_n[0] += 1
            dma("sp", W8[i][:], scr_ap, [scr_res(key)], [R_w8[i]], ch_w8[i])
            return i

        def load_w22(scr_ap, key):
            i = w22_n[0] % 2
            w22_n[0] += 1
            dma("sp", W22[i][:], scr_ap, [scr_res(key)], [R_w22[i]], ch_w22[i])
            return i

        def w8v(i):
            return W8[i][:].rearrange("p (kc c) -> p kc c", kc=8)

        def setup():
            dma("sp", ident[:], ident_d[:, :], (), [R_const], ch_misc)
            dma("sp", anti[:], anti_d[:, :], (), [R_const], ch_misc)
            dma("sp", oh_sb, onehot_d[:, :], (), [R_mean], ch_misc)
            dma("sp", t5_sb[:], t5_table[:, :], (), [R_const], ch_misc)
            mset("dve", ones_bf[:], 1.0, [R_const])
            mset("dve", onesf[:], 1.0 / 1024.0, [R_const])
            mset("pool", carry[:], 0.0, [R_carry[l][c] for l in range(4) for c in range(NCH)])
            mset("pool", E0buf[:], 0.0, [R_e0])
            mset("pool", hT[:], 0.0, R_h)
            mset("pool", KB[:], 0.0, R_kb[0] + R_kb[1])
            for la_ in range(2):
                mset("pool", KA[la_][:], 0.0, R_ka[la_])
            for r in range(V_ROWS // 128):
                st = stage[r % 2]
                dma("sp", st[:, 0:128], vecs[r * 128:(r + 1) * 128, :], (), [R_stage[r % 2]], ch_stage[r % 2])
                b = D0 + (r % 2)
                tr(PS[b][:, 0:128], st[:, 0:128], ident[:], [R_stage[r % 2], R_const], [R_ps[b]])
                cpy("dve", VF[:, r * 128:(r + 1) * 128], PS[b][:, 0:128], [R_ps[b]], [R_vf])
            act(SCt[:], VF[:, V_C:V_C + 16], AF.Silu, [R_vf], [R_sc])
            blocks = [(w_ada[l], cb, l * 48 + cb * 2) for l in range(4) for cb in range(24)]
            blocks += [(w_ada_kv, cb, 192 + cb * 2) for cb in range(8)]
            for bi, (w2d, cb, col) in enumerate(blocks):
                i = w8_n[0] % 3
                w8_n[0] += 1
                wv = W8[i][:].bitcast(F32).rearrange("p (kc c) -> p kc c", kc=8)
                src = w2d[:, cb * 256:(cb + 1) * 256].rearrange("(kc p) c -> p kc c", p=128)
                dma("sp", wv, src, (), [R_w8[i]], ch_w8[i])
                b = D0 + (bi % 2)
                for kc in range(8):
                    mm(PS[b][0:2, 0:256], SCt[:, kc:16:8], wv[:, kc, :], kc == 0, kc == 7,
                       [R_sc, R_w8[i]], [R_ps[b]])
                cpy("act", tokA, PS[b][0:2, 0:256], [R_ps[b]], [R_tmp[0]])
                b2 = O0 + (bi % 2)
                for hh in range(2):
                    tr(PS[b2][:, hh * 2:hh * 2 + 2], tokA[0:2, hh * 128:(hh + 1) * 128], ident[0:2, 0:2],
                       [R_tmp[0], R_const], [R_ps[b2]])
                cpy("dve", ADA[:, :, col:col + 2].rearrange("p g c -> p c g"),
                    PS[b2][:, 0:4].rearrange("p (c g) -> p c g", c=2), [R_ps[b2]], [R_ada])
            for g in range(2):
                ttn("dve", ADA[:, g, :], ADA[:, g, :], VF[:, 0:208], ALU.add, [R_ada, R_vf], [R_ada])
            tsc("dve", ADA1[:], ADA[:], 1.0, None, ALU.add, None, [R_ada], [R_tabs])
            tsc("dve", ADAG[:], ADA[:], 1.0, 1.0 / ALPHA, ALU.add, ALU.mult, [R_ada], [R_tabs])
            for g in range(2):
                for l in range(4):
                    g1 = VF[:, V_LNG + (l * 2) * 8: V_LNG + (l * 2) * 8 + 8]
                    b1 = VF[:, V_LNB + (l * 2) * 8: V_LNB + (l * 2) * 8 + 8]
                    a1 = ADA1[:, g, l * 48 + 32: l * 48 + 40]
                    sh = ADA[:, g, l * 48 + 24: l * 48 + 32]
                    ttn("dve", FT[:, g, l, 0, 0, :], g1, a1, ALU.mult, [R_vf, R_tabs], [R_tabs])
                    ttn("dve", FT[:, g, l, 0, 1, :], b1, a1, ALU.mult, [R_vf, R_tabs], [R_tabs])
                    ttn("dve", FT[:, g, l, 0, 1, :], FT[:, g, l, 0, 1, :], sh, ALU.add, [R_tabs, R_ada], [R_tabs])
                    if l < 3:
                        g2 = VF[:, V_LNG + (l * 2 + 1) * 8: V_LNG + (l * 2 + 1) * 8 + 8]
                        b2 = VF[:, V_LNB + (l * 2 + 1) * 8: V_LNB + (l * 2 + 1) * 8 + 8]
                        a1 = ADA1[:, g, (l + 1) * 48 + 8: (l + 1) * 48 + 16]
                        sh = ADA[:, g, (l + 1) * 48: (l + 1) * 48 + 8]
                        ttn("dve", FT[:, g, l, 1, 0, :], g2, a1, ALU.mult, [R_vf, R_tabs], [R_tabs])
                        ttn("dve", FT[:, g, l, 1, 1, :], b2, a1, ALU.mult, [R_vf, R_tabs], [R_tabs])
                        ttn("dve", FT[:, g, l, 1, 1, :], FT[:, g, l, 1, 1, :], sh, ALU.add, [R_tabs, R_ada], [R_tabs])
                g2 = VF[:, V_LNG + 3 * 8: V_LNG + 3 * 8 + 8]
                b2 = VF[:, V_LNB + 3 * 8: V_LNB + 3 * 8 + 8]
                a1 = ADA1[:, g, 200:208]
                sh = ADA[:, g, 192:200]
                ttn("dve", FKV[:, g, 0, :], g2, a1, ALU.mult, [R_vf, R_tabs], [R_tabs])
                ttn("dve", FKV[:, g, 1, :], b2, a1, ALU.mult, [R_vf, R_tabs], [R_tabs])
                ttn("dve", FKV[:, g, 1, :], FKV[:, g, 1, :], sh, ALU.add, [R_tabs, R_ada], [R_tabs])
            for l in range(2):
                for par in range(2):
                    src = bass.AP(sinks_a.tensor, l * 16 + par, [[0, 64], [2, 8]])
                    dma("sp", esink[par * 64:(par + 1) * 64, l, :], src, (), [R_esink], ch_misc, slow=True)
            act(esink[:], esink[:], AF.Exp, [R_esink], [R_esink])
            mm(PS[D0][0:16, 0:510], t5_sb[:], oh_sb, True, True, [R_const, R_mean], [R_ps[D0]])
            cpy("dve", vec_sb, PS[D0][0:16, 0:510], [R_ps[D0]], [R_var])
            dma("sp", sc_veca[:, :], vec_sb, [R_var], [scr_res("veca")], ch_misc)
            make_eb(sc_veca, "veca", EBA, R_eba)
            mset("pool", EBA[0:64, 0, :, :, 64:128], 0.0, [R_eba])
            mset("pool", EBA[64:128, 1, :, :, 0:64], 0.0, [R_eba])
            for j in range(2):
                dma("sp", rp_sb, relpos_b[j], (), [R_rstd], ch_misc)
                mset("dve", vec_sb, 0.0, [R_var])
                tsc("dve", vec_sb[:, 0:128], rp_sb[:, 129:257], rp_sb[:, 256:257], None, ALU.subtract, None,
                    [R_rstd, R_var], [R_var])
                tsc("dve", vec_sb[:, 255:510], rp_sb[:, 1:256], rp_sb[:, 256:257], None, ALU.subtract, None,
                    [R_rstd, R_var], [R_var])
                dma("sp", sc_vecb[j][:, :], vec_sb, [R_var], [scr_res(("vecb", j))], ch_misc)
                make_eb(sc_vecb[j], ("vecb", j), EBB, R_ebb)
                mset("pool", EBB[64:128, 1, :, :, 0:64], 0.0, [R_ebb])
                dma("sp", sc_ebb[j][:, :], EBB[:].rearrange("p a b c d -> p (a b c d)"), [R_ebb],
                    [scr_res(("ebb", j))], ch_misc)

        def make_eb(vec_dram, key, EB, R_eb):
            for ty in range(2):
                i = w8_n[0] % 3
                w8_n[0] += 1
                brev = W8[i][:].bitcast(F32).rearrange("p (h q) -> p h q", h=16)
                src = bass.AP(vec_dram.tensor, ty * 255, [[1, 128], [510, 16], [1, 128]])
                dma("sp", brev, src, [scr_res(key)], [R_w8[i]], ch_w8[i])
                for par in range(2):
                    for jg in range(2):
                        b = D0 + ((par * 2 + jg) % 2)
                        rhs = brev[:, par + 8 * jg: par + 8 * jg + 7: 2, :]
                        mm(PS[b][:, :], anti[:], rhs, True, True, [R_const, R_w8[i]], [R_ps[b]])
                        act(EB[:, ty, par, jg * 4:(jg + 1) * 4, :],
                            PS[b][:, :].rearrange("p (j q) -> p j q", j=4), AF.Exp, [R_ps[b]], [R_eb])

        def tab(g, l, part, c):
            col = l * 48 + part * 8 + c
            return col

        def load_x_group(src_rows, nrows, si):
            dma("sp", stage[si][0:nrows, :], src_rows, (), [R_stage[si]], ch_stage[si])

        def x_in(xsrc, T, g, first_groups_loaded=0):
            ngr = (T + 127) // 128
            for gi in range(ngr):
                nr = min(128, T - gi * 128)
                si = gi % 2
                if gi >= first_groups_loaded:
                    load_x_group(xsrc[gi * 128: gi * 128 + nr, :], nr, si)
                for half in range(2):
                    b = D0 + half
                    for cc in range(4):
                        c = half * 4 + cc
                        tr(PS[b][:, cc * 128: cc * 128 + nr], stage[si][0:nr, c * 128:(c + 1) * 128],
                           ident[0:nr, 0:nr], [R_stage[si], R_const], [R_ps[b]])
                    out = xT[:, half * 4:(half + 1) * 4, gi * 128: gi * 128 + nr]
                    inp = PS[b][:, :].rearrange("p (c t) -> p c t", c=4)[:, :, 0:nr]
                    if half == 0:
                        cpy("act", out, inp, [R_ps[b]], R_x[0:4])
                    else:
                        cpy("dve", out, inp, [R_ps[b]], R_x[4:8])

        def h_from_x(g, l, T):
            for c in range(C8):
                act(hT[:, c, 0:T], xT[:, c, 0:T], AF.Identity, [R_x[c], R_tabs], [R_h[c]],
                    bias=ADA[:, g, tab(g, l, 0, c): tab(g, l, 0, c) + 1],
                    scale=ADA1[:, g, tab(g, l, 1, c): tab(g, l, 1, c) + 1])

        dcount = [0]

        DRING = [D0, D1, S0, S1, O0, O1]

        def next_d():
            b = DRING[dcount[0] % len(DRING)]
            dcount[0] += 1
            return b

        def proj_fm(slot_keys, scr_list, KCn, src_fn, src_res, n_out, T, evac):
            cur = None
            for m in range(n_out):
                if m % 4 == 0:
                    cur = load_w8(scr_list[m // 4], slot_keys[m // 4])
                wv = w8v(cur)
                b = next_d()
                for kc in range(KCn):
                    mm(PS[b][:, 0:T], wv[:, kc, (m % 4) * 128:(m % 4 + 1) * 128], src_fn(kc), kc == 0, kc == KCn - 1,
                       [R_w8[cur], src_res[kc]], [R_ps[b]])
                evac(m, b)

        def layer_norm(g, l, sub, T, ft1, ft2, h_out):
            for c in range(C8):
                i = c % 2
                act(ybf[i][:, 0:T], xT[:, c, 0:T], AF.Copy, [R_x[c]], [R_ybf[i]])
                act(sqb[i][:, 0:T], xT[:, c, 0:T], AF.Square, [R_x[c]], [R_sq[i]])
                mm(PS[D0][:, 0:T], onesf[:], ybf[i][:, 0:T], c == 0, c == 7, [R_const, R_ybf[i]], [R_ps[D0]])
                mm(PS[D1][:, 0:T], onesf[:], sqb[i][:, 0:T], c == 0, c == 7, [R_const, R_sq[i]], [R_ps[D1]])
            cpy("act", mean_sb[:, 0:T], PS[D0][:, 0:T], [R_ps[D0]], [R_mean])
            ttn("dve", var_sb[:, 0:T], mean_sb[:, 0:T], mean_sb[:, 0:T], ALU.mult, [R_mean], [R_var])
            ttn("dve", var_sb[:, 0:T], PS[D1][:, 0:T], var_sb[:, 0:T], ALU.subtract, [R_ps[D1], R_var], [R_var])
            tsc("dve", var_sb[:, 0:T], var_sb[:, 0:T], EPS_P, None, ALU.add, None, [R_var], [R_var])
            act(var_sb[:, 0:T], var_sb[:, 0:T], AF.Sqrt, [R_var], [R_var])
            P.emit("dve", lambda e: e.reciprocal(rstd_sb[:, 0:T], var_sb[:, 0:T]), [R_var], [R_rstd])
            lng = V_LNG + (l * 2 + sub) * 8
            lnb = V_LNB + (l * 2 + sub) * 8
            for c in range(C8):
                i = c % 2
                eng = "pool" if c % 2 == 0 else "dve"
                ttn(eng, tmpb[i][:, 0:T], xT[:, c, 0:T], mean_sb[:, 0:T], ALU.subtract, [R_x[c], R_mean], [R_tmp[i]])
                ttn(eng, tmpb[i][:, 0:T], tmpb[i][:, 0:T], rstd_sb[:, 0:T], ALU.mult, [R_tmp[i], R_rstd], [R_tmp[i]])
                act(xT[:, c, 0:T], tmpb[i][:, 0:T], AF.Identity, [R_tmp[i], R_vf], [R_x[c]],
                    bias=VF[:, lnb + c: lnb + c + 1], scale=VF[:, lng + c: lng + c + 1])
                if h_out:
                    act(hT[:, c, 0:T], tmpb[i][:, 0:T], AF.Identity, [R_tmp[i], R_tabs], [R_h[c]],
                        bias=ft2[:, c:c + 1], scale=ft1[:, c:c + 1])

        def attention(l, g, T, pairs, tiles_of_pair, is_A):
            QT = actT
            OT = hT
            groups = []
            for pi, (q0, nq) in enumerate(pairs):
                kts = tiles_of_pair(pi)
                for jg in range(2):
                    for par in range(2):
                        for ti, kt in enumerate(kts):
                            groups.append((pi, q0, nq, jg, par, ti, len(kts), kt))
            nG = len(groups)
            state = {}
            pjcount = [0]

            def rec_S(gi):
                pi, q0, nq, jg, par, ti, nkt, kt = groups[gi]
                b = (S0, S1, D0, D1)[gi % 4]
                nk = kt["nk"]
                for jj in range(4):
                    j = jg * 4 + jj
                    mm(PS[b][0:max(nk, 32), jj * nq:(jj + 1) * nq], kt["kT"](par, j),
                       QT[par * 64:(par + 1) * 64, j, q0:q0 + nq],
                       True, True, kt["kres"] + [R_act[j]], [R_ps[b]])
                ty = kt["type"]
                if ty == "m0":
                    ebuf, eres = E0buf, R_e0
                else:
                    ebuf, eres = Ebuf[gi % 5], R_e[gi % 5]
                ev = ebuf[:, 0:4 * nq].rearrange("p (j q) -> p j q", j=4)
                sv = PS[b][:, 0:4 * nq].rearrange("p (j q) -> p j q", j=4)
                if ty == "m0":
                    n1 = min(nq, 64)
                    act(ev[0:nk, :, 0:n1], sv[0:nk, :, 0:n1], AF.Exp, [R_ps[b]], [eres], scale=SCALE)
                    if nq > 64:
                        act(ev[64:128, :, 64:nq], sv[64:128, :, 64:nq], AF.Exp, [R_ps[b]], [eres], scale=SCALE)
                else:
                    act(ebuf[0:nk, 0:4 * nq], PS[b][0:nk, 0:4 * nq], AF.Exp, [R_ps[b]], [eres], scale=SCALE)
                if ty in ("e0", "e1"):
                    t_i = 0 if ty == "e0" else 1
                    EB, R_eb = (EBA, R_eba) if is_A else (EBB, R_ebb)
                    eng = "dve" if gi % 4 != 3 else "pool"
                    ttn(eng, ev[0:nk, :, :], ev[0:nk, :, :], EB[0:nk, t_i, par, jg * 4:(jg + 1) * 4, 0:nq], ALU.mult,
                        [eres, R_eb], [eres])
                state[gi] = (ebuf, eres)

            def rec_DP(gi):
                pi, q0, nq, jg, par, ti, nkt, kt = groups[gi]
                ebuf, eres = state.pop(gi)
                nk = kt["nk"]
                pj = pjcount[0]
                ob = O0 + (pj % 2)
                nb = N0 + (pj % 2)
                first = ti == 0
                last = ti == nkt - 1
                mm(PS[nb][par * 64:(par + 1) * 64, 0:4 * nq], ones_bf[0:nk, :], ebuf[0:nk, 0:4 * nq], first, last,
                   [R_const, eres], [R_ps[nb]])
                for jj in range(4):
                    j = jg * 4 + jj
                    mm(PS[ob][par * 64:(par + 1) * 64, jj * nq:(jj + 1) * nq], kt["v"](par, j),
                       ebuf[0:nk, jj * nq:(jj + 1) * nq], first and jj == 0, last and jj == 3,
                       kt["vres"] + [eres], [R_ps[ob]], skip=True)
                if last and par == 1:
                    n4 = 4 * nq
                    if is_A:
                        ttn("dve", Tbuf[:, 0:n4].rearrange("p (j q) -> p j q", j=4),
                            PS[nb][:, 0:n4].rearrange("p (j q) -> p j q", j=4),
                            esink[:, l, jg * 4:(jg + 1) * 4].unsqueeze(2).to_broadcast([128, 4, nq]), ALU.add,
                            [R_ps[nb], R_esink], [R_t])
                        P.emit("dve", lambda e: e.reciprocal(R2buf[:, 0:n4], Tbuf[:, 0:n4]), [R_t], [R_r2])
                    else:
                        P.emit("dve", lambda e: e.reciprocal(R2buf[:, 0:n4], PS[nb][:, 0:n4]), [R_ps[nb]], [R_r2])
                    ttn("dve", OT[:, jg * 4:(jg + 1) * 4, q0:q0 + nq],
                        PS[ob][:, 0:n4].rearrange("p (j q) -> p j q", j=4),
                        R2buf[:, 0:n4].rearrange("p (j q) -> p j q", j=4), ALU.mult,
                        [R_ps[ob], R_r2], R_h[jg * 4:(jg + 1) * 4])
                    pjcount[0] += 1

            for gi in range(nG + 3):
                if gi < nG:
                    rec_S(gi)
                if gi >= 3:
                    rec_DP(gi - 3)

        def ffn(l, g, T, last_out):
            cur = None
            pending_gelu = []
            for oi, ch in enumerate(UP_ORDER):
                if oi % 4 == 0:
                    cur = load_w8(sc_up[l][oi // 4], ("up", l, oi // 4))
                wv = w8v(cur)
                b = next_d()
                for kc in range(8):
                    mm(PS[b][:, 0:T], wv[:, kc, (oi % 4) * 128:(oi % 4 + 1) * 128], hT[:, kc, 0:T], kc == 0, kc == 7,
                       [R_w8[cur], R_h[kc]], [R_ps[b]])
                isg = ch >= FC
                i = ch - FC if isg else ch
                bq = oi % 4
                U = Ubuf[bq // 2][bq % 2]
                Y = Ybuf[bq // 2][bq % 2]
                RU = R_u[bq // 2][bq % 2]
                RUH = R_uh[bq // 2][bq % 2]
                RY = R_y[bq // 2][bq % 2]
                w0 = VF[:, V_CW + (l * 3 + 0) * NCH + ch: V_CW + (l * 3 + 0) * NCH + ch + 1]
                w1 = VF[:, V_CW + (l * 3 + 1) * NCH + ch: V_CW + (l * 3 + 1) * NCH + ch + 1]
                w2 = VF[:, V_CW + (l * 3 + 2) * NCH + ch: V_CW + (l * 3 + 2) * NCH + ch + 1]
                cb = VF[:, V_CB + l * NCH + ch: V_CB + l * NCH + ch + 1]
                cpy("act", U[:, 0:2], carry[:, l, ch, :], [R_carry[l][ch]], [RUH])
                cpy("act", U[:, 2:2 + T], PS[b][:, 0:T], [R_ps[b]], [RU])
                cpy("act", carry[:, l, ch, :], PS[b][:, T - 2:T], [R_ps[b]], [R_carry[l][ch]])
                if isg:
                    tsc("dve", Y[:, 0:T], PS[b][:, 0:T], w2, cb, ALU.mult, ALU.add, [R_ps[b], R_vf], [RY])
                    stt("dve", Y[:, 0:T], U[:, 1:1 + T], w1, Y[:, 0:T], ALU.mult, ALU.add, [RU, RUH, RY, R_vf], [RY])
                    stt("dve", Y[:, 0:T], U[:, 0:T], w0, Y[:, 0:T], ALU.mult, ALU.add, [RU, RUH, RY, R_vf], [RY])
                    if pending_gelu:
                        pending_gelu.pop()()
                    pending_gelu.append(lambda Y=Y, RY=RY, i=i: act(actT[:, i, 0:T], Y[:, 0:T], AF.Gelu_apprx_tanh,
                                                                    [RY], [R_act[i]]))
                else:
                    if pending_gelu:
                        pending_gelu.pop()()
                    act(Y[:, 0:T], PS[b][:, 0:T], AF.Identity, [R_ps[b], R_vf], [RY], bias=cb, scale=w2)
                    stt("dve", Y[:, 0:T], U[:, 1:1 + T], w1, Y[:, 0:T], ALU.mult, ALU.add, [RU, RUH, RY, R_vf], [RY])
                    stt("dve", Y[:, 0:T], U[:, 0:T], w0, Y[:, 0:T], ALU.mult, ALU.add, [RU, RUH, RY, R_vf], [RY])
                    ttn("pool", actT[:, i, 0:T], Y[:, 0:T], actT[:, i, 0:T], ALU.mult, [RY, R_act[i]], [R_act[i]])
            if last_out is not None:
                for r in range(2):
                    dst = bass.AP(last_out.tensor, l * 2 * 2 * DFF + r * 2 * DFF, [[1, 128], [128, NCH]])
                    o = dma("sp", dst, carry[:, l, :, r], [R_carry[l][c] for c in range(NCH)], [Res()], ch_misc,
                            slow=True)
                    R_out.append(o)
            for m in range(8):
                wi = load_w22(sc_dn[l][m], ("dn", l, m))
                wv = W22[wi][:].rearrange("p (kc c) -> p kc c", kc=22)
                b = next_d()
                for kc in range(FC):
                    mm(PS[b][:, 0:T], wv[:, kc, :], actT[:, kc, 0:T], kc == 0, kc == FC - 1,
                       [R_w22[wi], R_act[kc]], [R_ps[b]])
                gate = ADAG[:, g, tab(g, l, 5, m): tab(g, l, 5, m) + 1]
                stt("dve", xT[:, m, 0:T], PS[b][:, 0:T], gate, xT[:, m, 0:T], ALU.mult, ALU.add,
                    [R_ps[b], R_x[m], R_tabs], [R_x[m]])

        def tok_major(src_chunks_res, T, w_slot, col0, ncols, dst_fn, f32_out_fn):
            wv = w8v(w_slot)
            ngr = (T + 127) // 128
            for gi in range(ngr):
                nr = min(128, T - gi * 128)
                mr = max(nr, 32)
                b = next_d()
                for kc in range(8):
                    mm(PS[b][0:mr, 0:ncols], hT[:, kc, gi * 128: gi * 128 + mr], wv[:, kc, col0:col0 + ncols],
                       kc == 0, kc == 7, [R_h[kc], R_w8[w_slot]], [R_ps[b]])
                if dst_fn is not None:
                    dst_fn(gi, nr, b)
                if f32_out_fn is not None:
                    f32_out_fn(gi, nr, b)

        def run_tile(g, T, t, xsrc, ydst, is_last, outs):
            sample = g == 1
            npairs = (T + 127) // 128
            pairs = [(p * 128, min(128, T - p * 128)) for p in range(npairs)]
            cur_half = 1 if sample else t % 2
            prev_half = 1 - cur_half
            import os as _os2
            KSUB = int(_os2.environ.get("KSUB", "99"))
            x_in(xsrc, T, g)
            h_from_x(g, 0, T)
            if KSUB <= 1:
                return
            for l in range(4):
                is_A = l < 2
                if is_A:
                    la = l
                    def evq(m, b):
                        eng = "act" if m % 2 == 0 else "dve"
                        cpy(eng, actT[:, m, 0:T], PS[b][:, 0:T], [R_ps[b]], [R_act[m]])
                    proj_fm([("qa", la, 0), ("qa", la, 1)], [sc_qa[la][0], sc_qa[la][1]], 8,
                            lambda kc: hT[:, kc, 0:T], R_h, 8, T, evq)
                    KS2 = int(_os2.environ.get("KSUB2", "99"))
                    if KS2 <= 1:
                        return
                    kslot = load_w8(sc_qa[la][2], ("qa", la, 2))
                    wv = w8v(kslot)
                    for g4 in range(4):
                        b = next_d()
                        for kc in range(8):
                            mm(PS[b][:, 0:T], wv[:, kc, g4 * 128:(g4 + 1) * 128], hT[:, kc, 0:T], kc == 0, kc == 7,
                               [R_w8[kslot], R_h[kc]], [R_ps[b]])
                        eng = "act" if g4 % 2 == 0 else "dve"
                        cpy(eng, KA[la][:, g4, 128:128 + T], PS[b][:, 0:T], [R_ps[b]], R_ka[la][1:5])
                    if KS2 <= 2:
                        return
                    kvslot = load_w8(sc_qa[la][3], ("qa", la, 3))
                    KS3 = int(_os2.environ.get("KS3", "99"))
                    if KS3 <= 1:
                        return
                    if KS3 <= 2:
                        tok_major(R_h, T, kvslot, 256, 256, None, None)
                        return

                    def vdst(gi, nr, b):
                        if _os2.environ.get("NOVD3") and gi == 3 and not sample:
                            return
                        cpy("act" if nr == 128 else "dve", VA[la][0:nr, 1 + gi, :], PS[b][0:nr, 0:256], [R_ps[b]],
                            [R_va[la][1 + gi]])
                    want_out = (is_last or sample) and _os2.environ.get("NOWANT", "0") != "1"
                    if want_out:
                        okd, ovd = outs["ak"], outs["av"]

                        def vout(gi, nr, b):
                            if sample or gi == 3:
                                si = 0
                                if _os2.environ.get("ALT"):
                                    if _os2.environ.get("ALT") == "2":
                                        cpy("dve", R2buf[0:nr, 0:256], Tbuf[0:nr, 0:256], [R_ps[b]], [R_r2])
                                    elif _os2.environ.get("ALT") == "3":
                                        cpy("dve", R2buf[0:nr, 0:256], PS[b][0:nr, 0:256], [R_ps[b]], [R_r2])
                                        cpy("dve", Tbuf[0:nr, 0:256], PS[b][0:nr, 0:256], [R_ps[b]], [R_t])
                                    else:
                                        tsc("dve", R2buf[0:nr, 0:256], PS[b][0:nr, 0:256], 1.0, None, ALU.mult, None, [R_ps[b]], [R_r2])
                                else:
                                    cpy(_os2.environ.get("VENG2", "dve"), stage[si][0:nr, 0:256], PS[b][0:nr, 0:256], [R_ps[b]], [R_stage[si]])
                                o = dma("sp", ovd[la][0:nr, :], stage[si][0:nr, 0:256], [R_stage[si]], [Res()],
                                        ch_stage[si])
                                R_out.append(o)

                        def kout(gi, nr, b):
                            if sample or gi == 3:
                                si = 1
                                cpy(_os2.environ.get("VENG2", "dve"), stage[si][0:nr, 0:256], PS[b][0:nr, 0:256], [R_ps[b]], [R_stage[si]])
                                o = dma("sp", okd[la][0:nr, :], stage[si][0:nr, 0:256], [R_stage[si]], [Res()],
                                        ch_stage[si])
                                R_out.append(o)
                        tok_major(R_h, T, kvslot, 256, 256, vdst, None if _os2.environ.get("NOV") else vout)
                        if not _os2.environ.get("NOK"):
                            tok_major(R_h, T, kvslot, 0, 256, None, kout)
                    else:
                        tok_major(R_h, T, kvslot, 256, 256, vdst, None)

                    if KSUB <= 2:
                        return
                    def tiles_of_pair(pi, la=la):
                        res = []
                        for ty, ti in (("e0", pi), ("e1", pi + 1)):
                            if ti == 0 and (not sample) and t == 0:
                                continue
                            nk = 128
                            if ti >= 1:
                                nk = min(128, T - (ti - 1) * 128)
                            res.append({
                                "nk": nk, "type": ty,
                                "kT": (lambda par, j, ti=ti, nk=nk: KA[la][par * 64:(par + 1) * 64, j // 2,
                                                                            ti * 128: ti * 128 + max(nk, 32)]),
                                "kres": [R_ka[la][ti]],
                                "v": (lambda par, j, ti=ti, nk=nk: VA[la][0:nk, ti, (j // 2) * 64:(j // 2) * 64 + 64]),
                                "vres": [R_va[la][ti]],
                            })
                        return res
                    attention(la, g, T, pairs, tiles_of_pair, True)
                    if KSUB <= 3:
                        return
                    if not sample:
                        cpy("pool", KA[la][:, :, 0:128], KA[la][:, :, 512:640], [R_ka[la][4]], [R_ka[la][0]])
                        cpy("pool", VA[la][:, 0, :], VA[la][:, 4, :], [R_va[la][4]], [R_va[la][0]])
                    wo_keys = [("oa", la, 0), ("oa", la, 1)]
                    wo_scr = [sc_oa[la][0], sc_oa[la][1]]
                else:
                    jb = l - 2
                    dma("sp", EBB[:].rearrange("p a b c d -> p (a b c d)"), sc_ebb[jb][:, :],
                        [scr_res(("ebb", jb))], [R_ebb], ch_ebb)

                    def evq(m, b):
                        eng = "act" if m % 2 == 0 else "dve"
                        cpy(eng, actT[:, m, 0:T], PS[b][:, 0:T], [R_ps[b]], [R_act[m]])
                    proj_fm([("qb", jb, 0), ("qb", jb, 1)], [sc_qb[jb][0], sc_qb[jb][1]], 8,
                            lambda kc: hT[:, kc, 0:T], R_h, 8, T, evq)

                    def tiles_of_pair(pi):
                        res = []
                        for k5 in range(5):
                            ti = pi + k5
                            if ti < 4 and (not sample) and t == 0:
                                continue
                            half = prev_half if ti < 4 else cur_half
                            tl = ti % 4
                            nk = 128 if ti < 4 else min(128, T - tl * 128)
                            ty = ("m0", "n", "n", "e0", "e1")[k5]
                            res.append({
                                "nk": nk, "type": ty,
                                "kT": (lambda par, j, half=half, tl=tl, nk=nk:
                                       KB[par * 64:(par + 1) * 64, half, j, tl * 128: tl * 128 + max(nk, 32)]),
                                "kres": [R_kb[half][tl]],
                                "v": (lambda par, j, half=half, tl=tl, nk=nk:
                                      VB[0:nk, half, tl, (2 * j + par) * 64:(2 * j + par) * 64 + 64]),
                                "vres": [R_vb[half][tl]],
                            })
                        return res
                    attention(l, g, T, pairs, tiles_of_pair, False)
                    wo_keys = [("ob", jb, 0), ("ob", jb, 1)]
                    wo_scr = [sc_ob[jb][0], sc_ob[jb][1]]

                def evo(m, b, l=l):
                    gate = ADAG[:, g, tab(g, l, 2, m): tab(g, l, 2, m) + 1]
                    stt("dve", xT[:, m, 0:T], PS[b][:, 0:T], gate, xT[:, m, 0:T], ALU.mult, ALU.add,
                        [R_ps[b], R_x[m], R_tabs], [R_x[m]])
                proj_fm(wo_keys, wo_scr, 8, lambda kc: hT[:, kc, 0:T], R_h, 8, T, evo)
                layer_norm(g, l, 0, T, FT[:, g, l, 0, 0, :], FT[:, g, l, 0, 1, :], True)
                if KSUB <= 4:
                    return
                conv_out = outs["conv"] if (is_last or sample) else None
                ffn(l, g, T, conv_out)
                if KSUB <= 5:
                    return
                if l == 1:
                    layer_norm(g, l, 1, T, FKV[:, g, 0, :], FKV[:, g, 1, :], True)
                    def evk(m, b):
                        eng = "act" if m % 2 == 0 else "dve"
                        if T >= 128:
                            cpy(eng, KB[:, cur_half, m, 0:T], PS[b][:, 0:T], [R_ps[b]], R_kb[cur_half])
                        else:
                            cpy(eng, KB[:, cur_half, m, 0:T], PS[b][:, 0:T], [R_ps[b]], [R_kb[cur_half][0]])
                    proj_fm([("kvb", 0), ("kvb", 1)], [sc_kvb[0], sc_kvb[1]], 8,
                            lambda kc: hT[:, kc, 0:T], R_h, 8, T, evk)
                    want_out = is_last or sample
                    for hv in range(2):
                        vslot = load_w8(sc_kvb[2 + hv], ("kvb", 2 + hv))

                        def vdst(gi, nr, b, hv=hv):
                            cpy("act" if nr == 128 else "dve", VB[0:nr, cur_half, gi, hv * 512:(hv + 1) * 512],
                                PS[b][0:nr, 0:512], [R_ps[b]], [R_vb[cur_half][gi]])

                        def vout(gi, nr, b, hv=hv):
                            si = gi % 2
                            cpy("dve", stage[si][0:nr, 0:512], PS[b][0:nr, 0:512], [R_ps[b]], [R_stage[si]])
                            o = dma("sp", outs["bv"][gi * 128: gi * 128 + nr, hv * 512:(hv + 1) * 512],
                                    stage[si][0:nr, 0:512], [R_stage[si]], [Res()], ch_stage[si])
                            R_out.append(o)
                        tok_major(R_h, T, vslot, 0, 512, vdst, vout if want_out else None)
                    if want_out:
                        for hk in range(2):
                            kslot = load_w8(sc_kvb[hk], ("kvb", hk))

                            def kout(gi, nr, b, hk=hk):
                                si = gi % 2
                                cpy("dve", stage[si][0:nr, 0:512], PS[b][0:nr, 0:512], [R_ps[b]], [R_stage[si]])
                                o = dma("sp", outs["bk"][gi * 128: gi * 128 + nr, hk * 512:(hk + 1) * 512],
                                        stage[si][0:nr, 0:512], [R_stage[si]], [Res()], ch_stage[si])
                                R_out.append(o)
                            tok_major(R_h, T, kslot, 0, 512, None, kout)
                    h_from_x(g, 2, T)
                elif l < 3:
                    layer_norm(g, l, 1, T, FT[:, g, l, 1, 0, :], FT[:, g, l, 1, 1, :], True)
                else:
                    layer_norm(g, l, 1, T, None, None, False)
            ngr = (T + 127) // 128
            for gi in range(ngr):
                nr = min(128, T - gi * 128)
                si = gi % 2
                for half in range(2):
                    b = next_d()
                    for cc in range(4):
                        c = half * 4 + cc
                        tr(PS[b][0:nr, cc * 128:(cc + 1) * 128], xT[:, c, gi * 128: gi * 128 + nr], ident[:],
                           [R_x[c], R_const], [R_ps[b]])
                    eng = "act" if half == 0 else "dve"
                    cpy(eng, stage[si][0:nr, half * 512:(half + 1) * 512], PS[b][0:nr, 0:512], [R_ps[b]], [R_stage[si]])
                o = dma("sp", ydst[gi * 128: gi * 128 + nr, :], stage[si][0:nr, :], [R_stage[si]], [Res()], ch_stage[si])
                R_out.append(o)

        def load_sample_caches():
            for la in range(2):
                st = stage[0]
                stv = st[:, 0:512].rearrange("p (g u d) -> p g u d", g=4, u=2)
                for dup in range(2):
                    dma("sp", stv[:, :, dup, :], cache_a_k[la].rearrange("k (g d) -> k g d", g=4), (), [R_stage[0]],
                        ch_stage[0])
                for g4 in range(4):
                    b = next_d()
                    tr(PS[b][:, 0:128], st[:, g4 * 128:(g4 + 1) * 128], ident[:], [R_stage[0], R_const], [R_ps[b]])
                    cpy("dve", KA[la][:, g4, 0:128], PS[b][:, 0:128], [R_ps[b]], [R_ka[la][0]])
                dma("sp", stage[1][:, 0:256], cache_a_v[la], (), [R_stage[1]], ch_stage[1])
                cpy("dve", VA[la][:, 0, :], stage[1][:, 0:256], [R_stage[1]], [R_va[la][0]])
            for tl in range(4):
                si = tl % 2
                dma("sp", stage[si][:, :], cache_b_k[tl * 128:(tl + 1) * 128, :], (), [R_stage[si]], ch_stage[si])
                for half in range(2):
                    b = next_d()
                    for cc in range(4):
                        c = half * 4 + cc
                        tr(PS[b][:, cc * 128:(cc + 1) * 128], stage[si][:, c * 128:(c + 1) * 128], ident[:],
                           [R_stage[si], R_const], [R_ps[b]])
                    cpy("dve" if half else "act", KB[:, 0, half * 4:(half + 1) * 4, tl * 128:(tl + 1) * 128],
                        PS[b][:, :].rearrange("p (c t) -> p c t", c=4), [R_ps[b]], [R_kb[0][tl]])
            for tl in range(4):
                si = tl % 2
                dma("sp", stage[si][:, :], cache_b_v[tl * 128:(tl + 1) * 128, :], (), [R_stage[si]], ch_stage[si])
                cpy("dve", VB[:, 0, tl, :], stage[si][:, :], [R_stage[si]], [R_vb[0][tl]])
            for l in range(4):
                for r in range(2):
                    src = bass.AP(state_conv.tensor, l * 2 * 2 * DFF + r * 2 * DFF, [[1, 128], [128, NCH]])
                    dma("sp", carry[:, l, :, r], src, (), [R_carry[l][c] for c in range(NCH)], ch_misc, slow=True)

        import os as _os
        KSTOP = int(_os.environ.get("KSTOP", "99"))
        if KSTOP >= 1:
            emit_casts()
        if KSTOP >= 2:
            setup()
        P.epoch = 1
        if KSTOP >= 3:
            load_sample_caches()
        if KSTOP >= 4:
            run_tile(1, DEC, -1, x_sample, y_sample, False,
                     {"ak": s_ak, "av": s_av, "bk": s_bk, "bv": s_bv, "conv": s_conv})
            mset("pool", carry[:], 0.0, [R_carry[l][c] for l in range(4) for c in range(NCH)])
        for t in range(NT if KSTOP >= 5 else 0):
            P.epoch = 2 + t
            run_tile(0, TT, t, x_prompt[t * TT:(t + 1) * TT, :], y_prompt[t * TT:(t + 1) * TT, :], t == NT - 1,
                     {"ak": p_ak, "av": p_av, "bk": p_bk, "bv": p_bv, "conv": p_conv})
        n_epochs = NT + 2

        eng_sems = {}
        for e in ENGS:
            users = set()
            for op in P.ops[e]:
                if op.dma is None:
                    users |= op.sigs
            for c in sorted(users):
                eng_sems[(e, c)] = new_sem(f"s_{e}_{c}")
        for e in ENGS:
            cnt = {}
            for op in P.ops[e]:
                if op.dma is None:
                    for c in op.sigs:
                        cnt[c] = cnt.get(c, 0) + 1
                        op.cnts[c] = cnt[c]
            print("SEMCNT", e, cnt)
            for op in P.ops[e]:
                if op.dma is None and len(op.sigs) > 1:
                    print("MULTISIG", e, op.pos, op.epoch, op.sigs)
                    break
        out_ops = list(R_out)

        if _os.environ.get("DUMPW"):
            for en in ENGS:
                for op in P.ops[en][-6:]:
                    print("TAIL", en, op.pos, op.epoch, "sig", op.sig, op.cnt, "dma", None if op.dma is None else op.dmaval,
                          [(d.eng, d.epoch, d.pos, d.cnt if d.dma is None else ("dma", d.dmaval)) for d in op.waits])

        def replay(eng_name, e):
            for op in P.ops[eng_name]:
                for d in op.waits:
                    if d.dma is not None:
                        e.wait_ge(d.dma.sem, d.dmaval)
                    else:
                        e.wait_ge(eng_sems[(d.eng, eng_name)], d.cnts[eng_name])
                if op.fn is None:
                    continue
                ins = op.fn(e)
                if op.dma is not None:
                    ins.then_inc(op.dma.sem, 16)
                else:
                    for c in sorted(op.sigs):
                        ins.then_inc(eng_sems[(op.eng, c)], 1)
            if eng_name == "sp":
                done = {}
                for o in out_ops:
                    if o.dma is None:
                        continue
                    key = id(o.dma)
                    if key not in done or done[key][1] < o.dmaval:
                        done[key] = (o.dma, o.dmaval)
                for chn, val in done.values():
                    e.wait_ge(chn.sem, val)

        with nc.Block() as block:
            @block.tensor
            def _(e):
                replay("pe", e)

            @block.scalar
            def _(e):
                replay("act", e)

            @block.vector
            def _(e):
                replay("dve", e)

            @block.gpsimd
            def _(e):
                replay("pool", e)

            @block.sync
            def _(e):
                replay("sp", e)
    stats = {e: len(P.ops[e]) for e in ENGS}
    print("SBUF remaining", nc.sbuf_bytes_remaining)
    return nc, stats


_CACHE = {}


def _host_inputs(inputs, NT):
    f = lambda a: np.ascontiguousarray(np.asarray(a, dtype=np.float32))
    shared = {}
    for k in ("w_ada", "w_qkv_a", "w_o_a", "sinks_a", "t5_table", "w_ada_kv", "w_kv_b", "w_q_b", "w_o_b",
              "relpos_b", "w_up", "w_down"):
        shared[k] = f(inputs[k])
    shared["ident"] = np.eye(128, dtype=np.float32)
    shared["anti"] = np.ascontiguousarray(np.eye(128, dtype=np.float32)[::-1])
    shared["t5oh"] = _t5_onehot()
    in_maps = []
    for b in range(8):
        vec = np.zeros((V_ROWS, 128), np.float32)
        vec[V_BADA:V_BADA + 192] = f(inputs["b_ada"]).reshape(192, 128)
        vec[V_BKV:V_BKV + 16] = f(inputs["b_ada_kv"]).reshape(16, 128)
        vec[V_LNG:V_LNG + 64] = f(inputs["ln_g"]).reshape(64, 128)
        vec[V_LNB:V_LNB + 64] = f(inputs["ln_b"]).reshape(64, 128)
        vec[V_CW:V_CW + 528] = f(inputs["conv_w"]).reshape(528, 128)
        vec[V_CB:V_CB + 176] = f(inputs["conv_b"]).reshape(176, 128)
        vec[V_C:V_C + 8] = f(inputs["c_prompt"])[b].reshape(8, 128)
        vec[V_C + 8:V_C + 16] = f(inputs["c_sample"])[b].reshape(8, 128)
        m = dict(shared)
        m["vecs"] = vec
        m["x_prompt"] = f(inputs["x_prompt"][b])
        m["x_sample"] = f(inputs["x_sample"][b])
        m["cache_a_k"] = f(inputs["cache_a_k"][:, b]).reshape(2, 128, 256)
        m["cache_a_v"] = f(inputs["cache_a_v"][:, b]).reshape(2, 128, 256)
        m["cache_b_k"] = f(inputs["cache_b_k"][b]).reshape(512, 1024)
        m["cache_b_v"] = f(inputs["cache_b_v"][b]).reshape(512, 1024)
        m["state_conv"] = f(inputs["state_conv"][:, b])
        in_maps.append(m)
    return in_maps


def kernel(**inputs):
    S = int(np.asarray(inputs["x_prompt"]).shape[1])
    NT = S // TT
    if NT not in _CACHE:
        _CACHE[NT] = build(NT)[0]
    nc = _CACHE[NT]
    in_maps = _host_inputs(inputs, NT)
    res = run_bass_kernel_spmd(nc, in_maps, core_ids=list(range(8)))
    R = res.results
    st = lambda k: np.stack([np.asarray(r[k], dtype=np.float32) for r in R], axis=0)
    y_prompt = st("y_prompt")
    y_sample = st("y_sample")
    rows_a = min(128, S)
    rows_b = min(512, S)
    p_ak = np.ascontiguousarray(st("p_ak").transpose(1, 0, 2, 3)).reshape(2, 8, 128, 4, 64)[:, :, 128 - rows_a:]
    p_av = np.ascontiguousarray(st("p_av").transpose(1, 0, 2, 3)).reshape(2, 8, 128, 4, 64)[:, :, 128 - rows_a:]
    p_bk = st("p_bk").reshape(8, 512, 16, 64)[:, 512 - rows_b:]
    p_bv = st("p_bv").reshape(8, 512, 16, 64)[:, 512 - rows_b:]
    p_conv = np.ascontiguousarray(st("p_conv").transpose(1, 0, 2, 3))
    s_ak = np.ascontiguousarray(st("s_ak").transpose(1, 0, 2, 3)).reshape(2, 8, DEC, 4, 64)
    s_av = np.ascontiguousarray(st("s_av").transpose(1, 0, 2, 3)).reshape(2, 8, DEC, 4, 64)
    s_bk = st("s_bk").reshape(8, DEC, 16, 64)
    s_bv = st("s_bv").reshape(8, DEC, 16, 64)
    s_conv = np.ascontiguousarray(st("s_conv").transpose(1, 0, 2, 3))
    return (y_prompt, y_sample, p_ak, p_av, p_bk, p_bv, p_conv, s_ak, s_av, s_bk, s_bv, s_conv)
```

```python
import math
import contextlib
import numpy as np
import concourse.bass as bass
import concourse.mybir as mybir
from concourse.bass_utils import run_bass_kernel_spmd

F32 = mybir.dt.float32
BF16 = mybir.dt.bfloat16
AF = mybir.ActivationFunctionType
ALU = mybir.AluOpType

D = 1024
C8 = 8
TT = 512
DFF = 2816
FC = 22
NCH = 44
DEPTH = 4
DEC = 16
ALPHA = (2.0 * DEPTH) ** 0.25
LN_EPS = 1e-5
EPS_P = LN_EPS / (ALPHA * ALPHA)
SCALE = 64 ** -0.5
UP_ORDER = list(range(22, 44)) + list(range(0, 22))

V_BADA = 0
V_BKV = 192
V_LNG = 208
V_LNB = 272
V_CW = 336
V_CB = 864
V_C = 1040
V_ROWS = 1152


class Res:
    __slots__ = ("w", "r", "rd")

    def __init__(self):
        self.w = None
        self.r = {}
        self.rd = []


class Chan:
    def __init__(self, sem):
        self.sem = sem
        self.count = 0


class Op:
    __slots__ = ("eng", "fn", "waits", "sig", "pos", "cnt", "dma", "dmaval", "epoch", "sigs", "cnts", "waiter")


ENGS = ["pe", "act", "dve", "pool", "sp"]
import os as _osm
MAXSIG = int(_osm.environ.get("MAXSIG", "1"))


class Prog:
    def __init__(self):
        self.ops = {e: [] for e in ENGS}
        self.seen = {e: {} for e in ENGS}
        self.epoch = 0

    def emit(self, eng, fn, reads=(), writes=(), chan=None):
        op = Op()
        op.eng = eng
        op.fn = fn
        op.pos = len(self.ops[eng])
        op.sig = False
        op.sigs = set()
        op.cnts = {}
        op.waiter = {}
        op.dma = chan
        op.epoch = self.epoch
        op.cnt = 0
        if chan is not None:
            chan.count += 16
            op.dmaval = chan.count
        deps = []
        for r in reads:
            if r.w is not None:
                deps.append((r.w, True))
        for w in writes:
            if w.w is not None:
                deps.append((w.w, False))
            for rd in w.r.values():
                deps.append((rd, False))
            for rd in w.rd:
                deps.append((rd, False))
        waits = []
        seen = self.seen[eng]
        for d, raw in deps:
            if d is op:
                continue
            if d.dma is not None:
                key = ("dma", id(d.dma))
                if seen.get(key, -1) >= d.dmaval:
                    continue
                seen[key] = d.dmaval
                waits.append(d)
                continue
            if d.eng == eng:
                if eng == "pe":
                    continue
            hops = 0
            while d.dma is None and eng not in d.sigs and len(d.sigs) >= MAXSIG:
                plst = self.ops[d.eng]
                zf = None
                for k in range(d.pos + 1, min(len(plst), d.pos + 13)):
                    z = plst[k]
                    if z.dma is None and z.fn is not None and (eng in z.sigs or len(z.sigs) == 0):
                        zf = z
                        break
                if zf is not None and zf.eng != eng:
                    d = zf
                    break
                c0 = sorted(d.sigs)[0]
                w0 = d.waiter[c0]
                if w0 is None:
                    for k in range(d.pos + 1, min(len(plst), d.pos + 200)):
                        z = plst[k]
                        if z.dma is None and z.fn is not None and (eng in z.sigs or len(z.sigs) == 0):
                            zf = z
                            break
                    if zf is not None and zf.eng != eng:
                        d = zf
                        break
                    d.sigs.discard(c0)
                    d.waiter.pop(c0, None)
                    break
                hops += 1
                if w0.eng == eng or w0.dma is not None:
                    d = w0
                    break
                lst = self.ops[c0]
                found = None
                for k in range(w0.pos, min(len(lst), w0.pos + 400)):
                    z = lst[k]
                    if z.dma is None and z.fn is not None and (eng in z.sigs or len(z.sigs) == 0):
                        found = z
                        break
                if found is not None:
                    d = found
                    break
                d = lst[min(len(lst), w0.pos + 400) - 1]
                if d.dma is not None or d.fn is None:
                    d = w0
            if d.dma is not None:
                key = ("dma", id(d.dma))
                if seen.get(key, -1) >= d.dmaval:
                    continue
                seen[key] = d.dmaval
                waits.append(d)
                continue
            if d.eng == eng and (eng == "pe" or d is op):
                continue
            if d.eng == eng and hops > 0:
                continue
            key = d.eng
            if seen.get(key, -1) >= d.pos:
                continue
            seen[key] = d.pos
            d.sig = True
            d.sigs.add(eng)
            if d.waiter.get(eng) is None:
                d.waiter[eng] = op
            waits.append(d)
        op.waits = waits
        for r in reads:
            if chan is not None:
                r.rd.append(op)
            else:
                r.r[eng] = op
        for w in writes:
            w.w = op
            w.r = {}
            w.rd = []
        self.ops[eng].append(op)
        return op


def _t5_onehot():
    try:
        import jax
        import jax.numpy as jnp
        cpu = jax.devices("cpu")[0]
        with jax.default_device(cpu):
            rel = jnp.asarray(np.concatenate([(-128 + 127 - np.arange(255)), (127 - np.arange(255))]).astype(np.int32))
            nb = 16
            max_exact = 8
            n = jnp.abs(rel)
            large = max_exact + (jnp.log(jnp.maximum(n, 1).astype(jnp.float32) / max_exact)
                                 / math.log(128 / max_exact) * (nb - max_exact)).astype(jnp.int32)
            large = jnp.minimum(large, nb - 1)
            bucket = jnp.where(rel > 0, nb, 0) + jnp.where(n < max_exact, n, large)
            bucket = np.asarray(bucket)
    except Exception:
        rel = np.concatenate([(-128 + 127 - np.arange(255)), (127 - np.arange(255))]).astype(np.int32)
        n = np.abs(rel)
        large = 8 + (np.log(np.maximum(n, 1).astype(np.float32) / np.float32(8))
                     / np.float32(math.log(16.0)) * np.float32(8)).astype(np.int32)
        large = np.minimum(large, 15)
        bucket = np.where(rel > 0, 16, 0) + np.where(n < 8, n, large)
    oh = np.zeros((32, 510), np.float32)
    oh[bucket, np.arange(510)] = 1.0
    return oh


def build(NT):
    S = NT * TT
    nc = bass.Bass("TRN2", target_bir_lowering=False)
    P = Prog()

    def din(name, shape, dt=F32):
        return nc.dram_tensor(name, list(shape), dt, kind="ExternalInput").ap()

    def dout(name, shape, dt=F32):
        return nc.dram_tensor(name, list(shape), dt, kind="ExternalOutput").ap()

    def dscr(name, shape, dt=BF16):
        return nc.dram_tensor(name, list(shape), dt, kind="Internal").ap()

    x_prompt = din("x_prompt", [S, D])
    x_sample = din("x_sample", [DEC, D])
    cache_a_k = din("cache_a_k", [2, 128, 256])
    cache_a_v = din("cache_a_v", [2, 128, 256])
    cache_b_k = din("cache_b_k", [512, 1024])
    cache_b_v = din("cache_b_v", [512, 1024])
    state_conv = din("state_conv", [4, 2, 2 * DFF])
    w_ada = din("w_ada", [4, D, 6 * D])
    w_qkv_a = din("w_qkv_a", [2, D, 1536])
    w_o_a = din("w_o_a", [2, D, D])
    sinks_a = din("sinks_a", [2, 16])
    t5_table = din("t5_table", [32, 16])
    w_ada_kv = din("w_ada_kv", [D, 2 * D])
    w_kv_b = din("w_kv_b", [D, 2 * D])
    w_q_b = din("w_q_b", [2, D, D])
    w_o_b = din("w_o_b", [2, D, D])
    relpos_b = din("relpos_b", [2, 16, 257])
    w_up = din("w_up", [4, D, 2 * DFF])
    w_down = din("w_down", [4, DFF, D])
    vecs = din("vecs", [V_ROWS, 128])
    ident_d = din("ident", [128, 128])
    anti_d = din("anti", [128, 128])
    onehot_d = din("t5oh", [32, 510])

    y_prompt = dout("y_prompt", [S, D])
    y_sample = dout("y_sample", [DEC, D])
    p_ak = dout("p_ak", [2, 128, 256])
    p_av = dout("p_av", [2, 128, 256])
    p_bk = dout("p_bk", [512, 1024])
    p_bv = dout("p_bv", [512, 1024])
    p_conv = dout("p_conv", [4, 2, 2 * DFF])
    s_ak = dout("s_ak", [2, DEC, 256])
    s_av = dout("s_av", [2, DEC, 256])
    s_bk = dout("s_bk", [DEC, 1024])
    s_bv = dout("s_bv", [DEC, 1024])
    s_conv = dout("s_conv", [4, 2, 2 * DFF])

    sc_up = [dscr(f"sc_up{l}", [11, 128, 8 * 512]) for l in range(4)]
    sc_dn = [dscr(f"sc_dn{l}", [8, 128, 22 * 128]) for l in range(4)]
    sc_qa = [dscr(f"sc_qa{l}", [4, 128, 8 * 512]) for l in range(2)]
    sc_oa = [dscr(f"sc_oa{l}", [2, 128, 8 * 512]) for l in range(2)]
    sc_qb = [dscr(f"sc_qb{l}", [2, 128, 8 * 512]) for l in range(2)]
    sc_ob = [dscr(f"sc_ob{l}", [2, 128, 8 * 512]) for l in range(2)]
    sc_kvb = dscr("sc_kvb", [4, 128, 8 * 512])
    sc_veca = dscr("sc_veca", [16, 510], F32)
    sc_vecb = [dscr(f"sc_vecb{j}", [16, 510], F32) for j in range(2)]
    sc_ebb = [dscr(f"sc_ebb{j}", [128, 4096]) for j in range(2)]

    es = contextlib.ExitStack()
    with es:
        def sb(name, shape, dt):
            return es.enter_context(nc.sbuf_tensor(name, list(shape), dt))

        def ps(name):
            return es.enter_context(nc.psum_tensor(name, [128, 512], F32))

        xT = sb("xT", [128, C8, TT], F32)
        hT = sb("hT", [128, C8, TT], BF16)
        actT = sb("actT", [128, FC, TT], BF16)
        KB = sb("KB", [128, 2, C8, TT], BF16)
        VB = sb("VB", [128, 2, 4, D], BF16)
        KA = [sb(f"KA{l}", [128, 4, 5 * 128], BF16) for l in range(2)]
        VA = [sb(f"VA{l}", [128, 5, 256], BF16) for l in range(2)]
        Ubuf = [[sb(f"U{a}{b}", [128, TT + 2], F32) for b in range(2)] for a in range(2)]
        Ybuf = [[sb(f"Y{a}{b}", [128, TT], F32) for b in range(2)] for a in range(2)]
        Ebuf = [sb(f"E{i}", [128, 512], BF16) for i in range(5)]
        E0buf = sb("E0m", [128, 512], BF16)
        Tbuf = sb("Tb", [128, 512], F32)
        R2buf = sb("R2", [128, 512], F32)
        ybf = [sb(f"ybf{i}", [128, TT], BF16) for i in range(2)]
        sqb = [sb(f"sq{i}", [128, TT], BF16) for i in range(2)]
        mean_sb = sb("mean", [128, TT], F32)
        var_sb = sb("var", [128, TT], F32)
        rstd_sb = sb("rstd", [128, TT], F32)
        tmpb = [sb(f"tmp{i}", [128, TT], F32) for i in range(2)]
        EBA = sb("EBA", [128, 2, 2, 8, 128], BF16)
        EBB = sb("EBB", [128, 2, 2, 8, 128], BF16)
        W8 = [sb(f"W8_{i}", [128, 8 * 512], BF16) for i in range(3)]
        W22 = [sb(f"W22_{i}", [128, 22 * 128], BF16) for i in range(2)]
        stage = [sb(f"stg{i}", [128, D], F32) for i in range(2)]
        ident = sb("identS", [128, 128], F32)
        anti = sb("antiS", [128, 128], F32)
        ones_bf = sb("ones_bf", [128, 64], BF16)
        onesf = sb("onesf", [128, 128], BF16)
        VF = sb("VF", [128, V_ROWS], F32)
        SCt = sb("SCt", [128, 16], F32)
        ADA = sb("ADA", [128, 2, 208], F32)
        ADA1 = sb("ADA1", [128, 2, 208], F32)
        ADAG = sb("ADAG", [128, 2, 208], F32)
        FT = sb("FT", [128, 2, 4, 2, 2, 8], F32)
        FKV = sb("FKV", [128, 2, 2, 8], F32)
        carry = sb("carry", [128, 4, NCH, 2], F32)
        esink = sb("esink", [128, 2, 8], F32)
        t5_sb = sb("t5_sb", [32, 16], F32)
        tokA = tmpb[0][0:2, 0:256]
        oh_sb = mean_sb[0:32, 0:510]
        vec_sb = var_sb[0:16, 0:510]
        rp_sb = rstd_sb[0:16, 0:257]

        print("SBUF bytes remaining after alloc", nc.sbuf_bytes_remaining)
        PS = [ps(f"ps{i}") for i in range(8)]
        D0, D1, S0, S1, O0, O1, N0, N1 = range(8)

        sems = {}
        sem_list = []

        def new_sem(name):
            s = es.enter_context(nc.semaphore(name))
            sem_list.append(s)
            return s

        def new_chan(name):
            return Chan(new_sem(name))

        R_x = [Res() for _ in range(C8)]
        R_h = [Res() for _ in range(C8)]
        R_act = [Res() for _ in range(FC)]
        R_ps = [Res() for _ in range(8)]
        R_w8 = [Res() for _ in range(3)]
        R_w22 = [Res() for _ in range(2)]
        R_ka = [[Res() for _ in range(5)] for _ in range(2)]
        R_va = [[Res() for _ in range(5)] for _ in range(2)]
        R_kb = [[Res() for _ in range(4)] for _ in range(2)]
        R_vb = [[Res() for _ in range(4)] for _ in range(2)]
        R_u = [[Res() for _ in range(2)] for _ in range(2)]
        R_uh = [[Res() for _ in range(2)] for _ in range(2)]
        R_y = [[Res() for _ in range(2)] for _ in range(2)]
        R_e = [Res() for _ in range(5)]
        R_e0 = Res()
        R_t = Res()
        R_r2 = Res()
        R_ybf = [Res(), Res()]
        R_sq = [Res(), Res()]
        R_mean = Res()
        R_var = Res()
        R_rstd = Res()
        R_tmp = [Res(), Res()]
        R_eba = Res()
        R_ebb = Res()
        R_stage = [Res(), Res()]
        R_const = Res()
        R_vf = Res()
        R_sc = Res()
        R_ada = Res()
        R_tabs = Res()
        R_carry = [[Res() for _ in range(NCH)] for _ in range(4)]
        R_esink = Res()
        R_tok = Res()
        R_misc = Res()
        R_scr = {}
        R_out = []

        ch_w8 = [new_chan(f"cw8_{i}") for i in range(3)]
        ch_w22 = [new_chan(f"cw22_{i}") for i in range(2)]
        ch_stage = [new_chan(f"cst{i}") for i in range(2)]
        ch_misc = new_chan("cmisc")
        ch_cast = [new_chan(f"ccast{i}") for i in range(4)]
        R_castq = [Res() for _ in range(4)]
        R_miscq = Res()
        cast_key_idx = {}
        ch_ebb = new_chan("cebb")

        def mm(out, lhsT, rhs, start, stop, reads, writes, skip=False, last_reader=False):
            if last_reader:
                o_ = P.emit("pe", lambda e: e.matmul(out, lhsT, rhs, start=start, stop=stop), reads, writes)
                o_.sig = True
                o_.sigs.add("sp")
                o_.waiter.setdefault("sp", None)
                return o_
            if skip:
                return P.emit("pe", lambda e: e.matmul(out, lhsT, rhs, start=start, stop=stop, skip_group_check=True),
                              reads, writes)
            return P.emit("pe", lambda e: e.matmul(out, lhsT, rhs, start=start, stop=stop), reads, writes)

        def tr(out, in_, idn, reads, writes):
            return P.emit("pe", lambda e: e.transpose(out, in_, idn), reads, writes)

        def act(out, in_, func, reads, writes, bias=0.0, scale=1.0):
            return P.emit("act", lambda e: e.activation(out, in_, func, bias=bias, scale=scale), reads, writes)

        def tsc(eng, out, in0, s1, s2, op0, op1, reads, writes):
            if op1 is None:
                return P.emit(eng, lambda e: e.tensor_scalar(out, in0, s1, None, op0), reads, writes)
            return P.emit(eng, lambda e: e.tensor_scalar(out, in0, s1, s2, op0, op1), reads, writes)

        def stt(eng, out, in0, scalar, in1, op0, op1, reads, writes):
            return P.emit(eng, lambda e: e.scalar_tensor_tensor(out, in0, scalar, in1, op0, op1), reads, writes)

        def ttn(eng, out, in0, in1, op, reads, writes):
            return P.emit(eng, lambda e: e.tensor_tensor(out, in0, in1, op), reads, writes)

        def cpy(eng, out, in_, reads, writes):
            if eng == "act":
                return P.emit("act", lambda e: e.activation(out, in_, AF.Identity), reads, writes)
            return P.emit(eng, lambda e: e.tensor_copy(out, in_), reads, writes)

        def mset(eng, ap, val, writes):
            return P.emit(eng, lambda e: e.memset(ap, val), (), writes)

        import os as _os0
        NOOUT = _os0.environ.get("NOOUT", "0") == "1"
        out_names = ("y_prompt", "y_sample", "p_ak", "p_av", "p_bk", "p_bv", "p_conv", "s_ak", "s_av", "s_bk",
                     "s_bv", "s_conv")

        def dma(q, out, in_, reads, writes, chan, slow=False):
            if NOOUT and out.tensor.name in out_names:
                op_ = Op()
                op_.dma = None
                return op_
            if chan is ch_misc:
                writes = list(writes) + [R_miscq]
            if slow:
                return P.emit(q, lambda e: e.dma_start(out=out, in_=in_, allow_slow_non_contiguous=True),
                              reads, writes, chan)
            return P.emit(q, lambda e: e.dma_start(out=out, in_=in_), reads, writes, chan)

        def scr_res(key):
            if key not in R_scr:
                R_scr[key] = Res()
            return R_scr[key]

        def cast(dst, src, key):
            if key not in cast_key_idx:
                cast_key_idx[key] = len(cast_key_idx) % 4
            ci = cast_key_idx[key]
            dma("pool", dst, src, (), [scr_res(key), R_castq[ci]], ch_cast[ci])

        def cast_cols(scr, slot, w2d, col0, key):
            src = w2d[:, col0:col0 + 512].rearrange("(kc p) c -> p kc c", p=128)
            dst = scr[slot].rearrange("p (kc c) -> p kc c", kc=8)
            cast(dst, src, key)

        def emit_casts():
            for l in range(4):
                if l < 2:
                    for s in range(2):
                        cast_cols(sc_qa[l], s, w_qkv_a[l], s * 512, ("qa", l, s))
                    for dup in range(2):
                        for kc in range(8):
                            src = w_qkv_a[l][kc * 128:(kc + 1) * 128, 1024:1280].rearrange("p (g d) -> p g d", g=4)
                            dst = sc_qa[l][2].rearrange("p (kc g u d) -> p kc g u d", kc=8, g=4, u=2)[:, kc, :, dup, :]
                            cast(dst, src, ("qa", l, 2))
                    cast_cols(sc_qa[l], 3, w_qkv_a[l], 1024, ("qa", l, 3))
                    for s in range(2):
                        cast_cols(sc_oa[l], s, w_o_a[l], s * 512, ("oa", l, s))
                else:
                    j = l - 2
                    for s in range(2):
                        cast_cols(sc_qb[j], s, w_q_b[j], s * 512, ("qb", j, s))
                    for s in range(2):
                        cast_cols(sc_ob[j], s, w_o_b[j], s * 512, ("ob", j, s))
                for s in range(11):
                    chs = UP_ORDER[4 * s:4 * s + 4]
                    if chs[3] == chs[0] + 3:
                        cast_cols(sc_up[l], s, w_up[l], chs[0] * 128, ("up", l, s))
                    else:
                        for hh in range(2):
                            c0 = chs[2 * hh] * 128
                            src = w_up[l][:, c0:c0 + 256].rearrange("(kc p) c -> p kc c", p=128)
                            dst = sc_up[l][s].rearrange("p (kc c) -> p kc c", kc=8)[:, :, hh * 256:(hh + 1) * 256]
                            cast(dst, src, ("up", l, s))
                for m in range(8):
                    src = w_down[l][:, m * 128:(m + 1) * 128].rearrange("(kc p) c -> p kc c", p=128)
                    dst = sc_dn[l][m].rearrange("p (kc c) -> p kc c", kc=22)
                    cast(dst, src, ("dn", l, m))
                if l == 1:
                    for s in range(4):
                        cast_cols(sc_kvb, s, w_kv_b, s * 512, ("kvb", s))

        w8_n = [0]
        w22_n = [0]

        def load_w8(scr_ap, key):
            i = w8_n[0] % 3
            w8_n[0] += 1
            dma("sp", W8[i][:], scr_ap, [scr_res(key)], [R_w8[i]], ch_w8[i])
            return i

        def load_w22(scr_ap, key):
            i = w22_n[0] % 2
            w22_n[0] += 1
            dma("sp", W22[i][:], scr_ap, [scr_res(key)], [R_w22[i]], ch_w22[i])
            return i

        def w8v(i):
            return W8[i][:].rearrange("p (kc c) -> p kc c", kc=8)

        def setup():
            dma("sp", ident[:], ident_d[:, :], (), [R_const], ch_misc)
            dma("sp", anti[:], anti_d[:, :], (), [R_const], ch_misc)
            dma("sp", oh_sb, onehot_d[:, :], (), [R_mean], ch_misc)
            dma("sp", t5_sb[:], t5_table[:, :], (), [R_const], ch_misc)
            mset("dve", ones_bf[:], 1.0, [R_const])
            mset("dve", onesf[:], 1.0 / 1024.0, [R_const])
            mset("pool", carry[:], 0.0, [R_carry[l][c] for l in range(4) for c in range(NCH)])
            mset("pool", E0buf[:], 0.0, [R_e0])
            mset("pool", hT[:], 0.0, R_h)
            mset("pool", KB[:], 0.0, R_kb[0] + R_kb[1])
            for la_ in range(2):
                mset("pool", KA[la_][:], 0.0, R_ka[la_])
            for r in range(V_ROWS // 128):
                st = stage[r % 2]
                dma("sp", st[:, 0:128], vecs[r * 128:(r + 1) * 128, :], (), [R_stage[r % 2]], ch_stage[r % 2])
                b = D0 + (r % 2)
                tr(PS[b][:, 0:128], st[:, 0:128], ident[:], [R_stage[r % 2], R_const], [R_ps[b]])
                cpy("dve", VF[:, r * 128:(r + 1) * 128], PS[b][:, 0:128], [R_ps[b]], [R_vf])
            act(SCt[:], VF[:, V_C:V_C + 16], AF.Silu, [R_vf], [R_sc])
            blocks = [(w_ada[l], cb, l * 48 + cb * 2) for l in range(4) for cb in range(24)]
            blocks += [(w_ada_kv, cb, 192 + cb * 2) for cb in range(8)]
            for bi, (w2d, cb, col) in enumerate(blocks):
                i = w8_n[0] % 3
                w8_n[0] += 1
                wv = W8[i][:].bitcast(F32).rearrange("p (kc c) -> p kc c", kc=8)
                src = w2d[:, cb * 256:(cb + 1) * 256].rearrange("(kc p) c -> p kc c", p=128)
                dma("sp", wv, src, (), [R_w8[i]], ch_w8[i])
                b = D0 + (bi % 2)
                for kc in range(8):
                    mm(PS[b][0:2, 0:256], SCt[:, kc:16:8], wv[:, kc, :], kc == 0, kc == 7,
                       [R_sc, R_w8[i]], [R_ps[b]])
                cpy("act", tokA, PS[b][0:2, 0:256], [R_ps[b]], [R_tmp[0]])
                b2 = O0 + (bi % 2)
                for hh in range(2):
                    tr(PS[b2][:, hh * 2:hh * 2 + 2], tokA[0:2, hh * 128:(hh + 1) * 128], ident[0:2, 0:2],
                       [R_tmp[0], R_const], [R_ps[b2]])
                cpy("dve", ADA[:, :, col:col + 2].rearrange("p g c -> p c g"),
                    PS[b2][:, 0:4].rearrange("p (c g) -> p c g", c=2), [R_ps[b2]], [R_ada])
            for g in range(2):
                ttn("dve", ADA[:, g, :], ADA[:, g, :], VF[:, 0:208], ALU.add, [R_ada, R_vf], [R_ada])
            tsc("dve", ADA1[:], ADA[:], 1.0, None, ALU.add, None, [R_ada], [R_tabs])
            tsc("dve", ADAG[:], ADA[:], 1.0, 1.0 / ALPHA, ALU.add, ALU.mult, [R_ada], [R_tabs])
            for g in range(2):
                for l in range(4):
                    g1 = VF[:, V_LNG + (l * 2) * 8: V_LNG + (l * 2) * 8 + 8]
                    b1 = VF[:, V_LNB + (l * 2) * 8: V_LNB + (l * 2) * 8 + 8]
                    a1 = ADA1[:, g, l * 48 + 32: l * 48 + 40]
                    sh = ADA[:, g, l * 48 + 24: l * 48 + 32]
                    ttn("dve", FT[:, g, l, 0, 0, :], g1, a1, ALU.mult, [R_vf, R_tabs], [R_tabs])
                    ttn("dve", FT[:, g, l, 0, 1, :], b1, a1, ALU.mult, [R_vf, R_tabs], [R_tabs])
                    ttn("dve", FT[:, g, l, 0, 1, :], FT[:, g, l, 0, 1, :], sh, ALU.add, [R_tabs, R_ada], [R_tabs])
                    if l < 3:
                        g2 = VF[:, V_LNG + (l * 2 + 1) * 8: V_LNG + (l * 2 + 1) * 8 + 8]
                        b2 = VF[:, V_LNB + (l * 2 + 1) * 8: V_LNB + (l * 2 + 1) * 8 + 8]
                        a1 = ADA1[:, g, (l + 1) * 48 + 8: (l + 1) * 48 + 16]
                        sh = ADA[:, g, (l + 1) * 48: (l + 1) * 48 + 8]
                        ttn("dve", FT[:, g, l, 1, 0, :], g2, a1, ALU.mult, [R_vf, R_tabs], [R_tabs])
                        ttn("dve", FT[:, g, l, 1, 1, :], b2, a1, ALU.mult, [R_vf, R_tabs], [R_tabs])
                        ttn("dve", FT[:, g, l, 1, 1, :], FT[:, g, l, 1, 1, :], sh, ALU.add, [R_tabs, R_ada], [R_tabs])
                g2 = VF[:, V_LNG + 3 * 8: V_LNG + 3 * 8 + 8]
                b2 = VF[:, V_LNB + 3 * 8: V_LNB + 3 * 8 + 8]
                a1 = ADA1[:, g, 200:208]
                sh = ADA[:, g, 192:200]
                ttn("dve", FKV[:, g, 0, :], g2, a1, ALU.mult, [R_vf, R_tabs], [R_tabs])
                ttn("dve", FKV[:, g, 1, :], b2, a1, ALU.mult, [R_vf, R_tabs], [R_tabs])
                ttn("dve", FKV[:, g, 1, :], FKV[:, g, 1, :], sh, ALU.add, [R_tabs, R_ada], [R_tabs])
            for l in range(2):
                for par in range(2):
                    src = bass.AP(sinks_a.tensor, l * 16 + par, [[0, 64], [2, 8]])
                    dma("sp", esink[par * 64:(par + 1) * 64, l, :], src, (), [R_esink], ch_misc, slow=True)
            act(esink[:], esink[:], AF.Exp, [R_esink], [R_esink])
            mm(PS[D0][0:16, 0:510], t5_sb[:], oh_sb, True, True, [R_const, R_mean], [R_ps[D0]])
            cpy("dve", vec_sb, PS[D0][0:16, 0:510], [R_ps[D0]], [R_var])
            dma("sp", sc_veca[:, :], vec_sb, [R_var], [scr_res("veca")], ch_misc)
            make_eb(sc_veca, "veca", EBA, R_eba)
            mset("pool", EBA[0:64, 0, :, :, 64:128], 0.0, [R_eba])
            mset("pool", EBA[64:128, 1, :, :, 0:64], 0.0, [R_eba])
            for j in range(2):
                dma("sp", rp_sb, relpos_b[j], (), [R_rstd], ch_misc)
                mset("dve", vec_sb, 0.0, [R_var])
                tsc("dve", vec_sb[:, 0:128], rp_sb[:, 129:257], rp_sb[:, 256:257], None, ALU.subtract, None,
                    [R_rstd, R_var], [R_var])
                tsc("dve", vec_sb[:, 255:510], rp_sb[:, 1:256], rp_sb[:, 256:257], None, ALU.subtract, None,
                    [R_rstd, R_var], [R_var])
                dma("sp", sc_vecb[j][:, :], vec_sb, [R_var], [scr_res(("vecb", j))], ch_misc)
                make_eb(sc_vecb[j], ("vecb", j), EBB, R_ebb)
                mset("pool", EBB[64:128, 1, :, :, 0:64], 0.0, [R_ebb])
                dma("sp", sc_ebb[j][:, :], EBB[:].rearrange("p a b c d -> p (a b c d)"), [R_ebb],
                    [scr_res(("ebb", j))], ch_misc)

        def make_eb(vec_dram, key, EB, R_eb):
            for ty in range(2):
                i = w8_n[0] % 3
                w8_n[0] += 1
                brev = W8[i][:].bitcast(F32).rearrange("p (h q) -> p h q", h=16)
                src = bass.AP(vec_dram.tensor, ty * 255, [[1, 128], [510, 16], [1, 128]])
                dma("sp", brev, src, [scr_res(key)], [R_w8[i]], ch_w8[i])
                for par in range(2):
                    for jg in range(2):
                        b = D0 + ((par * 2 + jg) % 2)
                        rhs = brev[:, par + 8 * jg: par + 8 * jg + 7: 2, :]
                        mm(PS[b][:, :], anti[:], rhs, True, True, [R_const, R_w8[i]], [R_ps[b]])
                        act(EB[:, ty, par, jg * 4:(jg + 1) * 4, :],
                            PS[b][:, :].rearrange("p (j q) -> p j q", j=4), AF.Exp, [R_ps[b]], [R_eb])

        def tab(g, l, part, c):
            col = l * 48 + part * 8 + c
            return col

        def load_x_group(src_rows, nrows, si):
            dma("sp", stage[si][0:nrows, :], src_rows, (), [R_stage[si]], ch_stage[si])

        def x_in(xsrc, T, g, first_groups_loaded=0):
            ngr = (T + 127) // 128
            for gi in range(ngr):
                nr = min(128, T - gi * 128)
                si = gi % 2
                if gi >= first_groups_loaded:
                    load_x_group(xsrc[gi * 128: gi * 128 + nr, :], nr, si)
                for half in range(2):
                    b = D0 + half
                    for cc in range(4):
                        c = half * 4 + cc
                        tr(PS[b][:, cc * 128: cc * 128 + nr], stage[si][0:nr, c * 128:(c + 1) * 128],
                           ident[0:nr, 0:nr], [R_stage[si], R_const], [R_ps[b]])
                    out = xT[:, half * 4:(half + 1) * 4, gi * 128: gi * 128 + nr]
                    inp = PS[b][:, :].rearrange("p (c t) -> p c t", c=4)[:, :, 0:nr]
                    if half == 0:
                        cpy("act", out, inp, [R_ps[b]], R_x[0:4])
                    else:
                        cpy("dve", out, inp, [R_ps[b]], R_x[4:8])

        def h_from_x(g, l, T):
            for c in range(C8):
                act(hT[:, c, 0:T], xT[:, c, 0:T], AF.Identity, [R_x[c], R_tabs], [R_h[c]],
                    bias=ADA[:, g, tab(g, l, 0, c): tab(g, l, 0, c) + 1],
                    scale=ADA1[:, g, tab(g, l, 1, c): tab(g, l, 1, c) + 1])

        dcount = [0]

        DRING = [D0, D1, S0, S1, O0, O1]

        def next_d():
            b = DRING[dcount[0] % len(DRING)]
            dcount[0] += 1
            return b

        def proj_fm(slot_keys, scr_list, KCn, src_fn, src_res, n_out, T, evac):
            cur = None
            for m in range(n_out):
                if m % 4 == 0:
                    cur = load_w8(scr_list[m // 4], slot_keys[m // 4])
                wv = w8v(cur)
                b = next_d()
                for kc in range(KCn):
                    mm(PS[b][:, 0:T], wv[:, kc, (m % 4) * 128:(m % 4 + 1) * 128], src_fn(kc), kc == 0, kc == KCn - 1,
                       [R_w8[cur], src_res[kc]], [R_ps[b]],
                       last_reader=(kc == KCn - 1 and (m % 4 == 3 or m == n_out - 1)))
                evac(m, b)

        def layer_norm(g, l, sub, T, ft1, ft2, h_out):
            for c in range(C8):
                i = c % 2
                act(ybf[i][:, 0:T], xT[:, c, 0:T], AF.Copy, [R_x[c]], [R_ybf[i]])
                act(sqb[i][:, 0:T], xT[:, c, 0:T], AF.Square, [R_x[c]], [R_sq[i]])
                mm(PS[D0][:, 0:T], onesf[:], ybf[i][:, 0:T], c == 0, c == 7, [R_const, R_ybf[i]], [R_ps[D0]])
                mm(PS[D1][:, 0:T], onesf[:], sqb[i][:, 0:T], c == 0, c == 7, [R_const, R_sq[i]], [R_ps[D1]])
            cpy("act", mean_sb[:, 0:T], PS[D0][:, 0:T], [R_ps[D0]], [R_mean])
            ttn("dve", var_sb[:, 0:T], mean_sb[:, 0:T], mean_sb[:, 0:T], ALU.mult, [R_mean], [R_var])
            ttn("dve", var_sb[:, 0:T], PS[D1][:, 0:T], var_sb[:, 0:T], ALU.subtract, [R_ps[D1], R_var], [R_var])
            tsc("dve", var_sb[:, 0:T], var_sb[:, 0:T], EPS_P, None, ALU.add, None, [R_var], [R_var])
            act(var_sb[:, 0:T], var_sb[:, 0:T], AF.Sqrt, [R_var], [R_var])
            P.emit("dve", lambda e: e.reciprocal(rstd_sb[:, 0:T], var_sb[:, 0:T]), [R_var], [R_rstd])
            lng = V_LNG + (l * 2 + sub) * 8
            lnb = V_LNB + (l * 2 + sub) * 8
            for c in range(C8):
                i = c % 2
                eng = "pool" if c % 2 == 0 else "dve"
                ttn(eng, tmpb[i][:, 0:T], xT[:, c, 0:T], mean_sb[:, 0:T], ALU.subtract, [R_x[c], R_mean], [R_tmp[i]])
                ttn(eng, tmpb[i][:, 0:T], tmpb[i][:, 0:T], rstd_sb[:, 0:T], ALU.mult, [R_tmp[i], R_rstd], [R_tmp[i]])
                act(xT[:, c, 0:T], tmpb[i][:, 0:T], AF.Identity, [R_tmp[i], R_vf], [R_x[c]],
                    bias=VF[:, lnb + c: lnb + c + 1], scale=VF[:, lng + c: lng + c + 1])
                if h_out:
                    act(hT[:, c, 0:T], tmpb[i][:, 0:T], AF.Identity, [R_tmp[i], R_tabs], [R_h[c]],
                        bias=ft2[:, c:c + 1], scale=ft1[:, c:c + 1])

        def attention(l, g, T, pairs, tiles_of_pair, is_A):
            QT = actT
            OT = hT
            groups = []
            for pi, (q0, nq) in enumerate(pairs):
                kts = tiles_of_pair(pi)
                for jg in range(2):
                    for par in range(2):
                        for ti, kt in enumerate(kts):
                            groups.append((pi, q0, nq, jg, par, ti, len(kts), kt))
            nG = len(groups)
            state = {}
            pjcount = [0]

            def rec_S(gi):
                pi, q0, nq, jg, par, ti, nkt, kt = groups[gi]
                b = (S0, S1, D0, D1)[gi % 4]
                nk = kt["nk"]
                for jj in range(4):
                    j = jg * 4 + jj
                    mm(PS[b][0:max(nk, 32), jj * nq:(jj + 1) * nq], kt["kT"](par, j),
                       QT[par * 64:(par + 1) * 64, j, q0:q0 + nq],
                       True, True, kt["kres"] + [R_act[j]], [R_ps[b]])
                ty = kt["type"]
                if ty == "m0":
                    ebuf, eres = E0buf, R_e0
                else:
                    ebuf, eres = Ebuf[gi % 5], R_e[gi % 5]
                ev = ebuf[:, 0:4 * nq].rearrange("p (j q) -> p j q", j=4)
                sv = PS[b][:, 0:4 * nq].rearrange("p (j q) -> p j q", j=4)
                if ty == "m0":
                    n1 = min(nq, 64)
                    act(ev[0:nk, :, 0:n1], sv[0:nk, :, 0:n1], AF.Exp, [R_ps[b]], [eres], scale=SCALE)
                    if nq > 64:
                        act(ev[64:128, :, 64:nq], sv[64:128, :, 64:nq], AF.Exp, [R_ps[b]], [eres], scale=SCALE)
                else:
                    act(ebuf[0:nk, 0:4 * nq], PS[b][0:nk, 0:4 * nq], AF.Exp, [R_ps[b]], [eres], scale=SCALE)
                if ty in ("e0", "e1"):
                    t_i = 0 if ty == "e0" else 1
                    EB, R_eb = (EBA, R_eba) if is_A else (EBB, R_ebb)
                    eng = "dve" if gi % 4 != 3 else "pool"
                    ttn(eng, ev[0:nk, :, :], ev[0:nk, :, :], EB[0:nk, t_i, par, jg * 4:(jg + 1) * 4, 0:nq], ALU.mult,
                        [eres, R_eb], [eres])
                state[gi] = (ebuf, eres)

            def rec_DP(gi):
                pi, q0, nq, jg, par, ti, nkt, kt = groups[gi]
                ebuf, eres = state.pop(gi)
                nk = kt["nk"]
                pj = pjcount[0]
                ob = O0 + (pj % 2)
                nb = N0 + (pj % 2)
                first = ti == 0
                last = ti == nkt - 1
                mm(PS[nb][par * 64:(par + 1) * 64, 0:4 * nq], ones_bf[0:nk, :], ebuf[0:nk, 0:4 * nq], first, last,
                   [R_const, eres], [R_ps[nb]])
                for jj in range(4):
                    j = jg * 4 + jj
                    mm(PS[ob][par * 64:(par + 1) * 64, jj * nq:(jj + 1) * nq], kt["v"](par, j),
                       ebuf[0:nk, jj * nq:(jj + 1) * nq], first and jj == 0, last and jj == 3,
                       kt["vres"] + [eres], [R_ps[ob]], skip=True)
                if last and par == 1:
                    n4 = 4 * nq
                    if is_A:
                        ttn("dve", Tbuf[:, 0:n4].rearrange("p (j q) -> p j q", j=4),
                            PS[nb][:, 0:n4].rearrange("p (j q) -> p j q", j=4),
                            esink[:, l, jg * 4:(jg + 1) * 4].unsqueeze(2).to_broadcast([128, 4, nq]), ALU.add,
                            [R_ps[nb], R_esink], [R_t])
                        P.emit("dve", lambda e: e.reciprocal(R2buf[:, 0:n4], Tbuf[:, 0:n4]), [R_t], [R_r2])
                    else:
                        P.emit("dve", lambda e: e.reciprocal(R2buf[:, 0:n4], PS[nb][:, 0:n4]), [R_ps[nb]], [R_r2])
                    ttn("dve", OT[:, jg * 4:(jg + 1) * 4, q0:q0 + nq],
                        PS[ob][:, 0:n4].rearrange("p (j q) -> p j q", j=4),
                        R2buf[:, 0:n4].rearrange("p (j q) -> p j q", j=4), ALU.mult,
                        [R_ps[ob], R_r2], R_h[jg * 4:(jg + 1) * 4])
                    pjcount[0] += 1

            for gi in range(nG + 3):
                if gi < nG:
                    rec_S(gi)
                if gi >= 3:
                    rec_DP(gi - 3)

        def ffn(l, g, T, last_out):
            cur = None
            pending_gelu = []
            for oi, ch in enumerate(UP_ORDER):
                if oi % 4 == 0:
                    cur = load_w8(sc_up[l][oi // 4], ("up", l, oi // 4))
                wv = w8v(cur)
                b = next_d()
                for kc in range(8):
                    mm(PS[b][:, 0:T], wv[:, kc, (oi % 4) * 128:(oi % 4 + 1) * 128], hT[:, kc, 0:T], kc == 0, kc == 7,
                       [R_w8[cur], R_h[kc]], [R_ps[b]], last_reader=(kc == 7 and oi % 4 == 3))
                isg = ch >= FC
                i = ch - FC if isg else ch
                bq = oi % 4
                U = Ubuf[bq // 2][bq % 2]
                Y = Ybuf[bq // 2][bq % 2]
                RU = R_u[bq // 2][bq % 2]
                RUH = R_uh[bq // 2][bq % 2]
                RY = R_y[bq // 2][bq % 2]
                w0 = VF[:, V_CW + (l * 3 + 0) * NCH + ch: V_CW + (l * 3 + 0) * NCH + ch + 1]
                w1 = VF[:, V_CW + (l * 3 + 1) * NCH + ch: V_CW + (l * 3 + 1) * NCH + ch + 1]
                w2 = VF[:, V_CW + (l * 3 + 2) * NCH + ch: V_CW + (l * 3 + 2) * NCH + ch + 1]
                cb = VF[:, V_CB + l * NCH + ch: V_CB + l * NCH + ch + 1]
                cpy("act", U[:, 0:2], carry[:, l, ch, :], [R_carry[l][ch]], [RUH])
                cpy("act", U[:, 2:2 + T], PS[b][:, 0:T], [R_ps[b]], [RU])
                cpy("act", carry[:, l, ch, :], PS[b][:, T - 2:T], [R_ps[b]], [R_carry[l][ch]])
                if isg:
                    tsc("dve", Y[:, 0:T], PS[b][:, 0:T], w2, cb, ALU.mult, ALU.add, [R_ps[b], R_vf], [RY])
                    stt("dve", Y[:, 0:T], U[:, 1:1 + T], w1, Y[:, 0:T], ALU.mult, ALU.add, [RU, RUH, RY, R_vf], [RY])
                    stt("dve", Y[:, 0:T], U[:, 0:T], w0, Y[:, 0:T], ALU.mult, ALU.add, [RU, RUH, RY, R_vf], [RY])
                    if pending_gelu:
                        pending_gelu.pop()()
                    pending_gelu.append(lambda Y=Y, RY=RY, i=i: act(actT[:, i, 0:T], Y[:, 0:T], AF.Gelu_apprx_tanh,
                                                                    [RY], [R_act[i]]))
                else:
                    if pending_gelu:
                        pending_gelu.pop()()
                    act(Y[:, 0:T], PS[b][:, 0:T], AF.Identity, [R_ps[b], R_vf], [RY], bias=cb, scale=w2)
                    stt("dve", Y[:, 0:T], U[:, 1:1 + T], w1, Y[:, 0:T], ALU.mult, ALU.add, [RU, RUH, RY, R_vf], [RY])
                    stt("dve", Y[:, 0:T], U[:, 0:T], w0, Y[:, 0:T], ALU.mult, ALU.add, [RU, RUH, RY, R_vf], [RY])
                    ttn("pool", actT[:, i, 0:T], Y[:, 0:T], actT[:, i, 0:T], ALU.mult, [RY, R_act[i]], [R_act[i]])
            if last_out is not None:
                for r in range(2):
                    dst = bass.AP(last_out.tensor, l * 2 * 2 * DFF + r * 2 * DFF, [[1, 128], [128, NCH]])
                    o = dma("sp", dst, carry[:, l, :, r], [R_carry[l][c] for c in range(NCH)], [Res()], ch_misc,
                            slow=True)
                    R_out.append(o)
            for m in range(8):
                wi = load_w22(sc_dn[l][m], ("dn", l, m))
                wv = W22[wi][:].rearrange("p (kc c) -> p kc c", kc=22)
                b = next_d()
                for kc in range(FC):
                    mm(PS[b][:, 0:T], wv[:, kc, :], actT[:, kc, 0:T], kc == 0, kc == FC - 1,
                       [R_w22[wi], R_act[kc]], [R_ps[b]], last_reader=(kc == FC - 1))
                gate = ADAG[:, g, tab(g, l, 5, m): tab(g, l, 5, m) + 1]
                stt("dve", xT[:, m, 0:T], PS[b][:, 0:T], gate, xT[:, m, 0:T], ALU.mult, ALU.add,
                    [R_ps[b], R_x[m], R_tabs], [R_x[m]])

        def tok_major(src_chunks_res, T, w_slot, col0, ncols, dst_fn, f32_out_fn):
            wv = w8v(w_slot)
            ngr = (T + 127) // 128
            for gi in range(ngr):
                nr = min(128, T - gi * 128)
                mr = max(nr, 32)
                b = next_d()
                for kc in range(8):
                    mm(PS[b][0:mr, 0:ncols], hT[:, kc, gi * 128: gi * 128 + mr], wv[:, kc, col0:col0 + ncols],
                       kc == 0, kc == 7, [R_h[kc], R_w8[w_slot]], [R_ps[b]])
                if dst_fn is not None:
                    dst_fn(gi, nr, b)
                if f32_out_fn is not None:
                    f32_out_fn(gi, nr, b)

        def run_tile(g, T, t, xsrc, ydst, is_last, outs):
            sample = g == 1
            npairs = (T + 127) // 128
            pairs = [(p * 128, min(128, T - p * 128)) for p in range(npairs)]
            cur_half = 1 if sample else t % 2
            prev_half = 1 - cur_half
            import os as _os2
            KSUB = int(_os2.environ.get("KSUB", "99"))
            x_in(xsrc, T, g)
            h_from_x(g, 0, T)
            if KSUB <= 1:
                return
            for l in range(4):
                is_A = l < 2
                if is_A:
                    la = l
                    def evq(m, b):
                        eng = "act" if m % 2 == 0 else "dve"
                        cpy(eng, actT[:, m, 0:T], PS[b][:, 0:T], [R_ps[b]], [R_act[m]])
                    proj_fm([("qa", la, 0), ("qa", la, 1)], [sc_qa[la][0], sc_qa[la][1]], 8,
                            lambda kc: hT[:, kc, 0:T], R_h, 8, T, evq)
                    KS2 = int(_os2.environ.get("KSUB2", "99"))
                    if KS2 <= 1:
                        return
                    kslot = load_w8(sc_qa[la][2], ("qa", la, 2))
                    wv = w8v(kslot)
                    for g4 in range(4):
                        b = next_d()
                        for kc in range(8):
                            mm(PS[b][:, 0:T], wv[:, kc, g4 * 128:(g4 + 1) * 128], hT[:, kc, 0:T], kc == 0, kc == 7,
                               [R_w8[kslot], R_h[kc]], [R_ps[b]])
                        eng = "act" if g4 % 2 == 0 else "dve"
                        cpy(eng, KA[la][:, g4, 128:128 + T], PS[b][:, 0:T], [R_ps[b]], R_ka[la][1:5])
                    if KS2 <= 2:
                        return
                    kvslot = load_w8(sc_qa[la][3], ("qa", la, 3))
                    KS3 = int(_os2.environ.get("KS3", "99"))
                    if KS3 <= 1:
                        return
                    if KS3 <= 2:
                        tok_major(R_h, T, kvslot, 256, 256, None, None)
                        return

                    def vdst(gi, nr, b):
                        if _os2.environ.get("NOVD3") and gi == 3 and not sample:
                            return
                        cpy("act" if nr == 128 else "dve", VA[la][0:nr, 1 + gi, :], PS[b][0:nr, 0:256], [R_ps[b]],
                            [R_va[la][1 + gi]])
                    want_out = (is_last or sample) and _os2.environ.get("NOWANT", "0") != "1"
                    if want_out:
                        okd, ovd = outs["ak"], outs["av"]

                        def vout(gi, nr, b):
                            if sample or gi == 3:
                                si = 0
                                if _os2.environ.get("ALT"):
                                    if _os2.environ.get("ALT") == "2":
                                        cpy("dve", R2buf[0:nr, 0:256], Tbuf[0:nr, 0:256], [R_ps[b]], [R_r2])
                                    elif _os2.environ.get("ALT") == "3":
                                        cpy("dve", R2buf[0:nr, 0:256], PS[b][0:nr, 0:256], [R_ps[b]], [R_r2])
                                        cpy("dve", Tbuf[0:nr, 0:256], PS[b][0:nr, 0:256], [R_ps[b]], [R_t])
                                    else:
                                        tsc("dve", R2buf[0:nr, 0:256], PS[b][0:nr, 0:256], 1.0, None, ALU.mult, None, [R_ps[b]], [R_r2])
                                else:
                                    cpy(_os2.environ.get("VENG2", "dve"), stage[si][0:nr, 0:256], PS[b][0:nr, 0:256], [R_ps[b]], [R_stage[si]])
                                o = dma("sp", ovd[la][0:nr, :], stage[si][0:nr, 0:256], [R_stage[si]], [Res()],
                                        ch_stage[si])
                                R_out.append(o)

                        def kout(gi, nr, b):
                            if sample or gi == 3:
                                si = 1
                                cpy(_os2.environ.get("VENG2", "dve"), stage[si][0:nr, 0:256], PS[b][0:nr, 0:256], [R_ps[b]], [R_stage[si]])
                                o = dma("sp", okd[la][0:nr, :], stage[si][0:nr, 0:256], [R_stage[si]], [Res()],
                                        ch_stage[si])
                                R_out.append(o)
                        tok_major(R_h, T, kvslot, 256, 256, vdst, None if _os2.environ.get("NOV") else vout)
                        if not _os2.environ.get("NOK"):
                            tok_major(R_h, T, kvslot, 0, 256, None, kout)
                    else:
                        tok_major(R_h, T, kvslot, 256, 256, vdst, None)

                    if KSUB <= 2:
                        return
                    def tiles_of_pair(pi, la=la):
                        res = []
                        for ty, ti in (("e0", pi), ("e1", pi + 1)):
                            if ti == 0 and (not sample) and t == 0:
                                continue
                            nk = 128
                            if ti >= 1:
                                nk = min(128, T - (ti - 1) * 128)
                            res.append({
                                "nk": nk, "type": ty,
                                "kT": (lambda par, j, ti=ti, nk=nk: KA[la][par * 64:(par + 1) * 64, j // 2,
                                                                            ti * 128: ti * 128 + max(nk, 32)]),
                                "kres": [R_ka[la][ti]],
                                "v": (lambda par, j, ti=ti, nk=nk: VA[la][0:nk, ti, (j // 2) * 64:(j // 2) * 64 + 64]),
                                "vres": [R_va[la][ti]],
                            })
                        return res
                    attention(la, g, T, pairs, tiles_of_pair, True)
                    if KSUB <= 3:
                        return
                    if not sample:
                        cpy("pool", KA[la][:, :, 0:128], KA[la][:, :, 512:640], [R_ka[la][4]], [R_ka[la][0]])
                        cpy("pool", VA[la][:, 0, :], VA[la][:, 4, :], [R_va[la][4]], [R_va[la][0]])
                    wo_keys = [("oa", la, 0), ("oa", la, 1)]
                    wo_scr = [sc_oa[la][0], sc_oa[la][1]]
                else:
                    jb = l - 2
                    dma("sp", EBB[:].rearrange("p a b c d -> p (a b c d)"), sc_ebb[jb][:, :],
                        [scr_res(("ebb", jb))], [R_ebb], ch_ebb)

                    def evq(m, b):
                        eng = "act" if m % 2 == 0 else "dve"
                        cpy(eng, actT[:, m, 0:T], PS[b][:, 0:T], [R_ps[b]], [R_act[m]])
                    proj_fm([("qb", jb, 0), ("qb", jb, 1)], [sc_qb[jb][0], sc_qb[jb][1]], 8,
                            lambda kc: hT[:, kc, 0:T], R_h, 8, T, evq)

                    def tiles_of_pair(pi):
                        res = []
                        for k5 in range(5):
                            ti = pi + k5
                            if ti < 4 and (not sample) and t == 0:
                                continue
                            half = prev_half if ti < 4 else cur_half
                            tl = ti % 4
                            nk = 128 if ti < 4 else min(128, T - tl * 128)
                            ty = ("m0", "n", "n", "e0", "e1")[k5]
                            res.append({
                                "nk": nk, "type": ty,
                                "kT": (lambda par, j, half=half, tl=tl, nk=nk:
                                       KB[par * 64:(par + 1) * 64, half, j, tl * 128: tl * 128 + max(nk, 32)]),
                                "kres": [R_kb[half][tl]],
                                "v": (lambda par, j, half=half, tl=tl, nk=nk:
                                      VB[0:nk, half, tl, (2 * j + par) * 64:(2 * j + par) * 64 + 64]),
                                "vres": [R_vb[half][tl]],
                            })
                        return res
                    attention(l, g, T, pairs, tiles_of_pair, False)
                    wo_keys = [("ob", jb, 0), ("ob", jb, 1)]
                    wo_scr = [sc_ob[jb][0], sc_ob[jb][1]]

                def evo(m, b, l=l):
                    gate = ADAG[:, g, tab(g, l, 2, m): tab(g, l, 2, m) + 1]
                    stt("dve", xT[:, m, 0:T], PS[b][:, 0:T], gate, xT[:, m, 0:T], ALU.mult, ALU.add,
                        [R_ps[b], R_x[m], R_tabs], [R_x[m]])
                proj_fm(wo_keys, wo_scr, 8, lambda kc: hT[:, kc, 0:T], R_h, 8, T, evo)
                layer_norm(g, l, 0, T, FT[:, g, l, 0, 0, :], FT[:, g, l, 0, 1, :], True)
                if KSUB <= 4:
                    return
                conv_out = outs["conv"] if (is_last or sample) else None
                ffn(l, g, T, conv_out)
                if KSUB <= 5:
                    return
                if l == 1:
                    layer_norm(g, l, 1, T, FKV[:, g, 0, :], FKV[:, g, 1, :], True)
                    def evk(m, b):
                        eng = "act" if m % 2 == 0 else "dve"
                        if T >= 128:
                            cpy(eng, KB[:, cur_half, m, 0:T], PS[b][:, 0:T], [R_ps[b]], R_kb[cur_half])
                        else:
                            cpy(eng, KB[:, cur_half, m, 0:T], PS[b][:, 0:T], [R_ps[b]], [R_kb[cur_half][0]])
                    proj_fm([("kvb", 0), ("kvb", 1)], [sc_kvb[0], sc_kvb[1]], 8,
                            lambda kc: hT[:, kc, 0:T], R_h, 8, T, evk)
                    want_out = is_last or sample
                    for hv in range(2):
                        vslot = load_w8(sc_kvb[2 + hv], ("kvb", 2 + hv))

                        def vdst(gi, nr, b, hv=hv):
                            cpy("act" if nr == 128 else "dve", VB[0:nr, cur_half, gi, hv * 512:(hv + 1) * 512],
                                PS[b][0:nr, 0:512], [R_ps[b]], [R_vb[cur_half][gi]])

                        def vout(gi, nr, b, hv=hv):
                            si = gi % 2
                            cpy("dve", stage[si][0:nr, 0:512], PS[b][0:nr, 0:512], [R_ps[b]], [R_stage[si]])
                            o = dma("sp", outs["bv"][gi * 128: gi * 128 + nr, hv * 512:(hv + 1) * 512],
                                    stage[si][0:nr, 0:512], [R_stage[si]], [Res()], ch_stage[si])
                            R_out.append(o)
                        tok_major(R_h, T, vslot, 0, 512, vdst, vout if want_out else None)
                    if want_out:
                        for hk in range(2):
                            kslot = load_w8(sc_kvb[hk], ("kvb", hk))

                            def kout(gi, nr, b, hk=hk):
                                si = gi % 2
                                cpy("dve", stage[si][0:nr, 0:512], PS[b][0:nr, 0:512], [R_ps[b]], [R_stage[si]])
                                o = dma("sp", outs["bk"][gi * 128: gi * 128 + nr, hk * 512:(hk + 1) * 512],
                                        stage[si][0:nr, 0:512], [R_stage[si]], [Res()], ch_stage[si])
                                R_out.append(o)
                            tok_major(R_h, T, kslot, 0, 512, None, kout)
                    h_from_x(g, 2, T)
                elif l < 3:
                    layer_norm(g, l, 1, T, FT[:, g, l, 1, 0, :], FT[:, g, l, 1, 1, :], True)
                else:
                    layer_norm(g, l, 1, T, None, None, False)
            ngr = (T + 127) // 128
            for gi in range(ngr):
                nr = min(128, T - gi * 128)
                si = gi % 2
                for half in range(2):
                    b = next_d()
                    for cc in range(4):
                        c = half * 4 + cc
                        tr(PS[b][0:nr, cc * 128:(cc + 1) * 128], xT[:, c, gi * 128: gi * 128 + nr], ident[:],
                           [R_x[c], R_const], [R_ps[b]])
                    eng = "act" if half == 0 else "dve"
                    cpy(eng, stage[si][0:nr, half * 512:(half + 1) * 512], PS[b][0:nr, 0:512], [R_ps[b]], [R_stage[si]])
                o = dma("sp", ydst[gi * 128: gi * 128 + nr, :], stage[si][0:nr, :], [R_stage[si]], [Res()], ch_stage[si])
                R_out.append(o)

        def load_sample_caches():
            for la in range(2):
                st = stage[0]
                stv = st[:, 0:512].rearrange("p (g u d) -> p g u d", g=4, u=2)
                for dup in range(2):
                    dma("sp", stv[:, :, dup, :], cache_a_k[la].rearrange("k (g d) -> k g d", g=4), (), [R_stage[0]],
                        ch_stage[0])
                for g4 in range(4):
                    b = next_d()
                    tr(PS[b][:, 0:128], st[:, g4 * 128:(g4 + 1) * 128], ident[:], [R_stage[0], R_const], [R_ps[b]])
                    cpy("dve", KA[la][:, g4, 0:128], PS[b][:, 0:128], [R_ps[b]], [R_ka[la][0]])
                dma("sp", stage[1][:, 0:256], cache_a_v[la], (), [R_stage[1]], ch_stage[1])
                cpy("dve", VA[la][:, 0, :], stage[1][:, 0:256], [R_stage[1]], [R_va[la][0]])
            for tl in range(4):
                si = tl % 2
                dma("sp", stage[si][:, :], cache_b_k[tl * 128:(tl + 1) * 128, :], (), [R_stage[si]], ch_stage[si])
                for half in range(2):
                    b = next_d()
                    for cc in range(4):
                        c = half * 4 + cc
                        tr(PS[b][:, cc * 128:(cc + 1) * 128], stage[si][:, c * 128:(c + 1) * 128], ident[:],
                           [R_stage[si], R_const], [R_ps[b]])
                    cpy("dve" if half else "act", KB[:, 0, half * 4:(half + 1) * 4, tl * 128:(tl + 1) * 128],
                        PS[b][:, :].rearrange("p (c t) -> p c t", c=4), [R_ps[b]], [R_kb[0][tl]])
            for tl in range(4):
                si = tl % 2
                dma("sp", stage[si][:, :], cache_b_v[tl * 128:(tl + 1) * 128, :], (), [R_stage[si]], ch_stage[si])
                cpy("dve", VB[:, 0, tl, :], stage[si][:, :], [R_stage[si]], [R_vb[0][tl]])
            for l in range(4):
                for r in range(2):
                    src = bass.AP(state_conv.tensor, l * 2 * 2 * DFF + r * 2 * DFF, [[1, 128], [128, NCH]])
                    dma("sp", carry[:, l, :, r], src, (), [R_carry[l][c] for c in range(NCH)], ch_misc, slow=True)

        import os as _os
        KSTOP = int(_os.environ.get("KSTOP", "99"))
        if KSTOP >= 1:
            emit_casts()
        if KSTOP >= 2:
            setup()
        P.epoch = 1
        if KSTOP >= 3:
            load_sample_caches()
        if KSTOP >= 4:
            run_tile(1, DEC, -1, x_sample, y_sample, False,
                     {"ak": s_ak, "av": s_av, "bk": s_bk, "bv": s_bv, "conv": s_conv})
            mset("pool", carry[:], 0.0, [R_carry[l][c] for l in range(4) for c in range(NCH)])
        for t in range(NT if KSTOP >= 5 else 0):
            P.epoch = 2 + t
            run_tile(0, TT, t, x_prompt[t * TT:(t + 1) * TT, :], y_prompt[t * TT:(t + 1) * TT, :], t == NT - 1,
                     {"ak": p_ak, "av": p_av, "bk": p_bk, "bv": p_bv, "conv": p_conv})
        n_epochs = NT + 2

        eng_sems = {}
        for e in ENGS:
            users = set()
            for op in P.ops[e]:
                if op.dma is None:
                    users |= op.sigs
            for c in sorted(users):
                eng_sems[(e, c)] = new_sem(f"s_{e}_{c}")
        for e in ENGS:
            cnt = {}
            for op in P.ops[e]:
                if op.dma is None:
                    for c in op.sigs:
                        cnt[c] = cnt.get(c, 0) + 1
                        op.cnts[c] = cnt[c]
            print("SEMCNT", e, cnt)
            for op in P.ops[e]:
                if op.dma is None and len(op.sigs) > 1:
                    print("MULTISIG", e, op.pos, op.epoch, op.sigs)
                    break
        out_ops = list(R_out)

        if _os.environ.get("DUMPW"):
            for en in ENGS:
                for op in P.ops[en][-6:]:
                    print("TAIL", en, op.pos, op.epoch, "sig", op.sig, op.cnt, "dma", None if op.dma is None else op.dmaval,
                          [(d.eng, d.epoch, d.pos, d.cnt if d.dma is None else ("dma", d.dmaval)) for d in op.waits])

        def replay(eng_name, e):
            for op in P.ops[eng_name]:
                for d in op.waits:
                    if d.dma is not None:
                        e.wait_ge(d.dma.sem, d.dmaval)
                    else:
                        e.wait_ge(eng_sems[(d.eng, eng_name)], d.cnts[eng_name])
                if op.fn is None:
                    continue
                ins = op.fn(e)
                if op.dma is not None:
                    ins.then_inc(op.dma.sem, 16)
                else:
                    for c in sorted(op.sigs):
                        ins.then_inc(eng_sems[(op.eng, c)], 1)
            if eng_name == "sp":
                done = {}
                for o in out_ops:
                    if o.dma is None:
                        continue
                    key = id(o.dma)
                    if key not in done or done[key][1] < o.dmaval:
                        done[key] = (o.dma, o.dmaval)
                for chn, val in done.values():
                    e.wait_ge(chn.sem, val)

        with nc.Block() as block:
            @block.tensor
            def _(e):
                replay("pe", e)

            @block.scalar
            def _(e):
                replay("act", e)

            @block.vector
            def _(e):
                replay("dve", e)

            @block.gpsimd
            def _(e):
                replay("pool", e)

            @block.sync
            def _(e):
                replay("sp", e)
    stats = {e: len(P.ops[e]) for e in ENGS}
    print("SBUF remaining", nc.sbuf_bytes_remaining)
    return nc, stats


_CACHE = {}


def _host_inputs(inputs, NT):
    f = lambda a: np.ascontiguousarray(np.asarray(a, dtype=np.float32))
    shared = {}
    for k in ("w_ada", "w_qkv_a", "w_o_a", "sinks_a", "t5_table", "w_ada_kv", "w_kv_b", "w_q_b", "w_o_b",
              "relpos_b", "w_up", "w_down"):
        shared[k] = f(inputs[k])
    shared["ident"] = np.eye(128, dtype=np.float32)
    shared["anti"] = np.ascontiguousarray(np.eye(128, dtype=np.float32)[::-1])
    shared["t5oh"] = _t5_onehot()
    in_maps = []
    for b in range(8):
        vec = np.zeros((V_ROWS, 128), np.float32)
        vec[V_BADA:V_BADA + 192] = f(inputs["b_ada"]).reshape(192, 128)
        vec[V_BKV:V_BKV + 16] = f(inputs["b_ada_kv"]).reshape(16, 128)
        vec[V_LNG:V_LNG + 64] = f(inputs["ln_g"]).reshape(64, 128)
        vec[V_LNB:V_LNB + 64] = f(inputs["ln_b"]).reshape(64, 128)
        vec[V_CW:V_CW + 528] = f(inputs["conv_w"]).reshape(528, 128)
        vec[V_CB:V_CB + 176] = f(inputs["conv_b"]).reshape(176, 128)
        vec[V_C:V_C + 8] = f(inputs["c_prompt"])[b].reshape(8, 128)
        vec[V_C + 8:V_C + 16] = f(inputs["c_sample"])[b].reshape(8, 128)
        m = dict(shared)
        m["vecs"] = vec
        m["x_prompt"] = f(inputs["x_prompt"][b])
        m["x_sample"] = f(inputs["x_sample"][b])
        m["cache_a_k"] = f(inputs["cache_a_k"][:, b]).reshape(2, 128, 256)
        m["cache_a_v"] = f(inputs["cache_a_v"][:, b]).reshape(2, 128, 256)
        m["cache_b_k"] = f(inputs["cache_b_k"][b]).reshape(512, 1024)
        m["cache_b_v"] = f(inputs["cache_b_v"][b]).reshape(512, 1024)
        m["state_conv"] = f(inputs["state_conv"][:, b])
        in_maps.append(m)
    return in_maps


def kernel(**inputs):
    S = int(np.asarray(inputs["x_prompt"]).shape[1])
    NT = S // TT
    if NT not in _CACHE:
        _CACHE[NT] = build(NT)[0]
    nc = _CACHE[NT]
    in_maps = _host_inputs(inputs, NT)
    res = run_bass_kernel_spmd(nc, in_maps, core_ids=list(range(8)))
    R = res.results
    st = lambda k: np.stack([np.asarray(r[k], dtype=np.float32) for r in R], axis=0)
    y_prompt = st("y_prompt")
    y_sample = st("y_sample")
    rows_a = min(128, S)
    rows_b = min(512, S)
    p_ak = np.ascontiguousarray(st("p_ak").transpose(1, 0, 2, 3)).reshape(2, 8, 128, 4, 64)[:, :, 128 - rows_a:]
    p_av = np.ascontiguousarray(st("p_av").transpose(1, 0, 2, 3)).reshape(2, 8, 128, 4, 64)[:, :, 128 - rows_a:]
    p_bk = st("p_bk").reshape(8, 512, 16, 64)[:, 512 - rows_b:]
    p_bv = st("p_bv").reshape(8, 512, 16, 64)[:, 512 - rows_b:]
    p_conv = np.ascontiguousarray(st("p_conv").transpose(1, 0, 2, 3))
    s_ak = np.ascontiguousarray(st("s_ak").transpose(1, 0, 2, 3)).reshape(2, 8, DEC, 4, 64)
    s_av = np.ascontiguousarray(st("s_av").transpose(1, 0, 2, 3)).reshape(2, 8, DEC, 4, 64)
    s_bk = st("s_bk").reshape(8, DEC, 16, 64)
    s_bv = st("s_bv").reshape(8, DEC, 16, 64)
    s_conv = np.ascontiguousarray(st("s_conv").transpose(1, 0, 2, 3))
    return (y_prompt, y_sample, p_ak, p_av, p_bk, p_bv, p_conv, s_ak, s_av, s_bk, s_bv, s_conv)
```

```python
import math
import contextlib
import numpy as np
import concourse.bass as bass
import concourse.mybir as mybir
from concourse.bass_utils import run_bass_kernel_spmd

F32 = mybir.dt.float32
BF16 = mybir.dt.bfloat16
AF = mybir.ActivationFunctionType
ALU = mybir.AluOpType

D = 1024
C8 = 8
TT = 512
DFF = 2816
FC = 22
NCH = 44
DEPTH = 4
DEC = 16
ALPHA = (2.0 * DEPTH) ** 0.25
LN_EPS = 1e-5
EPS_P = LN_EPS / (ALPHA * ALPHA)
SCALE = 64 ** -0.5
UP_ORDER = list(range(22, 44)) + list(range(0, 22))

V_BADA = 0
V_BKV = 192
V_LNG = 208
V_LNB = 272
V_CW = 336
V_CB = 864
V_C = 1040
V_ROWS = 1152


class Res:
    __slots__ = ("w", "r", "rd")

    def __init__(self):
        self.w = None
        self.r = {}
        self.rd = []


class Chan:
    def __init__(self, sem):
        self.sem = sem
        self.count = 0


class Op:
    __slots__ = ("eng", "fn", "waits", "sig", "pos", "cnt", "dma", "dmaval", "epoch", "sigs", "cnts", "waiter")


ENGS = ["pe", "act", "dve", "pool", "sp"]
import os as _osm
MAXSIG = int(_osm.environ.get("MAXSIG", "1"))


class Prog:
    def __init__(self):
        self.ops = {e: [] for e in ENGS}
        self.seen = {e: {} for e in ENGS}
        self.epoch = 0

    def emit(self, eng, fn, reads=(), writes=(), chan=None):
        op = Op()
        op.eng = eng
        op.fn = fn
        op.pos = len(self.ops[eng])
        op.sig = False
        op.sigs = set()
        op.cnts = {}
        op.waiter = {}
        op.dma = chan
        op.epoch = self.epoch
        op.cnt = 0
        if chan is not None:
            chan.count += 16
            op.dmaval = chan.count
        deps = []
        for r in reads:
            if r.w is not None:
                deps.append((r.w, True))
        for w in writes:
            if w.w is not None:
                deps.append((w.w, False))
            for rd in w.r.values():
                deps.append((rd, False))
            for rd in w.rd:
                deps.append((rd, False))
        waits = []
        seen = self.seen[eng]
        for d, raw in deps:
            if d is op:
                continue
            if d.dma is not None:
                key = ("dma", id(d.dma))
                if seen.get(key, -1) >= d.dmaval:
                    continue
                seen[key] = d.dmaval
                waits.append(d)
                continue
            if d.eng == eng:
                if eng == "pe":
                    continue
            hops = 0
            while d.dma is None and eng not in d.sigs and len(d.sigs) >= MAXSIG:
                plst = self.ops[d.eng]
                zf = None
                for k in range(d.pos + 1, min(len(plst), d.pos + 13)):
                    z = plst[k]
                    if z.dma is None and z.fn is not None and (eng in z.sigs or len(z.sigs) == 0):
                        zf = z
                        break
                if zf is not None and zf.eng != eng:
                    d = zf
                    break
                c0 = sorted(d.sigs)[0]
                w0 = d.waiter[c0]
                hops += 1
                if w0.eng == eng or w0.dma is not None:
                    d = w0
                    break
                lst = self.ops[c0]
                found = None
                for k in range(w0.pos, min(len(lst), w0.pos + 400)):
                    z = lst[k]
                    if z.dma is None and z.fn is not None and (eng in z.sigs or len(z.sigs) == 0):
                        found = z
                        break
                if found is not None:
                    d = found
                    break
                d = lst[min(len(lst), w0.pos + 400) - 1]
                if d.dma is not None or d.fn is None:
                    d = w0
            if d.dma is not None:
                key = ("dma", id(d.dma))
                if seen.get(key, -1) >= d.dmaval:
                    continue
                seen[key] = d.dmaval
                waits.append(d)
                continue
            if d.eng == eng and (eng == "pe" or d is op):
                continue
            if d.eng == eng and hops > 0:
                continue
            key = d.eng
            if seen.get(key, -1) >= d.pos:
                continue
            seen[key] = d.pos
            d.sig = True
            d.sigs.add(eng)
            if eng not in d.waiter:
                d.waiter[eng] = op
            waits.append(d)
        op.waits = waits
        for r in reads:
            if chan is not None:
                r.rd.append(op)
            else:
                r.r[eng] = op
        for w in writes:
            w.w = op
            w.r = {}
            w.rd = []
        self.ops[eng].append(op)
        return op


def _t5_onehot():
    try:
        import jax
        import jax.numpy as jnp
        cpu = jax.devices("cpu")[0]
        with jax.default_device(cpu):
            rel = jnp.asarray(np.concatenate([(-128 + 127 - np.arange(255)), (127 - np.arange(255))]).astype(np.int32))
            nb = 16
            max_exact = 8
            n = jnp.abs(rel)
            large = max_exact + (jnp.log(jnp.maximum(n, 1).astype(jnp.float32) / max_exact)
                                 / math.log(128 / max_exact) * (nb - max_exact)).astype(jnp.int32)
            large = jnp.minimum(large, nb - 1)
            bucket = jnp.where(rel > 0, nb, 0) + jnp.where(n < max_exact, n, large)
            bucket = np.asarray(bucket)
    except Exception:
        rel = np.concatenate([(-128 + 127 - np.arange(255)), (127 - np.arange(255))]).astype(np.int32)
        n = np.abs(rel)
        large = 8 + (np.log(np.maximum(n, 1).astype(np.float32) / np.float32(8))
                     / np.float32(math.log(16.0)) * np.float32(8)).astype(np.int32)
        large = np.minimum(large, 15)
        bucket = np.where(rel > 0, 16, 0) + np.where(n < 8, n, large)
    oh = np.zeros((32, 510), np.float32)
    oh[bucket, np.arange(510)] = 1.0
    return oh


def build(NT):
    S = NT * TT
    nc = bass.Bass("TRN2", target_bir_lowering=False)
    P = Prog()

    def din(name, shape, dt=F32):
        return nc.dram_tensor(name, list(shape), dt, kind="ExternalInput").ap()

    def dout(name, shape, dt=F32):
        return nc.dram_tensor(name, list(shape), dt, kind="ExternalOutput").ap()

    def dscr(name, shape, dt=BF16):
        return nc.dram_tensor(name, list(shape), dt, kind="Internal").ap()

    x_prompt = din("x_prompt", [S, D])
    x_sample = din("x_sample", [DEC, D])
    cache_a_k = din("cache_a_k", [2, 128, 256])
    cache_a_v = din("cache_a_v", [2, 128, 256])
    cache_b_k = din("cache_b_k", [512, 1024])
    cache_b_v = din("cache_b_v", [512, 1024])
    state_conv = din("state_conv", [4, 2, 2 * DFF])
    w_ada = din("w_ada", [4, D, 6 * D])
    w_qkv_a = din("w_qkv_a", [2, D, 1536])
    w_o_a = din("w_o_a", [2, D, D])
    sinks_a = din("sinks_a", [2, 16])
    t5_table = din("t5_table", [32, 16])
    w_ada_kv = din("w_ada_kv", [D, 2 * D])
    w_kv_b = din("w_kv_b", [D, 2 * D])
    w_q_b = din("w_q_b", [2, D, D])
    w_o_b = din("w_o_b", [2, D, D])
    relpos_b = din("relpos_b", [2, 16, 257])
    w_up = din("w_up", [4, D, 2 * DFF])
    w_down = din("w_down", [4, DFF, D])
    vecs = din("vecs", [V_ROWS, 128])
    ident_d = din("ident", [128, 128])
    anti_d = din("anti", [128, 128])
    onehot_d = din("t5oh", [32, 510])

    y_prompt = dout("y_prompt", [S, D])
    y_sample = dout("y_sample", [DEC, D])
    p_ak = dout("p_ak", [2, 128, 256])
    p_av = dout("p_av", [2, 128, 256])
    p_bk = dout("p_bk", [512, 1024])
    p_bv = dout("p_bv", [512, 1024])
    p_conv = dout("p_conv", [4, 2, 2 * DFF])
    s_ak = dout("s_ak", [2, DEC, 256])
    s_av = dout("s_av", [2, DEC, 256])
    s_bk = dout("s_bk", [DEC, 1024])
    s_bv = dout("s_bv", [DEC, 1024])
    s_conv = dout("s_conv", [4, 2, 2 * DFF])

    sc_up = [dscr(f"sc_up{l}", [11, 128, 8 * 512]) for l in range(4)]
    sc_dn = [dscr(f"sc_dn{l}", [8, 128, 22 * 128]) for l in range(4)]
    sc_qa = [dscr(f"sc_qa{l}", [4, 128, 8 * 512]) for l in range(2)]
    sc_oa = [dscr(f"sc_oa{l}", [2, 128, 8 * 512]) for l in range(2)]
    sc_qb = [dscr(f"sc_qb{l}", [2, 128, 8 * 512]) for l in range(2)]
    sc_ob = [dscr(f"sc_ob{l}", [2, 128, 8 * 512]) for l in range(2)]
    sc_kvb = dscr("sc_kvb", [4, 128, 8 * 512])
    sc_veca = dscr("sc_veca", [16, 510], F32)
    sc_vecb = [dscr(f"sc_vecb{j}", [16, 510], F32) for j in range(2)]
    sc_ebb = [dscr(f"sc_ebb{j}", [128, 4096]) for j in range(2)]

    es = contextlib.ExitStack()
    with es:
        def sb(name, shape, dt):
            return es.enter_context(nc.sbuf_tensor(name, list(shape), dt))

        def ps(name):
            return es.enter_context(nc.psum_tensor(name, [128, 512], F32))

        xT = sb("xT", [128, C8, TT], F32)
        hT = sb("hT", [128, C8, TT], BF16)
        actT = sb("actT", [128, FC, TT], BF16)
        KB = sb("KB", [128, 2, C8, TT], BF16)
        VB = sb("VB", [128, 2, 4, D], BF16)
        KA = [sb(f"KA{l}", [128, 4, 5 * 128], BF16) for l in range(2)]
        VA = [sb(f"VA{l}", [128, 5, 256], BF16) for l in range(2)]
        Ubuf = [[sb(f"U{a}{b}", [128, TT + 2], F32) for b in range(2)] for a in range(2)]
        Ybuf = [[sb(f"Y{a}{b}", [128, TT], F32) for b in range(2)] for a in range(2)]
        Ebuf = [sb(f"E{i}", [128, 512], BF16) for i in range(5)]
        E0buf = sb("E0m", [128, 512], BF16)
        Tbuf = sb("Tb", [128, 512], F32)
        R2buf = sb("R2", [128, 512], F32)
        ybf = [sb(f"ybf{i}", [128, TT], BF16) for i in range(2)]
        sqb = [sb(f"sq{i}", [128, TT], BF16) for i in range(2)]
        mean_sb = sb("mean", [128, TT], F32)
        var_sb = sb("var", [128, TT], F32)
        rstd_sb = sb("rstd", [128, TT], F32)
        tmpb = [sb(f"tmp{i}", [128, TT], F32) for i in range(2)]
        EBA = sb("EBA", [128, 2, 2, 8, 128], BF16)
        EBB = sb("EBB", [128, 2, 2, 8, 128], BF16)
        W8 = [sb(f"W8_{i}", [128, 8 * 512], BF16) for i in range(3)]
        W22 = [sb(f"W22_{i}", [128, 22 * 128], BF16) for i in range(2)]
        stage = [sb(f"stg{i}", [128, D], F32) for i in range(2)]
        ident = sb("identS", [128, 128], F32)
        anti = sb("antiS", [128, 128], F32)
        ones_bf = sb("ones_bf", [128, 64], BF16)
        onesf = sb("onesf", [128, 128], BF16)
        VF = sb("VF", [128, V_ROWS], F32)
        SCt = sb("SCt", [128, 16], F32)
        ADA = sb("ADA", [128, 2, 208], F32)
        ADA1 = sb("ADA1", [128, 2, 208], F32)
        ADAG = sb("ADAG", [128, 2, 208], F32)
        FT = sb("FT", [128, 2, 4, 2, 2, 8], F32)
        FKV = sb("FKV", [128, 2, 2, 8], F32)
        carry = sb("carry", [128, 4, NCH, 2], F32)
        esink = sb("esink", [128, 2, 8], F32)
        t5_sb = sb("t5_sb", [32, 16], F32)
        tokA = tmpb[0][0:2, 0:256]
        oh_sb = mean_sb[0:32, 0:510]
        vec_sb = var_sb[0:16, 0:510]
        rp_sb = rstd_sb[0:16, 0:257]

        print("SBUF bytes remaining after alloc", nc.sbuf_bytes_remaining)
        PS = [ps(f"ps{i}") for i in range(8)]
        D0, D1, S0, S1, O0, O1, N0, N1 = range(8)

        sems = {}
        sem_list = []

        def new_sem(name):
            s = es.enter_context(nc.semaphore(name))
            sem_list.append(s)
            return s

        def new_chan(name):
            return Chan(new_sem(name))

        R_x = [Res() for _ in range(C8)]
        R_h = [Res() for _ in range(C8)]
        R_act = [Res() for _ in range(FC)]
        R_ps = [Res() for _ in range(8)]
        R_w8 = [Res() for _ in range(3)]
        R_w22 = [Res() for _ in range(2)]
        R_ka = [[Res() for _ in range(5)] for _ in range(2)]
        R_va = [[Res() for _ in range(5)] for _ in range(2)]
        R_kb = [[Res() for _ in range(4)] for _ in range(2)]
        R_vb = [[Res() for _ in range(4)] for _ in range(2)]
        R_u = [[Res() for _ in range(2)] for _ in range(2)]
        R_uh = [[Res() for _ in range(2)] for _ in range(2)]
        R_y = [[Res() for _ in range(2)] for _ in range(2)]
        R_e = [Res() for _ in range(5)]
        R_e0 = Res()
        R_t = Res()
        R_r2 = Res()
        R_ybf = [Res(), Res()]
        R_sq = [Res(), Res()]
        R_mean = Res()
        R_var = Res()
        R_rstd = Res()
        R_tmp = [Res(), Res()]
        R_eba = Res()
        R_ebb = Res()
        R_stage = [Res(), Res()]
        R_const = Res()
        R_vf = Res()
        R_sc = Res()
        R_ada = Res()
        R_tabs = Res()
        R_carry = [[Res() for _ in range(NCH)] for _ in range(4)]
        R_esink = Res()
        R_tok = Res()
        R_misc = Res()
        R_scr = {}
        R_out = []

        ch_w8 = [new_chan(f"cw8_{i}") for i in range(3)]
        ch_w22 = [new_chan(f"cw22_{i}") for i in range(2)]
        ch_stage = [new_chan(f"cst{i}") for i in range(2)]
        ch_misc = new_chan("cmisc")
        ch_cast = [new_chan(f"ccast{i}") for i in range(4)]
        R_castq = [Res() for _ in range(4)]
        R_miscq = Res()
        cast_key_idx = {}
        ch_ebb = new_chan("cebb")

        def mm(out, lhsT, rhs, start, stop, reads, writes, skip=False):
            if skip:
                return P.emit("pe", lambda e: e.matmul(out, lhsT, rhs, start=start, stop=stop, skip_group_check=True),
                              reads, writes)
            return P.emit("pe", lambda e: e.matmul(out, lhsT, rhs, start=start, stop=stop), reads, writes)

        def tr(out, in_, idn, reads, writes):
            return P.emit("pe", lambda e: e.transpose(out, in_, idn), reads, writes)

        def act(out, in_, func, reads, writes, bias=0.0, scale=1.0):
            return P.emit("act", lambda e: e.activation(out, in_, func, bias=bias, scale=scale), reads, writes)

        def tsc(eng, out, in0, s1, s2, op0, op1, reads, writes):
            if op1 is None:
                return P.emit(eng, lambda e: e.tensor_scalar(out, in0, s1, None, op0), reads, writes)
            return P.emit(eng, lambda e: e.tensor_scalar(out, in0, s1, s2, op0, op1), reads, writes)

        def stt(eng, out, in0, scalar, in1, op0, op1, reads, writes):
            return P.emit(eng, lambda e: e.scalar_tensor_tensor(out, in0, scalar, in1, op0, op1), reads, writes)

        def ttn(eng, out, in0, in1, op, reads, writes):
            return P.emit(eng, lambda e: e.tensor_tensor(out, in0, in1, op), reads, writes)

        def cpy(eng, out, in_, reads, writes):
            if eng == "act":
                return P.emit("act", lambda e: e.activation(out, in_, AF.Identity), reads, writes)
            return P.emit(eng, lambda e: e.tensor_copy(out, in_), reads, writes)

        def mset(eng, ap, val, writes):
            return P.emit(eng, lambda e: e.memset(ap, val), (), writes)

        import os as _os0
        NOOUT = _os0.environ.get("NOOUT", "0") == "1"
        out_names = ("y_prompt", "y_sample", "p_ak", "p_av", "p_bk", "p_bv", "p_conv", "s_ak", "s_av", "s_bk",
                     "s_bv", "s_conv")

        def dma(q, out, in_, reads, writes, chan, slow=False):
            if NOOUT and out.tensor.name in out_names:
                op_ = Op()
                op_.dma = None
                return op_
            if chan is ch_misc:
                writes = list(writes) + [R_miscq]
            if slow:
                return P.emit(q, lambda e: e.dma_start(out=out, in_=in_, allow_slow_non_contiguous=True),
                              reads, writes, chan)
            return P.emit(q, lambda e: e.dma_start(out=out, in_=in_), reads, writes, chan)

        def scr_res(key):
            if key not in R_scr:
                R_scr[key] = Res()
            return R_scr[key]

        def cast(dst, src, key):
            if key not in cast_key_idx:
                cast_key_idx[key] = len(cast_key_idx) % 4
            ci = cast_key_idx[key]
            dma("pool", dst, src, (), [scr_res(key), R_castq[ci]], ch_cast[ci])

        def cast_cols(scr, slot, w2d, col0, key):
            src = w2d[:, col0:col0 + 512].rearrange("(kc p) c -> p kc c", p=128)
            dst = scr[slot].rearrange("p (kc c) -> p kc c", kc=8)
            cast(dst, src, key)

        def emit_casts():
            for l in range(4):
                if l < 2:
                    for s in range(2):
                        cast_cols(sc_qa[l], s, w_qkv_a[l], s * 512, ("qa", l, s))
                    for dup in range(2):
                        for kc in range(8):
                            src = w_qkv_a[l][kc * 128:(kc + 1) * 128, 1024:1280].rearrange("p (g d) -> p g d", g=4)
                            dst = sc_qa[l][2].rearrange("p (kc g u d) -> p kc g u d", kc=8, g=4, u=2)[:, kc, :, dup, :]
                            cast(dst, src, ("qa", l, 2))
                    cast_cols(sc_qa[l], 3, w_qkv_a[l], 1024, ("qa", l, 3))
                    for s in range(2):
                        cast_cols(sc_oa[l], s, w_o_a[l], s * 512, ("oa", l, s))
                else:
                    j = l - 2
                    for s in range(2):
                        cast_cols(sc_qb[j], s, w_q_b[j], s * 512, ("qb", j, s))
                    for s in range(2):
                        cast_cols(sc_ob[j], s, w_o_b[j], s * 512, ("ob", j, s))
                for s in range(11):
                    chs = UP_ORDER[4 * s:4 * s + 4]
                    if chs[3] == chs[0] + 3:
                        cast_cols(sc_up[l], s, w_up[l], chs[0] * 128, ("up", l, s))
                    else:
                        for hh in range(2):
                            c0 = chs[2 * hh] * 128
                            src = w_up[l][:, c0:c0 + 256].rearrange("(kc p) c -> p kc c", p=128)
                            dst = sc_up[l][s].rearrange("p (kc c) -> p kc c", kc=8)[:, :, hh * 256:(hh + 1) * 256]
                            cast(dst, src, ("up", l, s))
                for m in range(8):
                    src = w_down[l][:, m * 128:(m + 1) * 128].rearrange("(kc p) c -> p kc c", p=128)
                    dst = sc_dn[l][m].rearrange("p (kc c) -> p kc c", kc=22)
                    cast(dst, src, ("dn", l, m))
                if l == 1:
                    for s in range(4):
                        cast_cols(sc_kvb, s, w_kv_b, s * 512, ("kvb", s))

        w8_n = [0]
        w22_n = [0]

        def load_w8(scr_ap, key):
            i = w8_n[0] % 3
            w8_n[0] += 1
            dma("sp", W8[i][:], scr_ap, [scr_res(key)], [R_w8[i]], ch_w8[i])
            return i

        def load_w22(scr_ap, key):
            i = w22_n[0] % 2
            w22_n[0] += 1
            dma("sp", W22[i][:], scr_ap, [scr_res(key)], [R_w22[i]], ch_w22[i])
            return i

        def w8v(i):
            return W8[i][:].rearrange("p (kc c) -> p kc c", kc=8)

        def setup():
            dma("sp", ident[:], ident_d[:, :], (), [R_const], ch_misc)
            dma("sp", anti[:], anti_d[:, :], (), [R_const], ch_misc)
            dma("sp", oh_sb, onehot_d[:, :], (), [R_mean], ch_misc)
            dma("sp", t5_sb[:], t5_table[:, :], (), [R_const], ch_misc)
            mset("dve", ones_bf[:], 1.0, [R_const])
            mset("dve", onesf[:], 1.0 / 1024.0, [R_const])
            mset("pool", carry[:], 0.0, [R_carry[l][c] for l in range(4) for c in range(NCH)])
            mset("pool", E0buf[:], 0.0, [R_e0])
            mset("pool", hT[:], 0.0, R_h)
            mset("pool", KB[:], 0.0, R_kb[0] + R_kb[1])
            for la_ in range(2):
                mset("pool", KA[la_][:], 0.0, R_ka[la_])
            for r in range(V_ROWS // 128):
                st = stage[r % 2]
                dma("sp", st[:, 0:128], vecs[r * 128:(r + 1) * 128, :], (), [R_stage[r % 2]], ch_stage[r % 2])
                b = D0 + (r % 2)
                tr(PS[b][:, 0:128], st[:, 0:128], ident[:], [R_stage[r % 2], R_const], [R_ps[b]])
                cpy("dve", VF[:, r * 128:(r + 1) * 128], PS[b][:, 0:128], [R_ps[b]], [R_vf])
            act(SCt[:], VF[:, V_C:V_C + 16], AF.Silu, [R_vf], [R_sc])
            blocks = [(w_ada[l], cb, l * 48 + cb * 2) for l in range(4) for cb in range(24)]
            blocks += [(w_ada_kv, cb, 192 + cb * 2) for cb in range(8)]
            for bi, (w2d, cb, col) in enumerate(blocks):
                i = w8_n[0] % 3
                w8_n[0] += 1
                wv = W8[i][:].bitcast(F32).rearrange("p (kc c) -> p kc c", kc=8)
                src = w2d[:, cb * 256:(cb + 1) * 256].rearrange("(kc p) c -> p kc c", p=128)
                dma("sp", wv, src, (), [R_w8[i]], ch_w8[i])
                b = D0 + (bi % 2)
                for kc in range(8):
                    mm(PS[b][0:2, 0:256], SCt[:, kc:16:8], wv[:, kc, :], kc == 0, kc == 7,
                       [R_sc, R_w8[i]], [R_ps[b]])
                cpy("act", tokA, PS[b][0:2, 0:256], [R_ps[b]], [R_tmp[0]])
                b2 = O0 + (bi % 2)
                for hh in range(2):
                    tr(PS[b2][:, hh * 2:hh * 2 + 2], tokA[0:2, hh * 128:(hh + 1) * 128], ident[0:2, 0:2],
                       [R_tmp[0], R_const], [R_ps[b2]])
                cpy("dve", ADA[:, :, col:col + 2].rearrange("p g c -> p c g"),
                    PS[b2][:, 0:4].rearrange("p (c g) -> p c g", c=2), [R_ps[b2]], [R_ada])
            for g in range(2):
                ttn("dve", ADA[:, g, :], ADA[:, g, :], VF[:, 0:208], ALU.add, [R_ada, R_vf], [R_ada])
            tsc("dve", ADA1[:], ADA[:], 1.0, None, ALU.add, None, [R_ada], [R_tabs])
            tsc("dve", ADAG[:], ADA[:], 1.0, 1.0 / ALPHA, ALU.add, ALU.mult, [R_ada], [R_tabs])
            for g in range(2):
                for l in range(4):
                    g1 = VF[:, V_LNG + (l * 2) * 8: V_LNG + (l * 2) * 8 + 8]
                    b1 = VF[:, V_LNB + (l * 2) * 8: V_LNB + (l * 2) * 8 + 8]
                    a1 = ADA1[:, g, l * 48 + 32: l * 48 + 40]
                    sh = ADA[:, g, l * 48 + 24: l * 48 + 32]
                    ttn("dve", FT[:, g, l, 0, 0, :], g1, a1, ALU.mult, [R_vf, R_tabs], [R_tabs])
                    ttn("dve", FT[:, g, l, 0, 1, :], b1, a1, ALU.mult, [R_vf, R_tabs], [R_tabs])
                    ttn("dve", FT[:, g, l, 0, 1, :], FT[:, g, l, 0, 1, :], sh, ALU.add, [R_tabs, R_ada], [R_tabs])
                    if l < 3:
                        g2 = VF[:, V_LNG + (l * 2 + 1) * 8: V_LNG + (l * 2 + 1) * 8 + 8]
                        b2 = VF[:, V_LNB + (l * 2 + 1) * 8: V_LNB + (l * 2 + 1) * 8 + 8]
                        a1 = ADA1[:, g, (l + 1) * 48 + 8: (l + 1) * 48 + 16]
                        sh = ADA[:, g, (l + 1) * 48: (l + 1) * 48 + 8]
                        ttn("dve", FT[:, g, l, 1, 0, :], g2, a1, ALU.mult, [R_vf, R_tabs], [R_tabs])
                        ttn("dve", FT[:, g, l, 1, 1, :], b2, a1, ALU.mult, [R_vf, R_tabs], [R_tabs])
                        ttn("dve", FT[:, g, l, 1, 1, :], FT[:, g, l, 1, 1, :], sh, ALU.add, [R_tabs, R_ada], [R_tabs])
                g2 = VF[:, V_LNG + 3 * 8: V_LNG + 3 * 8 + 8]
                b2 = VF[:, V_LNB + 3 * 8: V_LNB + 3 * 8 + 8]
                a1 = ADA1[:, g, 200:208]
                sh = ADA[:, g, 192:200]
                ttn("dve", FKV[:, g, 0, :], g2, a1, ALU.mult, [R_vf, R_tabs], [R_tabs])
                ttn("dve", FKV[:, g, 1, :], b2, a1, ALU.mult, [R_vf, R_tabs], [R_tabs])
                ttn("dve", FKV[:, g, 1, :], FKV[:, g, 1, :], sh, ALU.add, [R_tabs, R_ada], [R_tabs])
            for l in range(2):
                for par in range(2):
                    src = bass.AP(sinks_a.tensor, l * 16 + par, [[0, 64], [2, 8]])
                    dma("sp", esink[par * 64:(par + 1) * 64, l, :], src, (), [R_esink], ch_misc, slow=True)
            act(esink[:], esink[:], AF.Exp, [R_esink], [R_esink])
            mm(PS[D0][0:16, 0:510], t5_sb[:], oh_sb, True, True, [R_const, R_mean], [R_ps[D0]])
            cpy("dve", vec_sb, PS[D0][0:16, 0:510], [R_ps[D0]], [R_var])
            dma("sp", sc_veca[:, :], vec_sb, [R_var], [scr_res("veca")], ch_misc)
            make_eb(sc_veca, "veca", EBA, R_eba)
            mset("pool", EBA[0:64, 0, :, :, 64:128], 0.0, [R_eba])
            mset("pool", EBA[64:128, 1, :, :, 0:64], 0.0, [R_eba])
            for j in range(2):
                dma("sp", rp_sb, relpos_b[j], (), [R_rstd], ch_misc)
                mset("dve", vec_sb, 0.0, [R_var])
                tsc("dve", vec_sb[:, 0:128], rp_sb[:, 129:257], rp_sb[:, 256:257], None, ALU.subtract, None,
                    [R_rstd, R_var], [R_var])
                tsc("dve", vec_sb[:, 255:510], rp_sb[:, 1:256], rp_sb[:, 256:257], None, ALU.subtract, None,
                    [R_rstd, R_var], [R_var])
                dma("sp", sc_vecb[j][:, :], vec_sb, [R_var], [scr_res(("vecb", j))], ch_misc)
                make_eb(sc_vecb[j], ("vecb", j), EBB, R_ebb)
                mset("pool", EBB[64:128, 1, :, :, 0:64], 0.0, [R_ebb])
                dma("sp", sc_ebb[j][:, :], EBB[:].rearrange("p a b c d -> p (a b c d)"), [R_ebb],
                    [scr_res(("ebb", j))], ch_misc)

        def make_eb(vec_dram, key, EB, R_eb):
            for ty in range(2):
                i = w8_n[0] % 3
                w8_n[0] += 1
                brev = W8[i][:].bitcast(F32).rearrange("p (h q) -> p h q", h=16)
                src = bass.AP(vec_dram.tensor, ty * 255, [[1, 128], [510, 16], [1, 128]])
                dma("sp", brev, src, [scr_res(key)], [R_w8[i]], ch_w8[i])
                for par in range(2):
                    for jg in range(2):
                        b = D0 + ((par * 2 + jg) % 2)
                        rhs = brev[:, par + 8 * jg: par + 8 * jg + 7: 2, :]
                        mm(PS[b][:, :], anti[:], rhs, True, True, [R_const, R_w8[i]], [R_ps[b]])
                        act(EB[:, ty, par, jg * 4:(jg + 1) * 4, :],
                            PS[b][:, :].rearrange("p (j q) -> p j q", j=4), AF.Exp, [R_ps[b]], [R_eb])

        def tab(g, l, part, c):
            col = l * 48 + part * 8 + c
            return col

        def load_x_group(src_rows, nrows, si):
            dma("sp", stage[si][0:nrows, :], src_rows, (), [R_stage[si]], ch_stage[si])

        def x_in(xsrc, T, g, first_groups_loaded=0):
            ngr = (T + 127) // 128
            for gi in range(ngr):
                nr = min(128, T - gi * 128)
                si = gi % 2
                if gi >= first_groups_loaded:
                    load_x_group(xsrc[gi * 128: gi * 128 + nr, :], nr, si)
                for half in range(2):
                    b = D0 + half
                    for cc in range(4):
                        c = half * 4 + cc
                        tr(PS[b][:, cc * 128: cc * 128 + nr], stage[si][0:nr, c * 128:(c + 1) * 128],
                           ident[0:nr, 0:nr], [R_stage[si], R_const], [R_ps[b]])
                    out = xT[:, half * 4:(half + 1) * 4, gi * 128: gi * 128 + nr]
                    inp = PS[b][:, :].rearrange("p (c t) -> p c t", c=4)[:, :, 0:nr]
                    if half == 0:
                        cpy("act", out, inp, [R_ps[b]], R_x[0:4])
                    else:
                        cpy("dve", out, inp, [R_ps[b]], R_x[4:8])

        def h_from_x(g, l, T):
            for c in range(C8):
                act(hT[:, c, 0:T], xT[:, c, 0:T], AF.Identity, [R_x[c], R_tabs], [R_h[c]],
                    bias=ADA[:, g, tab(g, l, 0, c): tab(g, l, 0, c) + 1],
                    scale=ADA1[:, g, tab(g, l, 1, c): tab(g, l, 1, c) + 1])

        dcount = [0]

        DRING = [D0, D1, S0, S1, O0, O1]

        def next_d():
            b = DRING[dcount[0] % len(DRING)]
            dcount[0] += 1
            return b

        def proj_fm(slot_keys, scr_list, KCn, src_fn, src_res, n_out, T, evac):
            cur = None
            for m in range(n_out):
                if m % 4 == 0:
                    cur = load_w8(scr_list[m // 4], slot_keys[m // 4])
                wv = w8v(cur)
                b = next_d()
                for kc in range(KCn):
                    mm(PS[b][:, 0:T], wv[:, kc, (m % 4) * 128:(m % 4 + 1) * 128], src_fn(kc), kc == 0, kc == KCn - 1,
                       [R_w8[cur], src_res[kc]], [R_ps[b]])
                evac(m, b)

        def layer_norm(g, l, sub, T, ft1, ft2, h_out):
            for c in range(C8):
                i = c % 2
                cpy("dve", ybf[i][:, 0:T], xT[:, c, 0:T], [R_x[c]], [R_ybf[i]])
                act(sqb[i][:, 0:T], xT[:, c, 0:T], AF.Square, [R_x[c]], [R_sq[i]])
                mm(PS[D0][:, 0:T], onesf[:], ybf[i][:, 0:T], c == 0, c == 7, [R_const, R_ybf[i]], [R_ps[D0]])
                mm(PS[D1][:, 0:T], onesf[:], sqb[i][:, 0:T], c == 0, c == 7, [R_const, R_sq[i]], [R_ps[D1]])
            cpy("act", mean_sb[:, 0:T], PS[D0][:, 0:T], [R_ps[D0]], [R_mean])
            ttn("dve", var_sb[:, 0:T], mean_sb[:, 0:T], mean_sb[:, 0:T], ALU.mult, [R_mean], [R_var])
            ttn("dve", var_sb[:, 0:T], PS[D1][:, 0:T], var_sb[:, 0:T], ALU.subtract, [R_ps[D1], R_var], [R_var])
            tsc("dve", var_sb[:, 0:T], var_sb[:, 0:T], EPS_P, None, ALU.add, None, [R_var], [R_var])
            act(var_sb[:, 0:T], var_sb[:, 0:T], AF.Ln, [R_var], [R_var])
            act(rstd_sb[:, 0:T], var_sb[:, 0:T], AF.Exp, [R_var], [R_rstd], scale=-0.5)
            lng = V_LNG + (l * 2 + sub) * 8
            lnb = V_LNB + (l * 2 + sub) * 8
            for c in range(C8):
                i = c % 2
                eng = "pool" if c in (0, 3, 6) else "dve"
                ttn(eng, tmpb[i][:, 0:T], xT[:, c, 0:T], mean_sb[:, 0:T], ALU.subtract, [R_x[c], R_mean], [R_tmp[i]])
                ttn(eng, tmpb[i][:, 0:T], tmpb[i][:, 0:T], rstd_sb[:, 0:T], ALU.mult, [R_tmp[i], R_rstd], [R_tmp[i]])
                if h_out:
                    act(hT[:, c, 0:T], tmpb[i][:, 0:T], AF.Identity, [R_tmp[i], R_tabs], [R_h[c]],
                        bias=ft2[:, c:c + 1], scale=ft1[:, c:c + 1])
                act(xT[:, c, 0:T], tmpb[i][:, 0:T], AF.Identity, [R_tmp[i], R_vf], [R_x[c]],
                    bias=VF[:, lnb + c: lnb + c + 1], scale=VF[:, lng + c: lng + c + 1])

        def attention(l, g, T, pairs, tiles_of_pair, is_A):
            QT = actT
            OT = hT
            groups = []
            for pi, (q0, nq) in enumerate(pairs):
                kts = tiles_of_pair(pi)
                for jg in range(2):
                    for par in range(2):
                        for ti, kt in enumerate(kts):
                            groups.append((pi, q0, nq, jg, par, ti, len(kts), kt))
            nG = len(groups)
            state = {}
            pjcount = [0]

            def rec_S(gi):
                pi, q0, nq, jg, par, ti, nkt, kt = groups[gi]
                b = (S0, S1, D0, D1)[gi % 4]
                nk = kt["nk"]
                for jj in range(4):
                    j = jg * 4 + jj
                    mm(PS[b][0:max(nk, 32), jj * nq:(jj + 1) * nq], kt["kT"](par, j),
                       QT[par * 64:(par + 1) * 64, j, q0:q0 + nq],
                       True, True, kt["kres"] + [R_act[j]], [R_ps[b]])
                ty = kt["type"]
                if ty == "m0":
                    ebuf, eres = E0buf, R_e0
                else:
                    ebuf, eres = Ebuf[gi % 5], R_e[gi % 5]
                ev = ebuf[:, 0:4 * nq].rearrange("p (j q) -> p j q", j=4)
                sv = PS[b][:, 0:4 * nq].rearrange("p (j q) -> p j q", j=4)
                if ty == "m0":
                    n1 = min(nq, 64)
                    act(ev[0:nk, :, 0:n1], sv[0:nk, :, 0:n1], AF.Exp, [R_ps[b]], [eres], scale=SCALE)
                    if nq > 64:
                        act(ev[64:128, :, 64:nq], sv[64:128, :, 64:nq], AF.Exp, [R_ps[b]], [eres], scale=SCALE)
                else:
                    act(ebuf[0:nk, 0:4 * nq], PS[b][0:nk, 0:4 * nq], AF.Exp, [R_ps[b]], [eres], scale=SCALE)
                if ty in ("e0", "e1"):
                    t_i = 0 if ty == "e0" else 1
                    EB, R_eb = (EBA, R_eba) if is_A else (EBB, R_ebb)
                    eng = "dve" if gi % 4 != 3 else "pool"
                    ttn(eng, ev[0:nk, :, :], ev[0:nk, :, :], EB[0:nk, t_i, par, jg * 4:(jg + 1) * 4, 0:nq], ALU.mult,
                        [eres, R_eb], [eres])
                state[gi] = (ebuf, eres)

            def rec_DP(gi):
                pi, q0, nq, jg, par, ti, nkt, kt = groups[gi]
                ebuf, eres = state.pop(gi)
                nk = kt["nk"]
                pj = pjcount[0]
                ob = O0 + (pj % 2)
                nb = N0 + (pj % 2)
                first = ti == 0
                last = ti == nkt - 1
                mm(PS[nb][par * 64:(par + 1) * 64, 0:4 * nq], ones_bf[0:nk, :], ebuf[0:nk, 0:4 * nq], first, last,
                   [R_const, eres], [R_ps[nb]])
                for jj in range(4):
                    j = jg * 4 + jj
                    mm(PS[ob][par * 64:(par + 1) * 64, jj * nq:(jj + 1) * nq], kt["v"](par, j),
                       ebuf[0:nk, jj * nq:(jj + 1) * nq], first and jj == 0, last and jj == 3,
                       kt["vres"] + [eres], [R_ps[ob]], skip=True)
                if last and par == 1:
                    n4 = 4 * nq
                    if is_A:
                        ttn("dve", Tbuf[:, 0:n4].rearrange("p (j q) -> p j q", j=4),
                            PS[nb][:, 0:n4].rearrange("p (j q) -> p j q", j=4),
                            esink[:, l, jg * 4:(jg + 1) * 4].unsqueeze(2).to_broadcast([128, 4, nq]), ALU.add,
                            [R_ps[nb], R_esink], [R_t])
                        P.emit("dve", lambda e: e.reciprocal(R2buf[:, 0:n4], Tbuf[:, 0:n4]), [R_t], [R_r2])
                    else:
                        P.emit("dve", lambda e: e.reciprocal(R2buf[:, 0:n4], PS[nb][:, 0:n4]), [R_ps[nb]], [R_r2])
                    ttn("dve", OT[:, jg * 4:(jg + 1) * 4, q0:q0 + nq],
                        PS[ob][:, 0:n4].rearrange("p (j q) -> p j q", j=4),
                        R2buf[:, 0:n4].rearrange("p (j q) -> p j q", j=4), ALU.mult,
                        [R_ps[ob], R_r2], R_h[jg * 4:(jg + 1) * 4])
                    pjcount[0] += 1

            for gi in range(nG + 3):
                if gi < nG:
                    rec_S(gi)
                if gi >= 3:
                    rec_DP(gi - 3)

        def ffn(l, g, T, last_out):
            cur = None
            pending_gelu = []
            for oi, ch in enumerate(UP_ORDER):
                if oi % 4 == 0:
                    cur = load_w8(sc_up[l][oi // 4], ("up", l, oi // 4))
                wv = w8v(cur)
                b = next_d()
                for kc in range(8):
                    mm(PS[b][:, 0:T], wv[:, kc, (oi % 4) * 128:(oi % 4 + 1) * 128], hT[:, kc, 0:T], kc == 0, kc == 7,
                       [R_w8[cur], R_h[kc]], [R_ps[b]])
                isg = ch >= FC
                i = ch - FC if isg else ch
                bq = oi % 4
                U = Ubuf[bq // 2][bq % 2]
                Y = Ybuf[bq // 2][bq % 2]
                RU = R_u[bq // 2][bq % 2]
                RUH = R_uh[bq // 2][bq % 2]
                RY = R_y[bq // 2][bq % 2]
                w0 = VF[:, V_CW + (l * 3 + 0) * NCH + ch: V_CW + (l * 3 + 0) * NCH + ch + 1]
                w1 = VF[:, V_CW + (l * 3 + 1) * NCH + ch: V_CW + (l * 3 + 1) * NCH + ch + 1]
                w2 = VF[:, V_CW + (l * 3 + 2) * NCH + ch: V_CW + (l * 3 + 2) * NCH + ch + 1]
                cb = VF[:, V_CB + l * NCH + ch: V_CB + l * NCH + ch + 1]
                cpy("act", U[:, 0:2], carry[:, l, ch, :], [R_carry[l][ch]], [RUH])
                cpy("act", U[:, 2:2 + T], PS[b][:, 0:T], [R_ps[b]], [RU])
                cpy("act", carry[:, l, ch, :], PS[b][:, T - 2:T], [R_ps[b]], [R_carry[l][ch]])
                if isg:
                    tsc("dve", Y[:, 0:T], PS[b][:, 0:T], w2, cb, ALU.mult, ALU.add, [R_ps[b], R_vf], [RY])
                    stt("dve", Y[:, 0:T], U[:, 1:1 + T], w1, Y[:, 0:T], ALU.mult, ALU.add, [RU, RUH, RY, R_vf], [RY])
                    stt("dve", Y[:, 0:T], U[:, 0:T], w0, Y[:, 0:T], ALU.mult, ALU.add, [RU, RUH, RY, R_vf], [RY])
                    if pending_gelu:
                        pending_gelu.pop()()
                    pending_gelu.append(lambda Y=Y, RY=RY, i=i: act(actT[:, i, 0:T], Y[:, 0:T], AF.Gelu_apprx_tanh,
                                                                    [RY], [R_act[i]]))
                else:
                    if pending_gelu:
                        pending_gelu.pop()()
                    act(Y[:, 0:T], PS[b][:, 0:T], AF.Identity, [R_ps[b], R_vf], [RY], bias=cb, scale=w2)
                    stt("dve", Y[:, 0:T], U[:, 1:1 + T], w1, Y[:, 0:T], ALU.mult, ALU.add, [RU, RUH, RY, R_vf], [RY])
                    stt("dve", Y[:, 0:T], U[:, 0:T], w0, Y[:, 0:T], ALU.mult, ALU.add, [RU, RUH, RY, R_vf], [RY])
                    ttn("pool", actT[:, i, 0:T], Y[:, 0:T], actT[:, i, 0:T], ALU.mult, [RY, R_act[i]], [R_act[i]])
            if last_out is not None:
                for r in range(2):
                    dst = bass.AP(last_out.tensor, l * 2 * 2 * DFF + r * 2 * DFF, [[1, 128], [128, NCH]])
                    o = dma("sp", dst, carry[:, l, :, r], [R_carry[l][c] for c in range(NCH)], [Res()], ch_misc,
                            slow=True)
                    R_out.append(o)
            for m in range(8):
                wi = load_w22(sc_dn[l][m], ("dn", l, m))
                wv = W22[wi][:].rearrange("p (kc c) -> p kc c", kc=22)
                b = next_d()
                for kc in range(FC):
                    mm(PS[b][:, 0:T], wv[:, kc, :], actT[:, kc, 0:T], kc == 0, kc == FC - 1,
                       [R_w22[wi], R_act[kc]], [R_ps[b]])
                gate = ADAG[:, g, tab(g, l, 5, m): tab(g, l, 5, m) + 1]
                stt("dve", xT[:, m, 0:T], PS[b][:, 0:T], gate, xT[:, m, 0:T], ALU.mult, ALU.add,
                    [R_ps[b], R_x[m], R_tabs], [R_x[m]])

        def tok_major(src_chunks_res, T, w_slot, col0, ncols, dst_fn, f32_out_fn):
            wv = w8v(w_slot)
            ngr = (T + 127) // 128
            for gi in range(ngr):
                nr = min(128, T - gi * 128)
                mr = max(nr, 32)
                b = next_d()
                for kc in range(8):
                    mm(PS[b][0:mr, 0:ncols], hT[:, kc, gi * 128: gi * 128 + mr], wv[:, kc, col0:col0 + ncols],
                       kc == 0, kc == 7, [R_h[kc], R_w8[w_slot]], [R_ps[b]])
                if dst_fn is not None:
                    dst_fn(gi, nr, b)
                if f32_out_fn is not None:
                    f32_out_fn(gi, nr, b)

        def run_tile(g, T, t, xsrc, ydst, is_last, outs):
            sample = g == 1
            npairs = (T + 127) // 128
            pairs = [(p * 128, min(128, T - p * 128)) for p in range(npairs)]
            cur_half = 1 if sample else t % 2
            prev_half = 1 - cur_half
            import os as _os2
            KSUB = int(_os2.environ.get("KSUB", "99"))
            x_in(xsrc, T, g)
            h_from_x(g, 0, T)
            if KSUB <= 1:
                return
            for l in range(4):
                is_A = l < 2
                if is_A:
                    la = l
                    def evq(m, b):
                        eng = "act" if m % 2 == 0 else "dve"
                        cpy(eng, actT[:, m, 0:T], PS[b][:, 0:T], [R_ps[b]], [R_act[m]])
                    proj_fm([("qa", la, 0), ("qa", la, 1)], [sc_qa[la][0], sc_qa[la][1]], 8,
                            lambda kc: hT[:, kc, 0:T], R_h, 8, T, evq)
                    KS2 = int(_os2.environ.get("KSUB2", "99"))
                    if KS2 <= 1:
                        return
                    kslot = load_w8(sc_qa[la][2], ("qa", la, 2))
                    wv = w8v(kslot)
                    for g4 in range(4):
                        b = next_d()
                        for kc in range(8):
                            mm(PS[b][:, 0:T], wv[:, kc, g4 * 128:(g4 + 1) * 128], hT[:, kc, 0:T], kc == 0, kc == 7,
                               [R_w8[kslot], R_h[kc]], [R_ps[b]])
                        eng = "act" if g4 % 2 == 0 else "dve"
                        cpy(eng, KA[la][:, g4, 128:128 + T], PS[b][:, 0:T], [R_ps[b]], R_ka[la][1:5])
                    if KS2 <= 2:
                        return
                    kvslot = load_w8(sc_qa[la][3], ("qa", la, 3))
                    KS3 = int(_os2.environ.get("KS3", "99"))
                    if KS3 <= 1:
                        return
                    if KS3 <= 2:
                        tok_major(R_h, T, kvslot, 256, 256, None, None)
                        return

                    def vdst(gi, nr, b):
                        if _os2.environ.get("NOVD3") and gi == 3 and not sample:
                            return
                        cpy("act" if nr == 128 else "dve", VA[la][0:nr, 1 + gi, :], PS[b][0:nr, 0:256], [R_ps[b]],
                            [R_va[la][1 + gi]])
                    want_out = (is_last or sample) and _os2.environ.get("NOWANT", "0") != "1"
                    if want_out:
                        okd, ovd = outs["ak"], outs["av"]

                        def vout(gi, nr, b):
                            if sample or gi == 3:
                                si = 0
                                if _os2.environ.get("ALT"):
                                    if _os2.environ.get("ALT") == "2":
                                        cpy("dve", R2buf[0:nr, 0:256], Tbuf[0:nr, 0:256], [R_ps[b]], [R_r2])
                                    elif _os2.environ.get("ALT") == "3":
                                        cpy("dve", R2buf[0:nr, 0:256], PS[b][0:nr, 0:256], [R_ps[b]], [R_r2])
                                        cpy("dve", Tbuf[0:nr, 0:256], PS[b][0:nr, 0:256], [R_ps[b]], [R_t])
                                    else:
                                        tsc("dve", R2buf[0:nr, 0:256], PS[b][0:nr, 0:256], 1.0, None, ALU.mult, None, [R_ps[b]], [R_r2])
                                else:
                                    cpy(_os2.environ.get("VENG2", "dve"), stage[si][0:nr, 0:256], PS[b][0:nr, 0:256], [R_ps[b]], [R_stage[si]])
                                o = dma("sp", ovd[la][0:nr, :], stage[si][0:nr, 0:256], [R_stage[si]], [Res()],
                                        ch_stage[si])
                                R_out.append(o)

                        def kout(gi, nr, b):
                            if sample or gi == 3:
                                si = 1
                                cpy(_os2.environ.get("VENG2", "dve"), stage[si][0:nr, 0:256], PS[b][0:nr, 0:256], [R_ps[b]], [R_stage[si]])
                                o = dma("sp", okd[la][0:nr, :], stage[si][0:nr, 0:256], [R_stage[si]], [Res()],
                                        ch_stage[si])
                                R_out.append(o)
                        tok_major(R_h, T, kvslot, 256, 256, vdst, None if _os2.environ.get("NOV") else vout)
                        if not _os2.environ.get("NOK"):
                            tok_major(R_h, T, kvslot, 0, 256, None, kout)
                    else:
                        tok_major(R_h, T, kvslot, 256, 256, vdst, None)

                    if KSUB <= 2:
                        return
                    def tiles_of_pair(pi, la=la):
                        res = []
                        for ty, ti in (("e0", pi), ("e1", pi + 1)):
                            if ti == 0 and (not sample) and t == 0:
                                continue
                            nk = 128
                            if ti >= 1:
                                nk = min(128, T - (ti - 1) * 128)
                            res.append({
                                "nk": nk, "type": ty,
                                "kT": (lambda par, j, ti=ti, nk=nk: KA[la][par * 64:(par + 1) * 64, j // 2,
                                                                            ti * 128: ti * 128 + max(nk, 32)]),
                                "kres": [R_ka[la][ti]],
                                "v": (lambda par, j, ti=ti, nk=nk: VA[la][0:nk, ti, (j // 2) * 64:(j // 2) * 64 + 64]),
                                "vres": [R_va[la][ti]],
                            })
                        return res
                    attention(la, g, T, pairs, tiles_of_pair, True)
                    if KSUB <= 3:
                        return
                    if not sample:
                        cpy("pool", KA[la][:, :, 0:128], KA[la][:, :, 512:640], [R_ka[la][4]], [R_ka[la][0]])
                        cpy("pool", VA[la][:, 0, :], VA[la][:, 4, :], [R_va[la][4]], [R_va[la][0]])
                    wo_keys = [("oa", la, 0), ("oa", la, 1)]
                    wo_scr = [sc_oa[la][0], sc_oa[la][1]]
                else:
                    jb = l - 2
                    dma("sp", EBB[:].rearrange("p a b c d -> p (a b c d)"), sc_ebb[jb][:, :],
                        [scr_res(("ebb", jb))], [R_ebb], ch_ebb)

                    def evq(m, b):
                        eng = "act" if m % 2 == 0 else "dve"
                        cpy(eng, actT[:, m, 0:T], PS[b][:, 0:T], [R_ps[b]], [R_act[m]])
                    proj_fm([("qb", jb, 0), ("qb", jb, 1)], [sc_qb[jb][0], sc_qb[jb][1]], 8,
                            lambda kc: hT[:, kc, 0:T], R_h, 8, T, evq)

                    def tiles_of_pair(pi):
                        res = []
                        for k5 in range(5):
                            ti = pi + k5
                            if ti < 4 and (not sample) and t == 0:
                                continue
                            half = prev_half if ti < 4 else cur_half
                            tl = ti % 4
                            nk = 128 if ti < 4 else min(128, T - tl * 128)
                            ty = ("m0", "n", "n", "e0", "e1")[k5]
                            res.append({
                                "nk": nk, "type": ty,
                                "kT": (lambda par, j, half=half, tl=tl, nk=nk:
                                       KB[par * 64:(par + 1) * 64, half, j, tl * 128: tl * 128 + max(nk, 32)]),
                                "kres": [R_kb[half][tl]],
                                "v": (lambda par, j, half=half, tl=tl, nk=nk:
                                      VB[0:nk, half, tl, (2 * j + par) * 64:(2 * j + par) * 64 + 64]),
                                "vres": [R_vb[half][tl]],
                            })
                        return res
                    attention(l, g, T, pairs, tiles_of_pair, False)
                    wo_keys = [("ob", jb, 0), ("ob", jb, 1)]
                    wo_scr = [sc_ob[jb][0], sc_ob[jb][1]]

                def evo(m, b, l=l):
                    gate = ADAG[:, g, tab(g, l, 2, m): tab(g, l, 2, m) + 1]
                    stt("dve", xT[:, m, 0:T], PS[b][:, 0:T], gate, xT[:, m, 0:T], ALU.mult, ALU.add,
                        [R_ps[b], R_x[m], R_tabs], [R_x[m]])
                proj_fm(wo_keys, wo_scr, 8, lambda kc: hT[:, kc, 0:T], R_h, 8, T, evo)
                layer_norm(g, l, 0, T, FT[:, g, l, 0, 0, :], FT[:, g, l, 0, 1, :], True)
                if KSUB <= 4:
                    return
                conv_out = outs["conv"] if (is_last or sample) else None
                ffn(l, g, T, conv_out)
                if KSUB <= 5:
                    return
                if l == 1:
                    layer_norm(g, l, 1, T, FKV[:, g, 0, :], FKV[:, g, 1, :], True)
                    def evk(m, b):
                        eng = "act" if m % 2 == 0 else "dve"
                        if T >= 128:
                            cpy(eng, KB[:, cur_half, m, 0:T], PS[b][:, 0:T], [R_ps[b]], R_kb[cur_half])
                        else:
                            cpy(eng, KB[:, cur_half, m, 0:T], PS[b][:, 0:T], [R_ps[b]], [R_kb[cur_half][0]])
                    proj_fm([("kvb", 0), ("kvb", 1)], [sc_kvb[0], sc_kvb[1]], 8,
                            lambda kc: hT[:, kc, 0:T], R_h, 8, T, evk)
                    want_out = is_last or sample
                    for hv in range(2):
                        vslot = load_w8(sc_kvb[2 + hv], ("kvb", 2 + hv))

                        def vdst(gi, nr, b, hv=hv):
                            cpy("act" if nr == 128 else "dve", VB[0:nr, cur_half, gi, hv * 512:(hv + 1) * 512],
                                PS[b][0:nr, 0:512], [R_ps[b]], [R_vb[cur_half][gi]])

                        def vout(gi, nr, b, hv=hv):
                            si = gi % 2
                            cpy("dve", stage[si][0:nr, 0:512], PS[b][0:nr, 0:512], [R_ps[b]], [R_stage[si]])
                            o = dma("sp", outs["bv"][gi * 128: gi * 128 + nr, hv * 512:(hv + 1) * 512],
                                    stage[si][0:nr, 0:512], [R_stage[si]], [Res()], ch_stage[si])
                            R_out.append(o)
                        tok_major(R_h, T, vslot, 0, 512, vdst, vout if want_out else None)
                    if want_out:
                        for hk in range(2):
                            kslot = load_w8(sc_kvb[hk], ("kvb", hk))

                            def kout(gi, nr, b, hk=hk):
                                si = gi % 2
                                cpy("dve", stage[si][0:nr, 0:512], PS[b][0:nr, 0:512], [R_ps[b]], [R_stage[si]])
                                o = dma("sp", outs["bk"][gi * 128: gi * 128 + nr, hk * 512:(hk + 1) * 512],
                                        stage[si][0:nr, 0:512], [R_stage[si]], [Res()], ch_stage[si])
                                R_out.append(o)
                            tok_major(R_h, T, kslot, 0, 512, None, kout)
                    h_from_x(g, 2, T)
                elif l < 3:
                    layer_norm(g, l, 1, T, FT[:, g, l, 1, 0, :], FT[:, g, l, 1, 1, :], True)
                else:
                    layer_norm(g, l, 1, T, None, None, False)
            ngr = (T + 127) // 128
            for gi in range(ngr):
                nr = min(128, T - gi * 128)
                si = gi % 2
                for half in range(2):
                    b = next_d()
                    for cc in range(4):
                        c = half * 4 + cc
                        tr(PS[b][0:nr, cc * 128:(cc + 1) * 128], xT[:, c, gi * 128: gi * 128 + nr], ident[:],
                           [R_x[c], R_const], [R_ps[b]])
                    eng = "act" if half == 0 else "dve"
                    cpy(eng, stage[si][0:nr, half * 512:(half + 1) * 512], PS[b][0:nr, 0:512], [R_ps[b]], [R_stage[si]])
                o = dma("sp", ydst[gi * 128: gi * 128 + nr, :], stage[si][0:nr, :], [R_stage[si]], [Res()], ch_stage[si])
                R_out.append(o)

        def load_sample_caches():
            for la in range(2):
                st = stage[0]
                stv = st[:, 0:512].rearrange("p (g u d) -> p g u d", g=4, u=2)
                for dup in range(2):
                    dma("sp", stv[:, :, dup, :], cache_a_k[la].rearrange("k (g d) -> k g d", g=4), (), [R_stage[0]],
                        ch_stage[0])
                for g4 in range(4):
                    b = next_d()
                    tr(PS[b][:, 0:128], st[:, g4 * 128:(g4 + 1) * 128], ident[:], [R_stage[0], R_const], [R_ps[b]])
                    cpy("dve", KA[la][:, g4, 0:128], PS[b][:, 0:128], [R_ps[b]], [R_ka[la][0]])
                dma("sp", stage[1][:, 0:256], cache_a_v[la], (), [R_stage[1]], ch_stage[1])
                cpy("dve", VA[la][:, 0, :], stage[1][:, 0:256], [R_stage[1]], [R_va[la][0]])
            for tl in range(4):
                si = tl % 2
                dma("sp", stage[si][:, :], cache_b_k[tl * 128:(tl + 1) * 128, :], (), [R_stage[si]], ch_stage[si])
                for half in range(2):
                    b = next_d()
                    for cc in range(4):
                        c = half * 4 + cc
                        tr(PS[b][:, cc * 128:(cc + 1) * 128], stage[si][:, c * 128:(c + 1) * 128], ident[:],
                           [R_stage[si], R_const], [R_ps[b]])
                    cpy("dve" if half else "act", KB[:, 0, half * 4:(half + 1) * 4, tl * 128:(tl + 1) * 128],
                        PS[b][:, :].rearrange("p (c t) -> p c t", c=4), [R_ps[b]], [R_kb[0][tl]])
            for tl in range(4):
                si = tl % 2
                dma("sp", stage[si][:, :], cache_b_v[tl * 128:(tl + 1) * 128, :], (), [R_stage[si]], ch_stage[si])
                cpy("dve", VB[:, 0, tl, :], stage[si][:, :], [R_stage[si]], [R_vb[0][tl]])
            for l in range(4):
                for r in range(2):
                    src = bass.AP(state_conv.tensor, l * 2 * 2 * DFF + r * 2 * DFF, [[1, 128], [128, NCH]])
                    dma("sp", carry[:, l, :, r], src, (), [R_carry[l][c] for c in range(NCH)], ch_misc, slow=True)

        import os as _os
        KSTOP = int(_os.environ.get("KSTOP", "99"))
        if KSTOP >= 1:
            emit_casts()
        if KSTOP >= 2:
            setup()
        P.epoch = 1
        for t in range(NT if KSTOP >= 5 else 0):
            P.epoch = 2 + t
            run_tile(0, TT, t, x_prompt[t * TT:(t + 1) * TT, :], y_prompt[t * TT:(t + 1) * TT, :], t == NT - 1,
                     {"ak": p_ak, "av": p_av, "bk": p_bk, "bv": p_bv, "conv": p_conv})
        if KSTOP >= 3:
            load_sample_caches()
        if KSTOP >= 4:
            run_tile(1, DEC, -1, x_sample, y_sample, False,
                     {"ak": s_ak, "av": s_av, "bk": s_bk, "bv": s_bv, "conv": s_conv})
        n_epochs = NT + 2

        eng_sems = {}
        for e in ENGS:
            users = set()
            for op in P.ops[e]:
                if op.dma is None:
                    users |= op.sigs
            for c in sorted(users):
                eng_sems[(e, c)] = new_sem(f"s_{e}_{c}")
        for e in ENGS:
            cnt = {}
            for op in P.ops[e]:
                if op.dma is None:
                    for c in op.sigs:
                        cnt[c] = cnt.get(c, 0) + 1
                        op.cnts[c] = cnt[c]
            print("SEMCNT", e, cnt)
            for op in P.ops[e]:
                if op.dma is None and len(op.sigs) > 1:
                    print("MULTISIG", e, op.pos, op.epoch, op.sigs)
                    break
        out_ops = list(R_out)

        if _os.environ.get("DUMPW"):
            for en in ENGS:
                for op in P.ops[en][-6:]:
                    print("TAIL", en, op.pos, op.epoch, "sig", op.sig, op.cnt, "dma", None if op.dma is None else op.dmaval,
                          [(d.eng, d.epoch, d.pos, d.cnt if d.dma is None else ("dma", d.dmaval)) for d in op.waits])

        def replay(eng_name, e):
            for op in P.ops[eng_name]:
                for d in op.waits:
                    if d.dma is not None:
                        e.wait_ge(d.dma.sem, d.dmaval)
                    else:
                        e.wait_ge(eng_sems[(d.eng, eng_name)], d.cnts[eng_name])
                if op.fn is None:
                    continue
                ins = op.fn(e)
                if op.dma is not None:
                    ins.then_inc(op.dma.sem, 16)
                else:
                    for c in sorted(op.sigs):
                        ins.then_inc(eng_sems[(op.eng, c)], 1)
            if eng_name == "sp":
                done = {}
                for o in out_ops:
                    if o.dma is None:
                        continue
                    key = id(o.dma)
                    if key not in done or done[key][1] < o.dmaval:
                        done[key] = (o.dma, o.dmaval)
                for chn, val in done.values():
                    e.wait_ge(chn.sem, val)

        with nc.Block() as block:
            @block.tensor
            def _(e):
                replay("pe", e)

            @block.scalar
            def _(e):
                replay("act", e)

            @block.vector
            def _(e):
                replay("dve", e)

            @block.gpsimd
            def _(e):
                replay("pool", e)

            @block.sync
            def _(e):
                replay("sp", e)
    stats = {e: len(P.ops[e]) for e in ENGS}
    print("SBUF remaining", nc.sbuf_bytes_remaining)
    return nc, stats


_CACHE = {}


def _host_inputs(inputs, NT):
    f = lambda a: np.ascontiguousarray(np.asarray(a, dtype=np.float32))
    shared = {}
    for k in ("w_ada", "w_qkv_a", "w_o_a", "sinks_a", "t5_table", "w_ada_kv", "w_kv_b", "w_q_b", "w_o_b",
              "relpos_b", "w_up", "w_down"):
        shared[k] = f(inputs[k])
    shared["ident"] = np.eye(128, dtype=np.float32)
    shared["anti"] = np.ascontiguousarray(np.eye(128, dtype=np.float32)[::-1])
    shared["t5oh"] = _t5_onehot()
    in_maps = []
    for b in range(8):
        vec = np.zeros((V_ROWS, 128), np.float32)
        vec[V_BADA:V_BADA + 192] = f(inputs["b_ada"]).reshape(192, 128)
        vec[V_BKV:V_BKV + 16] = f(inputs["b_ada_kv"]).reshape(16, 128)
        vec[V_LNG:V_LNG + 64] = f(inputs["ln_g"]).reshape(64, 128)
        vec[V_LNB:V_LNB + 64] = f(inputs["ln_b"]).reshape(64, 128)
        vec[V_CW:V_CW + 528] = f(inputs["conv_w"]).reshape(528, 128)
        vec[V_CB:V_CB + 176] = f(inputs["conv_b"]).reshape(176, 128)
        vec[V_C:V_C + 8] = f(inputs["c_prompt"])[b].reshape(8, 128)
        vec[V_C + 8:V_C + 16] = f(inputs["c_sample"])[b].reshape(8, 128)
        m = dict(shared)
        m["vecs"] = vec
        m["x_prompt"] = f(inputs["x_prompt"][b])
        m["x_sample"] = f(inputs["x_sample"][b])
        m["cache_a_k"] = f(inputs["cache_a_k"][:, b]).reshape(2, 128, 256)
        m["cache_a_v"] = f(inputs["cache_a_v"][:, b]).reshape(2, 128, 256)
        m["cache_b_k"] = f(inputs["cache_b_k"][b]).reshape(512, 1024)
        m["cache_b_v"] = f(inputs["cache_b_v"][b]).reshape(512, 1024)
        m["state_conv"] = f(inputs["state_conv"][:, b])
        in_maps.append(m)
    return in_maps


def kernel(**inputs):
    S = int(np.asarray(inputs["x_prompt"]).shape[1])
    NT = S // TT
    if NT not in _CACHE:
        _CACHE[NT] = build(NT)[0]
    nc = _CACHE[NT]
    in_maps = _host_inputs(inputs, NT)
    res = run_bass_kernel_spmd(nc, in_maps, core_ids=list(range(8)))
    R = res.results
    st = lambda k: np.stack([np.asarray(r[k], dtype=np.float32) for r in R], axis=0)
    y_prompt = st("y_prompt")
    y_sample = st("y_sample")
    rows_a = min(128, S)
    rows_b = min(512, S)
    p_ak = np.ascontiguousarray(st("p_ak").transpose(1, 0, 2, 3)).reshape(2, 8, 128, 4, 64)[:, :, 128 - rows_a:]
    p_av = np.ascontiguousarray(st("p_av").transpose(1, 0, 2, 3)).reshape(2, 8, 128, 4, 64)[:, :, 128 - rows_a:]
    p_bk = st("p_bk").reshape(8, 512, 16, 64)[:, 512 - rows_b:]
    p_bv = st("p_bv").reshape(8, 512, 16, 64)[:, 512 - rows_b:]
    p_conv = np.ascontiguousarray(st("p_conv").transpose(1, 0, 2, 3))
    s_ak = np.ascontiguousarray(st("s_ak").transpose(1, 0, 2, 3)).reshape(2, 8, DEC, 4, 64)
    s_av = np.ascontiguousarray(st("s_av").transpose(1, 0, 2, 3)).reshape(2, 8, DEC, 4, 64)
    s_bk = st("s_bk").reshape(8, DEC, 16, 64)
    s_bv = st("s_bv").reshape(8, DEC, 16, 64)
    s_conv = np.ascontiguousarray(st("s_conv").transpose(1, 0, 2, 3))
    return (y_prompt, y_sample, p_ak, p_av, p_bk, p_bv, p_conv, s_ak, s_av, s_bk, s_bv, s_conv)
```

```python
import math
import contextlib
import numpy as np
import concourse.bass as bass
import concourse.mybir as mybir
from concourse.bass_utils import run_bass_kernel_spmd

F32 = mybir.dt.float32
BF16 = mybir.dt.bfloat16
AF = mybir.ActivationFunctionType
ALU = mybir.AluOpType

D = 1024
C8 = 8
TT = 512
DFF = 2816
FC = 22
NCH = 44
DEPTH = 4
DEC = 16
ALPHA = (2.0 * DEPTH) ** 0.25
LN_EPS = 1e-5
EPS_P = LN_EPS / (ALPHA * ALPHA)
SCALE = 64 ** -0.5
UP_ORDER = list(range(22, 44)) + list(range(0, 22))

V_BADA = 0
V_BKV = 192
V_LNG = 208
V_LNB = 272
V_CW = 336
V_CB = 864
V_C = 1040
V_ROWS = 1152


class Res:
    __slots__ = ("w", "r", "rd")

    def __init__(self):
        self.w = None
        self.r = {}
        self.rd = []


class Chan:
    def __init__(self, sem):
        self.sem = sem
        self.count = 0


class Op:
    __slots__ = ("eng", "fn", "waits", "sig", "pos", "cnt", "dma", "dmaval", "epoch", "sigs", "cnts", "waiter")


ENGS = ["pe", "act", "dve", "pool", "sp"]
import os as _osm
MAXSIG = int(_osm.environ.get("MAXSIG", "1"))


class Prog:
    def __init__(self):
        self.ops = {e: [] for e in ENGS}
        self.seen = {e: {} for e in ENGS}
        self.epoch = 0

    def emit(self, eng, fn, reads=(), writes=(), chan=None):
        op = Op()
        op.eng = eng
        op.fn = fn
        op.pos = len(self.ops[eng])
        op.sig = False
        op.sigs = set()
        op.cnts = {}
        op.waiter = {}
        op.dma = chan
        op.epoch = self.epoch
        op.cnt = 0
        if chan is not None:
            chan.count += 16
            op.dmaval = chan.count
        deps = []
        for r in reads:
            if r.w is not None:
                deps.append((r.w, True))
        for w in writes:
            if w.w is not None:
                deps.append((w.w, False))
            for rd in w.r.values():
                deps.append((rd, False))
            for rd in w.rd:
                deps.append((rd, False))
        waits = []
        seen = self.seen[eng]
        for d, raw in deps:
            if d is op:
                continue
            if d.dma is not None:
                key = ("dma", id(d.dma))
                if seen.get(key, -1) >= d.dmaval:
                    continue
                seen[key] = d.dmaval
                waits.append(d)
                continue
            if d.eng == eng:
                if eng == "pe":
                    continue
            hops = 0
            while d.dma is None and eng not in d.sigs and len(d.sigs) >= MAXSIG:
                plst = self.ops[d.eng]
                zf = None
                for k in range(d.pos + 1, min(len(plst), d.pos + 13)):
                    z = plst[k]
                    if z.dma is None and z.fn is not None and (eng in z.sigs or len(z.sigs) == 0):
                        zf = z
                        break
                if zf is not None and zf.eng != eng:
                    d = zf
                    break
                c0 = sorted(d.sigs)[0]
                w0 = d.waiter[c0]
                hops += 1
                if w0.eng == eng or w0.dma is not None:
                    d = w0
                    break
                lst = self.ops[c0]
                found = None
                for k in range(w0.pos, min(len(lst), w0.pos + 400)):
                    z = lst[k]
                    if z.dma is None and z.fn is not None and (eng in z.sigs or len(z.sigs) == 0):
                        found = z
                        break
                if found is not None:
                    d = found
                    break
                d = lst[min(len(lst), w0.pos + 400) - 1]
                if d.dma is not None or d.fn is None:
                    d = w0
            if d.dma is not None:
                key = ("dma", id(d.dma))
                if seen.get(key, -1) >= d.dmaval:
                    continue
                seen[key] = d.dmaval
                waits.append(d)
                continue
            if d.eng == eng and (eng == "pe" or d is op):
                continue
            if d.eng == eng and hops > 0:
                continue
            key = d.eng
            if seen.get(key, -1) >= d.pos:
                continue
            seen[key] = d.pos
            d.sig = True
            d.sigs.add(eng)
            if eng not in d.waiter:
                d.waiter[eng] = op
            waits.append(d)
        op.waits = waits
        for r in reads:
            if chan is not None:
                r.rd.append(op)
            else:
                r.r[eng] = op
        for w in writes:
            w.w = op
            w.r = {}
            w.rd = []
        self.ops[eng].append(op)
        return op


def _t5_onehot():
    try:
        import jax
        import jax.numpy as jnp
        cpu = jax.devices("cpu")[0]
        with jax.default_device(cpu):
            rel = jnp.asarray(np.concatenate([(-128 + 127 - np.arange(255)), (127 - np.arange(255))]).astype(np.int32))
            nb = 16
            max_exact = 8
            n = jnp.abs(rel)
            large = max_exact + (jnp.log(jnp.maximum(n, 1).astype(jnp.float32) / max_exact)
                                 / math.log(128 / max_exact) * (nb - max_exact)).astype(jnp.int32)
            large = jnp.minimum(large, nb - 1)
            bucket = jnp.where(rel > 0, nb, 0) + jnp.where(n < max_exact, n, large)
            bucket = np.asarray(bucket)
    except Exception:
        rel = np.concatenate([(-128 + 127 - np.arange(255)), (127 - np.arange(255))]).astype(np.int32)
        n = np.abs(rel)
        large = 8 + (np.log(np.maximum(n, 1).astype(np.float32) / np.float32(8))
                     / np.float32(math.log(16.0)) * np.float32(8)).astype(np.int32)
        large = np.minimum(large, 15)
        bucket = np.where(rel > 0, 16, 0) + np.where(n < 8, n, large)
    oh = np.zeros((32, 510), np.float32)
    oh[bucket, np.arange(510)] = 1.0
    return oh


def build(NT):
    S = NT * TT
    nc = bass.Bass("TRN2", target_bir_lowering=False)
    P = Prog()

    def din(name, shape, dt=F32):
        return nc.dram_tensor(name, list(shape), dt, kind="ExternalInput").ap()

    def dout(name, shape, dt=F32):
        return nc.dram_tensor(name, list(shape), dt, kind="ExternalOutput").ap()

    def dscr(name, shape, dt=BF16):
        return nc.dram_tensor(name, list(shape), dt, kind="Internal").ap()

    x_prompt = din("x_prompt", [S, D])
    x_sample = din("x_sample", [DEC, D])
    cache_a_k = din("cache_a_k", [2, 128, 256])
    cache_a_v = din("cache_a_v", [2, 128, 256])
    cache_b_k = din("cache_b_k", [512, 1024])
    cache_b_v = din("cache_b_v", [512, 1024])
    state_conv = din("state_conv", [4, 2, 2 * DFF])
    w_ada = din("w_ada", [4, D, 6 * D])
    w_qkv_a = din("w_qkv_a", [2, D, 1536])
    w_o_a = din("w_o_a", [2, D, D])
    sinks_a = din("sinks_a", [2, 16])
    t5_table = din("t5_table", [32, 16])
    w_ada_kv = din("w_ada_kv", [D, 2 * D])
    w_kv_b = din("w_kv_b", [D, 2 * D])
    w_q_b = din("w_q_b", [2, D, D])
    w_o_b = din("w_o_b", [2, D, D])
    relpos_b = din("relpos_b", [2, 16, 257])
    w_up = din("w_up", [4, D, 2 * DFF])
    w_down = din("w_down", [4, DFF, D])
    vecs = din("vecs", [V_ROWS, 128])
    ident_d = din("ident", [128, 128])
    anti_d = din("anti", [128, 128])
    onehot_d = din("t5oh", [32, 510])

    y_prompt = dout("y_prompt", [S, D])
    y_sample = dout("y_sample", [DEC, D])
    p_ak = dout("p_ak", [2, 128, 256])
    p_av = dout("p_av", [2, 128, 256])
    p_bk = dout("p_bk", [512, 1024])
    p_bv = dout("p_bv", [512, 1024])
    p_conv = dout("p_conv", [4, 2, 2 * DFF])
    s_ak = dout("s_ak", [2, DEC, 256])
    s_av = dout("s_av", [2, DEC, 256])
    s_bk = dout("s_bk", [DEC, 1024])
    s_bv = dout("s_bv", [DEC, 1024])
    s_conv = dout("s_conv", [4, 2, 2 * DFF])

    sc_up = [dscr(f"sc_up{l}", [11, 128, 8 * 512]) for l in range(4)]
    sc_dn = [dscr(f"sc_dn{l}", [8, 128, 22 * 128]) for l in range(4)]
    sc_qa = [dscr(f"sc_qa{l}", [4, 128, 8 * 512]) for l in range(2)]
    sc_oa = [dscr(f"sc_oa{l}", [2, 128, 8 * 512]) for l in range(2)]
    sc_qb = [dscr(f"sc_qb{l}", [2, 128, 8 * 512]) for l in range(2)]
    sc_ob = [dscr(f"sc_ob{l}", [2, 128, 8 * 512]) for l in range(2)]
    sc_kvb = dscr("sc_kvb", [4, 128, 8 * 512])
    sc_veca = dscr("sc_veca", [16, 510], F32)
    sc_vecb = [dscr(f"sc_vecb{j}", [16, 510], F32) for j in range(2)]
    sc_ebb = [dscr(f"sc_ebb{j}", [128, 4096]) for j in range(2)]

    es = contextlib.ExitStack()
    with es:
        def sb(name, shape, dt):
            return es.enter_context(nc.sbuf_tensor(name, list(shape), dt))

        def ps(name):
            return es.enter_context(nc.psum_tensor(name, [128, 512], F32))

        xT = sb("xT", [128, C8, TT], F32)
        hT = sb("hT", [128, C8, TT], BF16)
        actT = sb("actT", [128, FC, TT], BF16)
        KB = sb("KB", [128, 2, C8, TT], BF16)
        VB = sb("VB", [128, 2, 4, D], BF16)
        KA = [sb(f"KA{l}", [128, 4, 5 * 128], BF16) for l in range(2)]
        VA = [sb(f"VA{l}", [128, 5, 256], BF16) for l in range(2)]
        Ubuf = [[sb(f"U{a}{b}", [128, TT + 2], F32) for b in range(2)] for a in range(2)]
        Ybuf = [[sb(f"Y{a}{b}", [128, TT], F32) for b in range(2)] for a in range(2)]
        Ebuf = [sb(f"E{i}", [128, 512], BF16) for i in range(5)]
        E0buf = sb("E0m", [128, 512], BF16)
        Tbuf = sb("Tb", [128, 512], F32)
        R2buf = sb("R2", [128, 512], F32)
        ybf = [sb(f"ybf{i}", [128, TT], BF16) for i in range(2)]
        sqb = [sb(f"sq{i}", [128, TT], BF16) for i in range(2)]
        mean_sb = sb("mean", [128, TT], F32)
        var_sb = sb("var", [128, TT], F32)
        rstd_sb = sb("rstd", [128, TT], F32)
        tmpb = [sb(f"tmp{i}", [128, TT], F32) for i in range(2)]
        EBA = sb("EBA", [128, 2, 2, 8, 128], BF16)
        EBB = sb("EBB", [128, 2, 2, 8, 128], BF16)
        W8 = [sb(f"W8_{i}", [128, 8 * 512], BF16) for i in range(3)]
        W22 = [sb(f"W22_{i}", [128, 22 * 128], BF16) for i in range(2)]
        stage = [sb(f"stg{i}", [128, D], F32) for i in range(2)]
        ident = sb("identS", [128, 128], F32)
        anti = sb("antiS", [128, 128], F32)
        ones_bf = sb("ones_bf", [128, 64], BF16)
        onesf = sb("onesf", [128, 128], BF16)
        VF = sb("VF", [128, V_ROWS], F32)
        SCt = sb("SCt", [128, 16], F32)
        ADA = sb("ADA", [128, 2, 208], F32)
        ADA1 = sb("ADA1", [128, 2, 208], F32)
        ADAG = sb("ADAG", [128, 2, 208], F32)
        FT = sb("FT", [128, 2, 4, 2, 2, 8], F32)
        FKV = sb("FKV", [128, 2, 2, 8], F32)
        carry = sb("carry", [128, 4, NCH, 2], F32)
        esink = sb("esink", [128, 2, 8], F32)
        t5_sb = sb("t5_sb", [32, 16], F32)
        tokA = tmpb[0][0:2, 0:256]
        oh_sb = mean_sb[0:32, 0:510]
        vec_sb = var_sb[0:16, 0:510]
        rp_sb = rstd_sb[0:16, 0:257]

        print("SBUF bytes remaining after alloc", nc.sbuf_bytes_remaining)
        PS = [ps(f"ps{i}") for i in range(8)]
        D0, D1, S0, S1, O0, O1, N0, N1 = range(8)

        sems = {}
        sem_list = []

        def new_sem(name):
            s = es.enter_context(nc.semaphore(name))
            sem_list.append(s)
            return s

        def new_chan(name):
            return Chan(new_sem(name))

        R_x = [Res() for _ in range(C8)]
        R_h = [Res() for _ in range(C8)]
        R_act = [Res() for _ in range(FC)]
        R_ps = [Res() for _ in range(8)]
        R_w8 = [Res() for _ in range(3)]
        R_w22 = [Res() for _ in range(2)]
        R_ka = [[Res() for _ in range(5)] for _ in range(2)]
        R_va = [[Res() for _ in range(5)] for _ in range(2)]
        R_kb = [[Res() for _ in range(4)] for _ in range(2)]
        R_vb = [[Res() for _ in range(4)] for _ in range(2)]
        R_u = [[Res() for _ in range(2)] for _ in range(2)]
        R_uh = [[Res() for _ in range(2)] for _ in range(2)]
        R_y = [[Res() for _ in range(2)] for _ in range(2)]
        R_e = [Res() for _ in range(5)]
        R_e0 = Res()
        R_t = Res()
        R_r2 = Res()
        R_ybf = [Res(), Res()]
        R_sq = [Res(), Res()]
        R_mean = Res()
        R_var = Res()
        R_rstd = Res()
        R_tmp = [Res(), Res()]
        R_eba = Res()
        R_ebb = Res()
        R_stage = [Res(), Res()]
        R_const = Res()
        R_vf = Res()
        R_sc = Res()
        R_ada = Res()
        R_tabs = Res()
        R_carry = [[Res() for _ in range(NCH)] for _ in range(4)]
        R_esink = Res()
        R_tok = Res()
        R_misc = Res()
        R_scr = {}
        R_out = []

        ch_w8 = [new_chan(f"cw8_{i}") for i in range(3)]
        ch_w22 = [new_chan(f"cw22_{i}") for i in range(2)]
        ch_stage = [new_chan(f"cst{i}") for i in range(2)]
        ch_misc = new_chan("cmisc")
        ch_cast = [new_chan(f"ccast{i}") for i in range(4)]
        R_castq = [Res() for _ in range(4)]
        R_miscq = Res()
        cast_key_idx = {}
        ch_ebb = new_chan("cebb")

        def mm(out, lhsT, rhs, start, stop, reads, writes, skip=False):
            if skip:
                return P.emit("pe", lambda e: e.matmul(out, lhsT, rhs, start=start, stop=stop, skip_group_check=True),
                              reads, writes)
            return P.emit("pe", lambda e: e.matmul(out, lhsT, rhs, start=start, stop=stop), reads, writes)

        def tr(out, in_, idn, reads, writes):
            return P.emit("pe", lambda e: e.transpose(out, in_, idn), reads, writes)

        def act(out, in_, func, reads, writes, bias=0.0, scale=1.0):
            return P.emit("act", lambda e: e.activation(out, in_, func, bias=bias, scale=scale), reads, writes)

        def tsc(eng, out, in0, s1, s2, op0, op1, reads, writes):
            if op1 is None:
                return P.emit(eng, lambda e: e.tensor_scalar(out, in0, s1, None, op0), reads, writes)
            return P.emit(eng, lambda e: e.tensor_scalar(out, in0, s1, s2, op0, op1), reads, writes)

        def stt(eng, out, in0, scalar, in1, op0, op1, reads, writes):
            return P.emit(eng, lambda e: e.scalar_tensor_tensor(out, in0, scalar, in1, op0, op1), reads, writes)

        def ttn(eng, out, in0, in1, op, reads, writes):
            return P.emit(eng, lambda e: e.tensor_tensor(out, in0, in1, op), reads, writes)

        def cpy(eng, out, in_, reads, writes):
            if eng == "act":
                return P.emit("act", lambda e: e.activation(out, in_, AF.Identity), reads, writes)
            return P.emit(eng, lambda e: e.tensor_copy(out, in_), reads, writes)

        def mset(eng, ap, val, writes):
            return P.emit(eng, lambda e: e.memset(ap, val), (), writes)

        import os as _os0
        NOOUT = _os0.environ.get("NOOUT", "0") == "1"
        out_names = ("y_prompt", "y_sample", "p_ak", "p_av", "p_bk", "p_bv", "p_conv", "s_ak", "s_av", "s_bk",
                     "s_bv", "s_conv")

        def dma(q, out, in_, reads, writes, chan, slow=False):
            if NOOUT and out.tensor.name in out_names:
                op_ = Op()
                op_.dma = None
                return op_
            if chan is ch_misc:
                writes = list(writes) + [R_miscq]
            if slow:
                return P.emit(q, lambda e: e.dma_start(out=out, in_=in_, allow_slow_non_contiguous=True),
                              reads, writes, chan)
            return P.emit(q, lambda e: e.dma_start(out=out, in_=in_), reads, writes, chan)

        def scr_res(key):
            if key not in R_scr:
                R_scr[key] = Res()
            return R_scr[key]

        def cast(dst, src, key):
            if key not in cast_key_idx:
                cast_key_idx[key] = len(cast_key_idx) % 4
            ci = cast_key_idx[key]
            dma("pool", dst, src, (), [scr_res(key), R_castq[ci]], ch_cast[ci])

        def cast_cols(scr, slot, w2d, col0, key):
            src = w2d[:, col0:col0 + 512].rearrange("(kc p) c -> p kc c", p=128)
            dst = scr[slot].rearrange("p (kc c) -> p kc c", kc=8)
            cast(dst, src, key)

        def emit_casts():
            for l in range(4):
                if l < 2:
                    for s in range(2):
                        cast_cols(sc_qa[l], s, w_qkv_a[l], s * 512, ("qa", l, s))
                    for dup in range(2):
                        for kc in range(8):
                            src = w_qkv_a[l][kc * 128:(kc + 1) * 128, 1024:1280].rearrange("p (g d) -> p g d", g=4)
                            dst = sc_qa[l][2].rearrange("p (kc g u d) -> p kc g u d", kc=8, g=4, u=2)[:, kc, :, dup, :]
                            cast(dst, src, ("qa", l, 2))
                    cast_cols(sc_qa[l], 3, w_qkv_a[l], 1024, ("qa", l, 3))
                    for s in range(2):
                        cast_cols(sc_oa[l], s, w_o_a[l], s * 512, ("oa", l, s))
                else:
                    j = l - 2
                    for s in range(2):
                        cast_cols(sc_qb[j], s, w_q_b[j], s * 512, ("qb", j, s))
                    for s in range(2):
                        cast_cols(sc_ob[j], s, w_o_b[j], s * 512, ("ob", j, s))
                for s in range(11):
                    chs = UP_ORDER[4 * s:4 * s + 4]
                    if chs[3] == chs[0] + 3:
                        cast_cols(sc_up[l], s, w_up[l], chs[0] * 128, ("up", l, s))
                    else:
                        for hh in range(2):
                            c0 = chs[2 * hh] * 128
                            src = w_up[l][:, c0:c0 + 256].rearrange("(kc p) c -> p kc c", p=128)
                            dst = sc_up[l][s].rearrange("p (kc c) -> p kc c", kc=8)[:, :, hh * 256:(hh + 1) * 256]
                            cast(dst, src, ("up", l, s))
                for m in range(8):
                    src = w_down[l][:, m * 128:(m + 1) * 128].rearrange("(kc p) c -> p kc c", p=128)
                    dst = sc_dn[l][m].rearrange("p (kc c) -> p kc c", kc=22)
                    cast(dst, src, ("dn", l, m))
                if l == 1:
                    for s in range(4):
                        cast_cols(sc_kvb, s, w_kv_b, s * 512, ("kvb", s))

        w8_n = [0]
        w22_n = [0]

        def load_w8(scr_ap, key):
            i = w8_n[0] % 3
            w8_n[0] += 1
            dma("sp", W8[i][:], scr_ap, [scr_res(key)], [R_w8[i]], ch_w8[i])
            return i

        def load_w22(scr_ap, key):
            i = w22_n[0] % 2
            w22_n[0] += 1
            dma("sp", W22[i][:], scr_ap, [scr_res(key)], [R_w22[i]], ch_w22[i])
            return i

        def w8v(i):
            return W8[i][:].rearrange("p (kc c) -> p kc c", kc=8)

        def setup():
            dma("sp", ident[:], ident_d[:, :], (), [R_const], ch_misc)
            dma("sp", anti[:], anti_d[:, :], (), [R_const], ch_misc)
            dma("sp", oh_sb, onehot_d[:, :], (), [R_mean], ch_misc)
            dma("sp", t5_sb[:], t5_table[:, :], (), [R_const], ch_misc)
            mset("dve", ones_bf[:], 1.0, [R_const])
            mset("dve", onesf[:], 1.0 / 1024.0, [R_const])
            mset("pool", carry[:], 0.0, [R_carry[l][c] for l in range(4) for c in range(NCH)])
            mset("pool", E0buf[:], 0.0, [R_e0])
            mset("pool", hT[:], 0.0, R_h)
            mset("pool", KB[:], 0.0, R_kb[0] + R_kb[1])
            for la_ in range(2):
                mset("pool", KA[la_][:], 0.0, R_ka[la_])
            for r in range(V_ROWS // 128):
                st = stage[r % 2]
                dma("sp", st[:, 0:128], vecs[r * 128:(r + 1) * 128, :], (), [R_stage[r % 2]], ch_stage[r % 2])
                b = D0 + (r % 2)
                tr(PS[b][:, 0:128], st[:, 0:128], ident[:], [R_stage[r % 2], R_const], [R_ps[b]])
                cpy("dve", VF[:, r * 128:(r + 1) * 128], PS[b][:, 0:128], [R_ps[b]], [R_vf])
            act(SCt[:], VF[:, V_C:V_C + 16], AF.Silu, [R_vf], [R_sc])
            blocks = [(w_ada[l], cb, l * 48 + cb * 2) for l in range(4) for cb in range(24)]
            blocks += [(w_ada_kv, cb, 192 + cb * 2) for cb in range(8)]
            for bi, (w2d, cb, col) in enumerate(blocks):
                i = w8_n[0] % 3
                w8_n[0] += 1
                wv = W8[i][:].bitcast(F32).rearrange("p (kc c) -> p kc c", kc=8)
                src = w2d[:, cb * 256:(cb + 1) * 256].rearrange("(kc p) c -> p kc c", p=128)
                dma("sp", wv, src, (), [R_w8[i]], ch_w8[i])
                b = D0 + (bi % 2)
                for kc in range(8):
                    mm(PS[b][0:2, 0:256], SCt[:, kc:16:8], wv[:, kc, :], kc == 0, kc == 7,
                       [R_sc, R_w8[i]], [R_ps[b]])
                cpy("act", tokA, PS[b][0:2, 0:256], [R_ps[b]], [R_tmp[0]])
                b2 = O0 + (bi % 2)
                for hh in range(2):
                    tr(PS[b2][:, hh * 2:hh * 2 + 2], tokA[0:2, hh * 128:(hh + 1) * 128], ident[0:2, 0:2],
                       [R_tmp[0], R_const], [R_ps[b2]])
                cpy("dve", ADA[:, :, col:col + 2].rearrange("p g c -> p c g"),
                    PS[b2][:, 0:4].rearrange("p (c g) -> p c g", c=2), [R_ps[b2]], [R_ada])
            for g in range(2):
                ttn("dve", ADA[:, g, :], ADA[:, g, :], VF[:, 0:208], ALU.add, [R_ada, R_vf], [R_ada])
            tsc("dve", ADA1[:], ADA[:], 1.0, None, ALU.add, None, [R_ada], [R_tabs])
            tsc("dve", ADAG[:], ADA[:], 1.0, 1.0 / ALPHA, ALU.add, ALU.mult, [R_ada], [R_tabs])
            for g in range(2):
                for l in range(4):
                    g1 = VF[:, V_LNG + (l * 2) * 8: V_LNG + (l * 2) * 8 + 8]
                    b1 = VF[:, V_LNB + (l * 2) * 8: V_LNB + (l * 2) * 8 + 8]
                    a1 = ADA1[:, g, l * 48 + 32: l * 48 + 40]
                    sh = ADA[:, g, l * 48 + 24: l * 48 + 32]
                    ttn("dve", FT[:, g, l, 0, 0, :], g1, a1, ALU.mult, [R_vf, R_tabs], [R_tabs])
                    ttn("dve", FT[:, g, l, 0, 1, :], b1, a1, ALU.mult, [R_vf, R_tabs], [R_tabs])
                    ttn("dve", FT[:, g, l, 0, 1, :], FT[:, g, l, 0, 1, :], sh, ALU.add, [R_tabs, R_ada], [R_tabs])
                    if l < 3:
                        g2 = VF[:, V_LNG + (l * 2 + 1) * 8: V_LNG + (l * 2 + 1) * 8 + 8]
                        b2 = VF[:, V_LNB + (l * 2 + 1) * 8: V_LNB + (l * 2 + 1) * 8 + 8]
                        a1 = ADA1[:, g, (l + 1) * 48 + 8: (l + 1) * 48 + 16]
                        sh = ADA[:, g, (l + 1) * 48: (l + 1) * 48 + 8]
                        ttn("dve", FT[:, g, l, 1, 0, :], g2, a1, ALU.mult, [R_vf, R_tabs], [R_tabs])
                        ttn("dve", FT[:, g, l, 1, 1, :], b2, a1, ALU.mult, [R_vf, R_tabs], [R_tabs])
                        ttn("dve", FT[:, g, l, 1, 1, :], FT[:, g, l, 1, 1, :], sh, ALU.add, [R_tabs, R_ada], [R_tabs])
                g2 = VF[:, V_LNG + 3 * 8: V_LNG + 3 * 8 + 8]
                b2 = VF[:, V_LNB + 3 * 8: V_LNB + 3 * 8 + 8]
                a1 = ADA1[:, g, 200:208]
                sh = ADA[:, g, 192:200]
                ttn("dve", FKV[:, g, 0, :], g2, a1, ALU.mult, [R_vf, R_tabs], [R_tabs])
                ttn("dve", FKV[:, g, 1, :], b2, a1, ALU.mult, [R_vf, R_tabs], [R_tabs])
                ttn("dve", FKV[:, g, 1, :], FKV[:, g, 1, :], sh, ALU.add, [R_tabs, R_ada], [R_tabs])
            for l in range(2):
                for par in range(2):
                    src = bass.AP(sinks_a.tensor, l * 16 + par, [[0, 64], [2, 8]])
                    dma("sp", esink[par * 64:(par + 1) * 64, l, :], src, (), [R_esink], ch_misc, slow=True)
            act(esink[:], esink[:], AF.Exp, [R_esink], [R_esink])
            mm(PS[D0][0:16, 0:510], t5_sb[:], oh_sb, True, True, [R_const, R_mean], [R_ps[D0]])
            cpy("dve", vec_sb, PS[D0][0:16, 0:510], [R_ps[D0]], [R_var])
            dma("sp", sc_veca[:, :], vec_sb, [R_var], [scr_res("veca")], ch_misc)
            make_eb(sc_veca, "veca", EBA, R_eba)
            mset("pool", EBA[0:64, 0, :, :, 64:128], 0.0, [R_eba])
            mset("pool", EBA[64:128, 1, :, :, 0:64], 0.0, [R_eba])
            for j in range(2):
                dma("sp", rp_sb, relpos_b[j], (), [R_rstd], ch_misc)
                mset("dve", vec_sb, 0.0, [R_var])
                tsc("dve", vec_sb[:, 0:128], rp_sb[:, 129:257], rp_sb[:, 256:257], None, ALU.subtract, None,
                    [R_rstd, R_var], [R_var])
                tsc("dve", vec_sb[:, 255:510], rp_sb[:, 1:256], rp_sb[:, 256:257], None, ALU.subtract, None,
                    [R_rstd, R_var], [R_var])
                dma("sp", sc_vecb[j][:, :], vec_sb, [R_var], [scr_res(("vecb", j))], ch_misc)
                make_eb(sc_vecb[j], ("vecb", j), EBB, R_ebb)
                mset("pool", EBB[64:128, 1, :, :, 0:64], 0.0, [R_ebb])
                dma("sp", sc_ebb[j][:, :], EBB[:].rearrange("p a b c d -> p (a b c d)"), [R_ebb],
                    [scr_res(("ebb", j))], ch_misc)

        def make_eb(vec_dram, key, EB, R_eb):
            for ty in range(2):
                i = w8_n[0] % 3
                w8_n[0] += 1
                brev = W8[i][:].bitcast(F32).rearrange("p (h q) -> p h q", h=16)
                src = bass.AP(vec_dram.tensor, ty * 255, [[1, 128], [510, 16], [1, 128]])
                dma("sp", brev, src, [scr_res(key)], [R_w8[i]], ch_w8[i])
                for par in range(2):
                    for jg in range(2):
                        b = D0 + ((par * 2 + jg) % 2)
                        rhs = brev[:, par + 8 * jg: par + 8 * jg + 7: 2, :]
                        mm(PS[b][:, :], anti[:], rhs, True, True, [R_const, R_w8[i]], [R_ps[b]])
                        act(EB[:, ty, par, jg * 4:(jg + 1) * 4, :],
                            PS[b][:, :].rearrange("p (j q) -> p j q", j=4), AF.Exp, [R_ps[b]], [R_eb])

        def tab(g, l, part, c):
            col = l * 48 + part * 8 + c
            return col

        def load_x_group(src_rows, nrows, si):
            dma("sp", stage[si][0:nrows, :], src_rows, (), [R_stage[si]], ch_stage[si])

        def x_in(xsrc, T, g, first_groups_loaded=0):
            ngr = (T + 127) // 128
            for gi in range(ngr):
                nr = min(128, T - gi * 128)
                si = gi % 2
                if gi >= first_groups_loaded:
                    load_x_group(xsrc[gi * 128: gi * 128 + nr, :], nr, si)
                for half in range(2):
                    b = D0 + half
                    for cc in range(4):
                        c = half * 4 + cc
                        tr(PS[b][:, cc * 128: cc * 128 + nr], stage[si][0:nr, c * 128:(c + 1) * 128],
                           ident[0:nr, 0:nr], [R_stage[si], R_const], [R_ps[b]])
                    out = xT[:, half * 4:(half + 1) * 4, gi * 128: gi * 128 + nr]
                    inp = PS[b][:, :].rearrange("p (c t) -> p c t", c=4)[:, :, 0:nr]
                    if half == 0:
                        cpy("act", out, inp, [R_ps[b]], R_x[0:4])
                    else:
                        cpy("dve", out, inp, [R_ps[b]], R_x[4:8])

        def h_from_x(g, l, T):
            for c in range(C8):
                act(hT[:, c, 0:T], xT[:, c, 0:T], AF.Identity, [R_x[c], R_tabs], [R_h[c]],
                    bias=ADA[:, g, tab(g, l, 0, c): tab(g, l, 0, c) + 1],
                    scale=ADA1[:, g, tab(g, l, 1, c): tab(g, l, 1, c) + 1])

        dcount = [0]

        DRING = [D0, D1, S0, S1, O0, O1]

        def next_d():
            b = DRING[dcount[0] % len(DRING)]
            dcount[0] += 1
            return b

        def proj_fm(slot_keys, scr_list, KCn, src_fn, src_res, n_out, T, evac):
            cur = None
            first = 0
            if n_out >= 4:
                cur = load_w8(scr_list[0], slot_keys[0])
                wv = w8v(cur)
                bs = [next_d() for _ in range(4)]
                for kc in range(KCn):
                    for mi in range(4):
                        mm(PS[bs[mi]][:, 0:T], wv[:, kc, mi * 128:(mi + 1) * 128], src_fn(kc), kc == 0, kc == KCn - 1,
                           [R_w8[cur], src_res[kc]], [R_ps[bs[mi]]])
                for mi in range(4):
                    evac(mi, bs[mi])
                first = 4
            for m in range(first, n_out):
                if m % 4 == 0:
                    cur = load_w8(scr_list[m // 4], slot_keys[m // 4])
                wv = w8v(cur)
                b = next_d()
                for kc in range(KCn):
                    mm(PS[b][:, 0:T], wv[:, kc, (m % 4) * 128:(m % 4 + 1) * 128], src_fn(kc), kc == 0, kc == KCn - 1,
                       [R_w8[cur], src_res[kc]], [R_ps[b]])
                evac(m, b)

        def layer_norm(g, l, sub, T, ft1, ft2, h_out):
            for c in range(C8):
                i = c % 2
                cpy("dve", ybf[i][:, 0:T], xT[:, c, 0:T], [R_x[c]], [R_ybf[i]])
                act(sqb[i][:, 0:T], xT[:, c, 0:T], AF.Square, [R_x[c]], [R_sq[i]])
                mm(PS[D0][:, 0:T], onesf[:], ybf[i][:, 0:T], c == 0, c == 7, [R_const, R_ybf[i]], [R_ps[D0]])
                mm(PS[D1][:, 0:T], onesf[:], sqb[i][:, 0:T], c == 0, c == 7, [R_const, R_sq[i]], [R_ps[D1]])
            cpy("act", mean_sb[:, 0:T], PS[D0][:, 0:T], [R_ps[D0]], [R_mean])
            ttn("dve", var_sb[:, 0:T], mean_sb[:, 0:T], mean_sb[:, 0:T], ALU.mult, [R_mean], [R_var])
            ttn("dve", var_sb[:, 0:T], PS[D1][:, 0:T], var_sb[:, 0:T], ALU.subtract, [R_ps[D1], R_var], [R_var])
            tsc("dve", var_sb[:, 0:T], var_sb[:, 0:T], EPS_P, None, ALU.add, None, [R_var], [R_var])
            act(var_sb[:, 0:T], var_sb[:, 0:T], AF.Ln, [R_var], [R_var])
            act(rstd_sb[:, 0:T], var_sb[:, 0:T], AF.Exp, [R_var], [R_rstd], scale=-0.5)
            lng = V_LNG + (l * 2 + sub) * 8
            lnb = V_LNB + (l * 2 + sub) * 8
            for c in range(C8):
                i = c % 2
                eng = "pool" if c in (0, 3, 6) else "dve"
                ttn(eng, tmpb[i][:, 0:T], xT[:, c, 0:T], mean_sb[:, 0:T], ALU.subtract, [R_x[c], R_mean], [R_tmp[i]])
                ttn(eng, tmpb[i][:, 0:T], tmpb[i][:, 0:T], rstd_sb[:, 0:T], ALU.mult, [R_tmp[i], R_rstd], [R_tmp[i]])
                if h_out:
                    act(hT[:, c, 0:T], tmpb[i][:, 0:T], AF.Identity, [R_tmp[i], R_tabs], [R_h[c]],
                        bias=ft2[:, c:c + 1], scale=ft1[:, c:c + 1])
                act(xT[:, c, 0:T], tmpb[i][:, 0:T], AF.Identity, [R_tmp[i], R_vf], [R_x[c]],
                    bias=VF[:, lnb + c: lnb + c + 1], scale=VF[:, lng + c: lng + c + 1])

        def attention(l, g, T, pairs, tiles_of_pair, is_A):
            QT = actT
            OT = hT
            groups = []
            for pi, (q0, nq) in enumerate(pairs):
                kts = tiles_of_pair(pi)
                for jg in range(2):
                    for par in range(2):
                        for ti, kt in enumerate(kts):
                            groups.append((pi, q0, nq, jg, par, ti, len(kts), kt))
            nG = len(groups)
            state = {}
            pjcount = [0]

            def rec_S(gi):
                pi, q0, nq, jg, par, ti, nkt, kt = groups[gi]
                b = (S0, S1, D0, D1)[gi % 4]
                nk = kt["nk"]
                for jj in range(4):
                    j = jg * 4 + jj
                    mm(PS[b][0:max(nk, 32), jj * nq:(jj + 1) * nq], kt["kT"](par, j),
                       QT[par * 64:(par + 1) * 64, j, q0:q0 + nq],
                       True, True, kt["kres"] + [R_act[j]], [R_ps[b]])
                ty = kt["type"]
                if ty == "m0":
                    ebuf, eres = E0buf, R_e0
                else:
                    ebuf, eres = Ebuf[gi % 5], R_e[gi % 5]
                ev = ebuf[:, 0:4 * nq].rearrange("p (j q) -> p j q", j=4)
                sv = PS[b][:, 0:4 * nq].rearrange("p (j q) -> p j q", j=4)
                if ty == "m0":
                    n1 = min(nq, 64)
                    act(ev[0:nk, :, 0:n1], sv[0:nk, :, 0:n1], AF.Exp, [R_ps[b]], [eres], scale=SCALE)
                    if nq > 64:
                        act(ev[64:128, :, 64:nq], sv[64:128, :, 64:nq], AF.Exp, [R_ps[b]], [eres], scale=SCALE)
                else:
                    act(ebuf[0:nk, 0:4 * nq], PS[b][0:nk, 0:4 * nq], AF.Exp, [R_ps[b]], [eres], scale=SCALE)
                if ty in ("e0", "e1"):
                    t_i = 0 if ty == "e0" else 1
                    EB, R_eb = (EBA, R_eba) if is_A else (EBB, R_ebb)
                    eng = "dve"
                    ttn(eng, ev[0:nk, :, :], ev[0:nk, :, :], EB[0:nk, t_i, par, jg * 4:(jg + 1) * 4, 0:nq], ALU.mult,
                        [eres, R_eb], [eres])
                state[gi] = (ebuf, eres)

            def rec_DP(gi):
                pi, q0, nq, jg, par, ti, nkt, kt = groups[gi]
                ebuf, eres = state.pop(gi)
                nk = kt["nk"]
                pj = pjcount[0]
                ob = O0 + (pj % 2)
                nb = N0 + (pj % 2)
                first = ti == 0
                last = ti == nkt - 1
                mm(PS[nb][par * 64:(par + 1) * 64, 0:4 * nq], ones_bf[0:nk, :], ebuf[0:nk, 0:4 * nq], first, last,
                   [R_const, eres], [R_ps[nb]])
                for jj in range(4):
                    j = jg * 4 + jj
                    mm(PS[ob][par * 64:(par + 1) * 64, jj * nq:(jj + 1) * nq], kt["v"](par, j),
                       ebuf[0:nk, jj * nq:(jj + 1) * nq], first and jj == 0, last and jj == 3,
                       kt["vres"] + [eres], [R_ps[ob]], skip=True)
                if last and par == 1:
                    n4 = 4 * nq
                    if is_A:
                        ttn("dve", Tbuf[:, 0:n4].rearrange("p (j q) -> p j q", j=4),
                            PS[nb][:, 0:n4].rearrange("p (j q) -> p j q", j=4),
                            esink[:, l, jg * 4:(jg + 1) * 4].unsqueeze(2).to_broadcast([128, 4, nq]), ALU.add,
                            [R_ps[nb], R_esink], [R_t])
                        P.emit("dve", lambda e: e.reciprocal(R2buf[:, 0:n4], Tbuf[:, 0:n4]), [R_t], [R_r2])
                    else:
                        P.emit("dve", lambda e: e.reciprocal(R2buf[:, 0:n4], PS[nb][:, 0:n4]), [R_ps[nb]], [R_r2])
                    ttn("dve", OT[:, jg * 4:(jg + 1) * 4, q0:q0 + nq],
                        PS[ob][:, 0:n4].rearrange("p (j q) -> p j q", j=4),
                        R2buf[:, 0:n4].rearrange("p (j q) -> p j q", j=4), ALU.mult,
                        [R_ps[ob], R_r2], R_h[jg * 4:(jg + 1) * 4])
                    pjcount[0] += 1

            for gi in range(nG + 3):
                if gi < nG:
                    rec_S(gi)
                if gi >= 3:
                    rec_DP(gi - 3)

        def ffn(l, g, T, last_out):
            cur = None
            pending_gelu = []
            for oi, ch in enumerate(UP_ORDER):
                if oi % 4 == 0:
                    cur = load_w8(sc_up[l][oi // 4], ("up", l, oi // 4))
                wv = w8v(cur)
                b = next_d()
                for kc in range(8):
                    mm(PS[b][:, 0:T], wv[:, kc, (oi % 4) * 128:(oi % 4 + 1) * 128], hT[:, kc, 0:T], kc == 0, kc == 7,
                       [R_w8[cur], R_h[kc]], [R_ps[b]])
                isg = ch >= FC
                i = ch - FC if isg else ch
                bq = oi % 4
                U = Ubuf[bq // 2][bq % 2]
                Y = Ybuf[bq // 2][bq % 2]
                RU = R_u[bq // 2][bq % 2]
                RUH = R_uh[bq // 2][bq % 2]
                RY = R_y[bq // 2][bq % 2]
                w0 = VF[:, V_CW + (l * 3 + 0) * NCH + ch: V_CW + (l * 3 + 0) * NCH + ch + 1]
                w1 = VF[:, V_CW + (l * 3 + 1) * NCH + ch: V_CW + (l * 3 + 1) * NCH + ch + 1]
                w2 = VF[:, V_CW + (l * 3 + 2) * NCH + ch: V_CW + (l * 3 + 2) * NCH + ch + 1]
                cb = VF[:, V_CB + l * NCH + ch: V_CB + l * NCH + ch + 1]
                cpy("act", U[:, 0:2], carry[:, l, ch, :], [R_carry[l][ch]], [RUH])
                cpy("act", U[:, 2:2 + T], PS[b][:, 0:T], [R_ps[b]], [RU])
                cpy("act", carry[:, l, ch, :], PS[b][:, T - 2:T], [R_ps[b]], [R_carry[l][ch]])
                if isg:
                    tsc("dve", Y[:, 0:T], PS[b][:, 0:T], w2, cb, ALU.mult, ALU.add, [R_ps[b], R_vf], [RY])
                    stt("dve", Y[:, 0:T], U[:, 1:1 + T], w1, Y[:, 0:T], ALU.mult, ALU.add, [RU, RUH, RY, R_vf], [RY])
                    stt("dve", Y[:, 0:T], U[:, 0:T], w0, Y[:, 0:T], ALU.mult, ALU.add, [RU, RUH, RY, R_vf], [RY])
                    if pending_gelu:
                        pending_gelu.pop()()
                    pending_gelu.append(lambda Y=Y, RY=RY, i=i: act(actT[:, i, 0:T], Y[:, 0:T], AF.Gelu_apprx_tanh,
                                                                    [RY], [R_act[i]]))
                else:
                    if pending_gelu:
                        pending_gelu.pop()()
                    act(Y[:, 0:T], PS[b][:, 0:T], AF.Identity, [R_ps[b], R_vf], [RY], bias=cb, scale=w2)
                    stt("dve", Y[:, 0:T], U[:, 1:1 + T], w1, Y[:, 0:T], ALU.mult, ALU.add, [RU, RUH, RY, R_vf], [RY])
                    stt("dve", Y[:, 0:T], U[:, 0:T], w0, Y[:, 0:T], ALU.mult, ALU.add, [RU, RUH, RY, R_vf], [RY])
                    ttn("dve" if (i % 3 == 2 or i >= 19) else "pool", actT[:, i, 0:T], Y[:, 0:T], actT[:, i, 0:T],
                        ALU.mult, [RY, R_act[i]], [R_act[i]])
            if last_out is not None:
                for r in range(2):
                    dst = bass.AP(last_out.tensor, l * 2 * 2 * DFF + r * 2 * DFF, [[1, 128], [128, NCH]])
                    o = dma("sp", dst, carry[:, l, :, r], [R_carry[l][c] for c in range(NCH)], [Res()], ch_misc,
                            slow=True)
                    R_out.append(o)
            for m in range(8):
                wi = load_w22(sc_dn[l][m], ("dn", l, m))
                wv = W22[wi][:].rearrange("p (kc c) -> p kc c", kc=22)
                b = next_d()
                for kc in range(FC):
                    mm(PS[b][:, 0:T], wv[:, kc, :], actT[:, kc, 0:T], kc == 0, kc == FC - 1,
                       [R_w22[wi], R_act[kc]], [R_ps[b]])
                gate = ADAG[:, g, tab(g, l, 5, m): tab(g, l, 5, m) + 1]
                stt("dve", xT[:, m, 0:T], PS[b][:, 0:T], gate, xT[:, m, 0:T], ALU.mult, ALU.add,
                    [R_ps[b], R_x[m], R_tabs], [R_x[m]])

        def tok_major(src_chunks_res, T, w_slot, col0, ncols, dst_fn, f32_out_fn):
            wv = w8v(w_slot)
            ngr = (T + 127) // 128
            for gi in range(ngr):
                nr = min(128, T - gi * 128)
                mr = max(nr, 32)
                b = next_d()
                for kc in range(8):
                    mm(PS[b][0:mr, 0:ncols], hT[:, kc, gi * 128: gi * 128 + mr], wv[:, kc, col0:col0 + ncols],
                       kc == 0, kc == 7, [R_h[kc], R_w8[w_slot]], [R_ps[b]])
                if dst_fn is not None:
                    dst_fn(gi, nr, b)
                if f32_out_fn is not None:
                    f32_out_fn(gi, nr, b)

        def run_tile(g, T, t, xsrc, ydst, is_last, outs):
            sample = g == 1
            npairs = (T + 127) // 128
            pairs = [(p * 128, min(128, T - p * 128)) for p in range(npairs)]
            cur_half = 1 if sample else t % 2
            prev_half = 1 - cur_half
            import os as _os2
            KSUB = int(_os2.environ.get("KSUB", "99"))
            x_in(xsrc, T, g)
            h_from_x(g, 0, T)
            if KSUB <= 1:
                return
            for l in range(4):
                is_A = l < 2
                if is_A:
                    la = l
                    def evq(m, b):
                        eng = "act" if m % 2 == 0 else "dve"
                        cpy(eng, actT[:, m, 0:T], PS[b][:, 0:T], [R_ps[b]], [R_act[m]])
                    proj_fm([("qa", la, 0), ("qa", la, 1)], [sc_qa[la][0], sc_qa[la][1]], 8,
                            lambda kc: hT[:, kc, 0:T], R_h, 8, T, evq)
                    KS2 = int(_os2.environ.get("KSUB2", "99"))
                    if KS2 <= 1:
                        return
                    kslot = load_w8(sc_qa[la][2], ("qa", la, 2))
                    wv = w8v(kslot)
                    for g4 in range(4):
                        b = next_d()
                        for kc in range(8):
                            mm(PS[b][:, 0:T], wv[:, kc, g4 * 128:(g4 + 1) * 128], hT[:, kc, 0:T], kc == 0, kc == 7,
                               [R_w8[kslot], R_h[kc]], [R_ps[b]])
                        eng = "act" if g4 % 2 == 0 else "dve"
                        cpy(eng, KA[la][:, g4, 128:128 + T], PS[b][:, 0:T], [R_ps[b]], R_ka[la][1:5])
                    if KS2 <= 2:
                        return
                    kvslot = load_w8(sc_qa[la][3], ("qa", la, 3))
                    KS3 = int(_os2.environ.get("KS3", "99"))
                    if KS3 <= 1:
                        return
                    if KS3 <= 2:
                        tok_major(R_h, T, kvslot, 256, 256, None, None)
                        return

                    def vdst(gi, nr, b):
                        if _os2.environ.get("NOVD3") and gi == 3 and not sample:
                            return
                        cpy("act" if nr == 128 else "dve", VA[la][0:nr, 1 + gi, :], PS[b][0:nr, 0:256], [R_ps[b]],
                            [R_va[la][1 + gi]])
                    want_out = (is_last or sample) and _os2.environ.get("NOWANT", "0") != "1"
                    if want_out:
                        okd, ovd = outs["ak"], outs["av"]

                        def vout(gi, nr, b):
                            if sample or gi == 3:
                                si = 0
                                if _os2.environ.get("ALT"):
                                    if _os2.environ.get("ALT") == "2":
                                        cpy("dve", R2buf[0:nr, 0:256], Tbuf[0:nr, 0:256], [R_ps[b]], [R_r2])
                                    elif _os2.environ.get("ALT") == "3":
                                        cpy("dve", R2buf[0:nr, 0:256], PS[b][0:nr, 0:256], [R_ps[b]], [R_r2])
                                        cpy("dve", Tbuf[0:nr, 0:256], PS[b][0:nr, 0:256], [R_ps[b]], [R_t])
                                    else:
                                        tsc("dve", R2buf[0:nr, 0:256], PS[b][0:nr, 0:256], 1.0, None, ALU.mult, None, [R_ps[b]], [R_r2])
                                else:
                                    cpy(_os2.environ.get("VENG2", "dve"), stage[si][0:nr, 0:256], PS[b][0:nr, 0:256], [R_ps[b]], [R_stage[si]])
                                o = dma("sp", ovd[la][0:nr, :], stage[si][0:nr, 0:256], [R_stage[si]], [Res()],
                                        ch_stage[si])
                                R_out.append(o)

                        def kout(gi, nr, b):
                            if sample or gi == 3:
                                si = 1
                                cpy(_os2.environ.get("VENG2", "dve"), stage[si][0:nr, 0:256], PS[b][0:nr, 0:256], [R_ps[b]], [R_stage[si]])
                                o = dma("sp", okd[la][0:nr, :], stage[si][0:nr, 0:256], [R_stage[si]], [Res()],
                                        ch_stage[si])
                                R_out.append(o)
                        tok_major(R_h, T, kvslot, 256, 256, vdst, None if _os2.environ.get("NOV") else vout)
                        if not _os2.environ.get("NOK"):
                            tok_major(R_h, T, kvslot, 0, 256, None, kout)
                    else:
                        tok_major(R_h, T, kvslot, 256, 256, vdst, None)

                    if KSUB <= 2:
                        return
                    def tiles_of_pair(pi, la=la):
                        res = []
                        for ty, ti in (("e0", pi), ("e1", pi + 1)):
                            if ti == 0 and (not sample) and t == 0:
                                continue
                            nk = 128
                            if ti >= 1:
                                nk = min(128, T - (ti - 1) * 128)
                            res.append({
                                "nk": nk, "type": ty,
                                "kT": (lambda par, j, ti=ti, nk=nk: KA[la][par * 64:(par + 1) * 64, j // 2,
                                                                            ti * 128: ti * 128 + max(nk, 32)]),
                                "kres": [R_ka[la][ti]],
                                "v": (lambda par, j, ti=ti, nk=nk: VA[la][0:nk, ti, (j // 2) * 64:(j // 2) * 64 + 64]),
                                "vres": [R_va[la][ti]],
                            })
                        return res
                    attention(la, g, T, pairs, tiles_of_pair, True)
                    if KSUB <= 3:
                        return
                    if not sample:
                        cpy("pool", KA[la][:, :, 0:128], KA[la][:, :, 512:640], [R_ka[la][4]], [R_ka[la][0]])
                        cpy("pool", VA[la][:, 0, :], VA[la][:, 4, :], [R_va[la][4]], [R_va[la][0]])
                    wo_keys = [("oa", la, 0), ("oa", la, 1)]
                    wo_scr = [sc_oa[la][0], sc_oa[la][1]]
                else:
                    jb = l - 2
                    dma("sp", EBB[:].rearrange("p a b c d -> p (a b c d)"), sc_ebb[jb][:, :],
                        [scr_res(("ebb", jb))], [R_ebb], ch_ebb)

                    def evq(m, b):
                        eng = "act" if m % 2 == 0 else "dve"
                        cpy(eng, actT[:, m, 0:T], PS[b][:, 0:T], [R_ps[b]], [R_act[m]])
                    proj_fm([("qb", jb, 0), ("qb", jb, 1)], [sc_qb[jb][0], sc_qb[jb][1]], 8,
                            lambda kc: hT[:, kc, 0:T], R_h, 8, T, evq)

                    def tiles_of_pair(pi):
                        res = []
                        for k5 in range(5):
                            ti = pi + k5
                            if ti < 4 and (not sample) and t == 0:
                                continue
                            half = prev_half if ti < 4 else cur_half
                            tl = ti % 4
                            nk = 128 if ti < 4 else min(128, T - tl * 128)
                            ty = ("m0", "n", "n", "e0", "e1")[k5]
                            res.append({
                                "nk": nk, "type": ty,
                                "kT": (lambda par, j, half=half, tl=tl, nk=nk:
                                       KB[par * 64:(par + 1) * 64, half, j, tl * 128: tl * 128 + max(nk, 32)]),
                                "kres": [R_kb[half][tl]],
                                "v": (lambda par, j, half=half, tl=tl, nk=nk:
                                      VB[0:nk, half, tl, (2 * j + par) * 64:(2 * j + par) * 64 + 64]),
                                "vres": [R_vb[half][tl]],
                            })
                        return res
                    attention(l, g, T, pairs, tiles_of_pair, False)
                    wo_keys = [("ob", jb, 0), ("ob", jb, 1)]
                    wo_scr = [sc_ob[jb][0], sc_ob[jb][1]]

                def evo(m, b, l=l):
                    gate = ADAG[:, g, tab(g, l, 2, m): tab(g, l, 2, m) + 1]
                    stt("dve", xT[:, m, 0:T], PS[b][:, 0:T], gate, xT[:, m, 0:T], ALU.mult, ALU.add,
                        [R_ps[b], R_x[m], R_tabs], [R_x[m]])
                proj_fm(wo_keys, wo_scr, 8, lambda kc: hT[:, kc, 0:T], R_h, 8, T, evo)
                layer_norm(g, l, 0, T, FT[:, g, l, 0, 0, :], FT[:, g, l, 0, 1, :], True)
                if KSUB <= 4:
                    return
                conv_out = outs["conv"] if (is_last or sample) else None
                ffn(l, g, T, conv_out)
                if KSUB <= 5:
                    return
                if l == 1:
                    layer_norm(g, l, 1, T, FKV[:, g, 0, :], FKV[:, g, 1, :], True)
                    def evk(m, b):
                        eng = "act" if m % 2 == 0 else "dve"
                        if T >= 128:
                            cpy(eng, KB[:, cur_half, m, 0:T], PS[b][:, 0:T], [R_ps[b]], R_kb[cur_half])
                        else:
                            cpy(eng, KB[:, cur_half, m, 0:T], PS[b][:, 0:T], [R_ps[b]], [R_kb[cur_half][0]])
                    proj_fm([("kvb", 0), ("kvb", 1)], [sc_kvb[0], sc_kvb[1]], 8,
                            lambda kc: hT[:, kc, 0:T], R_h, 8, T, evk)
                    want_out = is_last or sample
                    for hv in range(2):
                        vslot = load_w8(sc_kvb[2 + hv], ("kvb", 2 + hv))

                        def vdst(gi, nr, b, hv=hv):
                            cpy("act" if nr == 128 else "dve", VB[0:nr, cur_half, gi, hv * 512:(hv + 1) * 512],
                                PS[b][0:nr, 0:512], [R_ps[b]], [R_vb[cur_half][gi]])

                        def vout(gi, nr, b, hv=hv):
                            si = gi % 2
                            cpy("dve", stage[si][0:nr, 0:512], PS[b][0:nr, 0:512], [R_ps[b]], [R_stage[si]])
                            o = dma("sp", outs["bv"][gi * 128: gi * 128 + nr, hv * 512:(hv + 1) * 512],
                                    stage[si][0:nr, 0:512], [R_stage[si]], [Res()], ch_stage[si])
                            R_out.append(o)
                        tok_major(R_h, T, vslot, 0, 512, vdst, vout if want_out else None)
                    if want_out:
                        for hk in range(2):
                            kslot = load_w8(sc_kvb[hk], ("kvb", hk))

                            def kout(gi, nr, b, hk=hk):
                                si = gi % 2
                                cpy("dve", stage[si][0:nr, 0:512], PS[b][0:nr, 0:512], [R_ps[b]], [R_stage[si]])
                                o = dma("sp", outs["bk"][gi * 128: gi * 128 + nr, hk * 512:(hk + 1) * 512],
                                        stage[si][0:nr, 0:512], [R_stage[si]], [Res()], ch_stage[si])
                                R_out.append(o)
                            tok_major(R_h, T, kslot, 0, 512, None, kout)
                    h_from_x(g, 2, T)
                elif l < 3:
                    layer_norm(g, l, 1, T, FT[:, g, l, 1, 0, :], FT[:, g, l, 1, 1, :], True)
                else:
                    layer_norm(g, l, 1, T, None, None, False)
            ngr = (T + 127) // 128
            for gi in range(ngr):
                nr = min(128, T - gi * 128)
                si = gi % 2
                for half in range(2):
                    b = next_d()
                    for cc in range(4):
                        c = half * 4 + cc
                        tr(PS[b][0:nr, cc * 128:(cc + 1) * 128], xT[:, c, gi * 128: gi * 128 + nr], ident[:],
                           [R_x[c], R_const], [R_ps[b]])
                    eng = "act" if half == 0 else "dve"
                    cpy(eng, stage[si][0:nr, half * 512:(half + 1) * 512], PS[b][0:nr, 0:512], [R_ps[b]], [R_stage[si]])
                o = dma("sp", ydst[gi * 128: gi * 128 + nr, :], stage[si][0:nr, :], [R_stage[si]], [Res()], ch_stage[si])
                R_out.append(o)

        def load_sample_caches():
            for la in range(2):
                st = stage[0]
                stv = st[:, 0:512].rearrange("p (g u d) -> p g u d", g=4, u=2)
                for dup in range(2):
                    dma("sp", stv[:, :, dup, :], cache_a_k[la].rearrange("k (g d) -> k g d", g=4), (), [R_stage[0]],
                        ch_stage[0])
                for g4 in range(4):
                    b = next_d()
                    tr(PS[b][:, 0:128], st[:, g4 * 128:(g4 + 1) * 128], ident[:], [R_stage[0], R_const], [R_ps[b]])
                    cpy("dve", KA[la][:, g4, 0:128], PS[b][:, 0:128], [R_ps[b]], [R_ka[la][0]])
                dma("sp", stage[1][:, 0:256], cache_a_v[la], (), [R_stage[1]], ch_stage[1])
                cpy("dve", VA[la][:, 0, :], stage[1][:, 0:256], [R_stage[1]], [R_va[la][0]])
            for tl in range(4):
                si = tl % 2
                dma("sp", stage[si][:, :], cache_b_k[tl * 128:(tl + 1) * 128, :], (), [R_stage[si]], ch_stage[si])
                for half in range(2):
                    b = next_d()
                    for cc in range(4):
                        c = half * 4 + cc
                        tr(PS[b][:, cc * 128:(cc + 1) * 128], stage[si][:, c * 128:(c + 1) * 128], ident[:],
                           [R_stage[si], R_const], [R_ps[b]])
                    cpy("dve" if half else "act", KB[:, 0, half * 4:(half + 1) * 4, tl * 128:(tl + 1) * 128],
                        PS[b][:, :].rearrange("p (c t) -> p c t", c=4), [R_ps[b]], [R_kb[0][tl]])
            for tl in range(4):
                si = tl % 2
                dma("sp", stage[si][:, :], cache_b_v[tl * 128:(tl + 1) * 128, :], (), [R_stage[si]], ch_stage[si])
                cpy("dve", VB[:, 0, tl, :], stage[si][:, :], [R_stage[si]], [R_vb[0][tl]])
            for l in range(4):
                for r in range(2):
                    src = bass.AP(state_conv.tensor, l * 2 * 2 * DFF + r * 2 * DFF, [[1, 128], [128, NCH]])
                    dma("sp", carry[:, l, :, r], src, (), [R_carry[l][c] for c in range(NCH)], ch_misc, slow=True)

        import os as _os
        KSTOP = int(_os.environ.get("KSTOP", "99"))
        if KSTOP >= 1:
            emit_casts()
        if KSTOP >= 2:
            setup()
        P.epoch = 1
        for t in range(NT if KSTOP >= 5 else 0):
            P.epoch = 2 + t
            run_tile(0, TT, t, x_prompt[t * TT:(t + 1) * TT, :], y_prompt[t * TT:(t + 1) * TT, :], t == NT - 1,
                     {"ak": p_ak, "av": p_av, "bk": p_bk, "bv": p_bv, "conv": p_conv})
        if KSTOP >= 3:
            load_sample_caches()
        if KSTOP >= 4:
            run_tile(1, DEC, -1, x_sample, y_sample, False,
                     {"ak": s_ak, "av": s_av, "bk": s_bk, "bv": s_bv, "conv": s_conv})
        n_epochs = NT + 2

        eng_sems = {}
        for e in ENGS:
            users = set()
            for op in P.ops[e]:
                if op.dma is None:
                    users |= op.sigs
            for c in sorted(users):
                eng_sems[(e, c)] = new_sem(f"s_{e}_{c}")
        for e in ENGS:
            cnt = {}
            for op in P.ops[e]:
                if op.dma is None:
                    for c in op.sigs:
                        cnt[c] = cnt.get(c, 0) + 1
                        op.cnts[c] = cnt[c]
            print("SEMCNT", e, cnt)
            for op in P.ops[e]:
                if op.dma is None and len(op.sigs) > 1:
                    print("MULTISIG", e, op.pos, op.epoch, op.sigs)
                    break
        out_ops = list(R_out)

        if _os.environ.get("DUMPW"):
            for en in ENGS:
                for op in P.ops[en][-6:]:
                    print("TAIL", en, op.pos, op.epoch, "sig", op.sig, op.cnt, "dma", None if op.dma is None else op.dmaval,
                          [(d.eng, d.epoch, d.pos, d.cnt if d.dma is None else ("dma", d.dmaval)) for d in op.waits])

        def replay(eng_name, e):
            for op in P.ops[eng_name]:
                for d in op.waits:
                    if d.dma is not None:
                        e.wait_ge(d.dma.sem, d.dmaval)
                    else:
                        e.wait_ge(eng_sems[(d.eng, eng_name)], d.cnts[eng_name])
                if op.fn is None:
                    continue
                ins = op.fn(e)
                if op.dma is not None:
                    ins.then_inc(op.dma.sem, 16)
                else:
                    for c in sorted(op.sigs):
                        ins.then_inc(eng_sems[(op.eng, c)], 1)
            if eng_name == "sp":
                done = {}
                for o in out_ops:
                    if o.dma is None:
                        continue
                    key = id(o.dma)
                    if key not in done or done[key][1] < o.dmaval:
                        done[key] = (o.dma, o.dmaval)
                for chn, val in done.values():
                    e.wait_ge(chn.sem, val)

        with nc.Block() as block:
            @block.tensor
            def _(e):
                replay("pe", e)

            @block.scalar
            def _(e):
                replay("act", e)

            @block.vector
            def _(e):
                replay("dve", e)

            @block.gpsimd
            def _(e):
                replay("pool", e)

            @block.sync
            def _(e):
                replay("sp", e)
    stats = {e: len(P.ops[e]) for e in ENGS}
    print("SBUF remaining", nc.sbuf_bytes_remaining)
    return nc, stats


_CACHE = {}


def _host_inputs(inputs, NT):
    f = lambda a: np.ascontiguousarray(np.asarray(a, dtype=np.float32))
    shared = {}
    for k in ("w_ada", "w_qkv_a", "w_o_a", "sinks_a", "t5_table", "w_ada_kv", "w_kv_b", "w_q_b", "w_o_b",
              "relpos_b", "w_up", "w_down"):
        shared[k] = f(inputs[k])
    shared["ident"] = np.eye(128, dtype=np.float32)
    shared["anti"] = np.ascontiguousarray(np.eye(128, dtype=np.float32)[::-1])
    shared["t5oh"] = _t5_onehot()
    in_maps = []
    for b in range(8):
        vec = np.zeros((V_ROWS, 128), np.float32)
        vec[V_BADA:V_BADA + 192] = f(inputs["b_ada"]).reshape(192, 128)
        vec[V_BKV:V_BKV + 16] = f(inputs["b_ada_kv"]).reshape(16, 128)
        vec[V_LNG:V_LNG + 64] = f(inputs["ln_g"]).reshape(64, 128)
        vec[V_LNB:V_LNB + 64] = f(inputs["ln_b"]).reshape(64, 128)
        vec[V_CW:V_CW + 528] = f(inputs["conv_w"]).reshape(528, 128)
        vec[V_CB:V_CB + 176] = f(inputs["conv_b"]).reshape(176, 128)
        vec[V_C:V_C + 8] = f(inputs["c_prompt"])[b].reshape(8, 128)
        vec[V_C + 8:V_C + 16] = f(inputs["c_sample"])[b].reshape(8, 128)
        m = dict(shared)
        m["vecs"] = vec
        m["x_prompt"] = f(inputs["x_prompt"][b])
        m["x_sample"] = f(inputs["x_sample"][b])
        m["cache_a_k"] = f(inputs["cache_a_k"][:, b]).reshape(2, 128, 256)
        m["cache_a_v"] = f(inputs["cache_a_v"][:, b]).reshape(2, 128, 256)
        m["cache_b_k"] = f(inputs["cache_b_k"][b]).reshape(512, 1024)
        m["cache_b_v"] = f(inputs["cache_b_v"][b]).reshape(512, 1024)
        m["state_conv"] = f(inputs["state_conv"][:, b])
        in_maps.append(m)
    return in_maps


def kernel(**inputs):
    S = int(np.asarray(inputs["x_prompt"]).shape[1])
    NT = S // TT
    if NT not in _CACHE:
        _CACHE[NT] = build(NT)[0]
    nc = _CACHE[NT]
    in_maps = _host_inputs(inputs, NT)
    res = run_bass_kernel_spmd(nc, in_maps, core_ids=list(range(8)))
    R = res.results
    st = lambda k: np.stack([np.asarray(r[k], dtype=np.float32) for r in R], axis=0)
    y_prompt = st("y_prompt")
    y_sample = st("y_sample")
    rows_a = min(128, S)
    rows_b = min(512, S)
    p_ak = np.ascontiguousarray(st("p_ak").transpose(1, 0, 2, 3)).reshape(2, 8, 128, 4, 64)[:, :, 128 - rows_a:]
    p_av = np.ascontiguousarray(st("p_av").transpose(1, 0, 2, 3)).reshape(2, 8, 128, 4, 64)[:, :, 128 - rows_a:]
    p_bk = st("p_bk").reshape(8, 512, 16, 64)[:, 512 - rows_b:]
    p_bv = st("p_bv").reshape(8, 512, 16, 64)[:, 512 - rows_b:]
    p_conv = np.ascontiguousarray(st("p_conv").transpose(1, 0, 2, 3))
    s_ak = np.ascontiguousarray(st("s_ak").transpose(1, 0, 2, 3)).reshape(2, 8, DEC, 4, 64)
    s_av = np.ascontiguousarray(st("s_av").transpose(1, 0, 2, 3)).reshape(2, 8, DEC, 4, 64)
    s_bk = st("s_bk").reshape(8, DEC, 16, 64)
    s_bv = st("s_bv").reshape(8, DEC, 16, 64)
    s_conv = np.ascontiguousarray(st("s_conv").transpose(1, 0, 2, 3))
    return (y_prompt, y_sample, p_ak, p_av, p_bk, p_bv, p_conv, s_ak, s_av, s_bk, s_bv, s_conv)
```
